# Optimizing a Trainium2 kernel written in Bass

```python
import jax
import jax.numpy as jnp
from jax import lax
import numpy as np

D_MODEL = 1024
BATCH = 2
SEQ = 16384
DEPTH = 2
DEC_BATCH = 8
DEC_SEQ = 2048
PAST_LEN = 128

RW_WIDTH = D_MODEL
RW_HEAD_DIM = 64
RW_HEADS = RW_WIDTH // RW_HEAD_DIM
DECAY_RANK = 64
AAA_RANK = 64
RW_SHIFT_COLS = 3 * RW_WIDTH + 2 * DECAY_RANK + 2 * AAA_RANK
RW_IN_COLS = RW_SHIFT_COLS + RW_WIDTH
CV_CHANNELS = D_MODEL
CV_TAPS = 31
N_MIXERS = 2
N_RWKV = (DEPTH + N_MIXERS - 1) // N_MIXERS
N_CONV = DEPTH // N_MIXERS
RMS_EPS = 1e-5
LN_EPS = 1e-5
GN_EPS = 64e-5
NORM_EPS = 1e-12

kernel_name = "rwkv7_conformer_bidir_encoder"


def rms_norm(x, g):
    xf = x.astype(jnp.float32)
    y = xf * lax.rsqrt(jnp.mean(xf * xf, axis=-1, keepdims=True) + RMS_EPS)
    return (y * g.astype(jnp.float32)).astype(x.dtype)


def centred_shift_delta(u):
    prev = jnp.pad(u[:, :-1], ((0, 0), (1, 0), (0, 0)))
    nxt = jnp.pad(u[:, 1:], ((0, 0), (0, 1), (0, 0)))
    return 0.5 * (prev + nxt) - u


def delta_rule_scan(r, w, k, v, kk, a):
    b, _, h, n = r.shape
    xs = tuple(jnp.moveaxis(t, 1, 0) for t in (r, w, k, v, kk, a))

    def step(s, inp):
        r_t, w_t, k_t, v_t, kk_t, a_t = inp
        sa = jnp.einsum("bhvk,bhk->bhv", s, -kk_t)
        s = (s * w_t[:, :, None, :]
             + sa[..., None] * (kk_t * a_t)[:, :, None, :]
             + v_t[..., None] * k_t[:, :, None, :])
        return s, jnp.einsum("bhvk,bhk->bhv", s, r_t)

    s0 = jnp.zeros((b, h, n, n), jnp.float32)
    _, y = lax.scan(step, s0, xs)
    return jnp.moveaxis(y, 0, 1)


def rwkv_mixer(h, w_in, mu, w0, w2, a0, a2, k_k, k_a, r_k, lnx_g, lnx_b, w_out):
    b, l, _ = h.shape
    e, nh, n = RW_WIDTH, RW_HEADS, RW_HEAD_DIM
    f32 = jnp.float32
    u = h @ w_in
    us, gate = u[..., :RW_SHIFT_COLS], u[..., RW_SHIFT_COLS:]
    us = us + mu * centred_shift_delta(us)
    r, k, v = us[..., :e], us[..., e:2 * e], us[..., 2 * e:3 * e]
    wd = us[..., 3 * e:3 * e + 2 * DECAY_RANK].reshape(b, l, 2, DECAY_RANK)
    ad = us[..., 3 * e + 2 * DECAY_RANK:].reshape(b, l, 2, AAA_RANK)
    w_log = -jax.nn.softplus(-(w0 + jnp.einsum("bldr,dre->blde", jnp.tanh(wd), w2)).astype(f32)) - 0.5
    decay = jnp.exp(-jnp.exp(w_log))
    a = jax.nn.sigmoid((a0 + jnp.einsum("bldr,dre->blde", ad, a2)).astype(f32))

    def heads(t):
        return t.astype(f32).reshape(b, l, nh, n)

    kk = heads(k * k_k)
    kk = kk / jnp.maximum(jnp.sqrt(jnp.sum(kk * kk, axis=-1, keepdims=True)), NORM_EPS)
    k_dir = k.astype(f32)[:, :, None, :] * (1.0 + (a - 1.0) * k_a.astype(f32))
    r_h, v_h = heads(r), heads(v)

    def flip(t):
        return jnp.flip(t, axis=1)

    y_fwd = delta_rule_scan(r_h, heads(decay[:, :, 0]), heads(k_dir[:, :, 0]), v_h, kk,
                            heads(a[:, :, 0]))
    y_bwd = flip(delta_rule_scan(flip(r_h), flip(heads(decay[:, :, 1])), flip(heads(k_dir[:, :, 1])),
                                 flip(v_h), flip(kk), flip(heads(a[:, :, 1]))))
    y = y_fwd + y_bwd
    mean = jnp.mean(y, axis=-1, keepdims=True)
    var = jnp.mean(jnp.square(y - mean), axis=-1, keepdims=True)
    y = ((y - mean) * lax.rsqrt(var + GN_EPS)).reshape(b, l, e) * lnx_g.astype(f32) + lnx_b.astype(f32)
    bonus = jnp.sum(r_h * heads(k) * r_k.astype(f32), axis=-1, keepdims=True) * v_h
    y = (y + bonus.reshape(b, l, e)).astype(h.dtype) * jax.nn.silu(gate)
    return y @ w_out


def conv_mixer(h, w_in, b_in, w_dw, b_dw, ln_g, ln_b, w_out, b_out):
    c = CV_CHANNELS
    u = h @ w_in + b_in
    val, glu_gate, gate = u[..., :c], u[..., c:2 * c], u[..., 2 * c:]
    z = val * jax.nn.sigmoid(glu_gate)
    z = lax.conv_general_dilated(
        z, w_dw[:, None, :].astype(z.dtype), window_strides=(1,),
        padding=[(CV_TAPS // 2, CV_TAPS // 2)],
        dimension_numbers=("NWC", "WIO", "NWC"), feature_group_count=c) + b_dw
    zf = z.astype(jnp.float32)
    mean = jnp.mean(zf, axis=-1, keepdims=True)
    var = jnp.mean(jnp.square(zf - mean), axis=-1, keepdims=True)
    zn = ((zf - mean) * lax.rsqrt(var + LN_EPS) * ln_g.astype(jnp.float32) + ln_b.astype(jnp.float32)).astype(h.dtype)
    z = jax.nn.silu(zn) * jax.nn.silu(gate)
    return z @ w_out + b_out


def trunk(x, norm_g, final_g, rw, cv):
    for i in range(DEPTH):
        h = rms_norm(x, norm_g[i])
        j = i // N_MIXERS
        if i % N_MIXERS == 0:
            x = x + rwkv_mixer(h, *[p[j] for p in rw])
        else:
            x = x + conv_mixer(h, *[p[j] for p in cv])
    return rms_norm(x, final_g)


def setup_inputs(seed: int = 0) -> dict:
    key = jax.random.key(seed)
    ks = jax.random.split(key, 24)
    d, e, c = D_MODEL, RW_WIDTH, CV_CHANNELS

    def nrm(k, shape, s):
        return s * jax.random.normal(k, shape, jnp.float32)

    ramp = -6.0 + 5.0 * jnp.linspace(0.0, 1.0, e, dtype=jnp.float32) ** 0.85
    return {
        "x_prompt": nrm(ks[0], (BATCH, SEQ, d), 1.0),
        "x_sample": nrm(ks[1], (DEC_BATCH, DEC_SEQ, d), 1.0),
        "norm_g": 1.0 + nrm(ks[2], (DEPTH, d), 0.02),
        "final_g": 1.0 + nrm(ks[3], (d,), 0.02),
        "rw_in": nrm(ks[4], (N_RWKV, d, RW_IN_COLS), d ** -0.5),
        "rw_mu": jax.random.uniform(ks[5], (N_RWKV, RW_SHIFT_COLS), jnp.float32, 0.2, 0.8),
        "rw_w0": ramp + nrm(ks[6], (N_RWKV, 2, e), 0.1),
        "rw_w2": nrm(ks[7], (N_RWKV, 2, DECAY_RANK, e), 0.1 * DECAY_RANK ** -0.5),
        "rw_a0": nrm(ks[8], (N_RWKV, 2, e), 0.1),
        "rw_a2": nrm(ks[9], (N_RWKV, 2, AAA_RANK, e), 0.5 * AAA_RANK ** -0.5),
        "rw_kk": 0.85 + nrm(ks[10], (N_RWKV, e), 0.02),
        "rw_ka": 1.0 + nrm(ks[11], (N_RWKV, e), 0.02),
        "rw_rk": nrm(ks[12], (N_RWKV, RW_HEADS, RW_HEAD_DIM), 0.1),
        "rw_lnx_g": 1.0 + nrm(ks[13], (N_RWKV, e), 0.02),
        "rw_lnx_b": nrm(ks[14], (N_RWKV, e), 0.02),
        "rw_out": nrm(ks[15], (N_RWKV, e, d), e ** -0.5),
        "cv_in": nrm(ks[16], (N_CONV, d, 3 * c), d ** -0.5),
        "cv_b_in": nrm(ks[17], (N_CONV, 3 * c), 0.02),
        "cv_dw": nrm(ks[18], (N_CONV, CV_TAPS, c), CV_TAPS ** -0.5),
        "cv_b_dw": nrm(ks[19], (N_CONV, c), 0.02),
        "cv_ln_g": 1.0 + nrm(ks[20], (N_CONV, c), 0.02),
        "cv_ln_b": nrm(ks[21], (N_CONV, c), 0.02),
        "cv_out": nrm(ks[22], (N_CONV, c, d), c ** -0.5),
        "cv_b_out": nrm(ks[23], (N_CONV, d), 0.02),
    }


def reference(x_prompt, x_sample, norm_g, final_g,
              rw_in, rw_mu, rw_w0, rw_w2, rw_a0, rw_a2, rw_kk, rw_ka, rw_rk,
              rw_lnx_g, rw_lnx_b, rw_out,
              cv_in, cv_b_in, cv_dw, cv_b_dw, cv_ln_g, cv_ln_b, cv_out, cv_b_out):
    rw = (rw_in, rw_mu, rw_w0, rw_w2, rw_a0, rw_a2, rw_kk, rw_ka, rw_rk, rw_lnx_g, rw_lnx_b, rw_out)
    cv = (cv_in, cv_b_in, cv_dw, cv_b_dw, cv_ln_g, cv_ln_b, cv_out, cv_b_out)
    y_prompt = trunk(x_prompt, norm_g, final_g, rw, cv)
    y_sample = trunk(x_sample, norm_g, final_g, rw, cv)
    return (y_prompt, y_sample)
```

```python
import numpy as np
import ml_dtypes
from contextlib import ExitStack
import concourse.bass as bass
import concourse.mybir as mybir
from concourse.bass_utils import run_bass_kernel_spmd

F32 = mybir.dt.float32
BF16 = mybir.dt.bfloat16
ALU = mybir.AluOpType
AF = mybir.ActivationFunctionType
CDEC = 0.6065306597126334


class Buf:
    __slots__ = ("w", "r", "excl")

    def __init__(self, excl=False):
        self.w = None
        self.r = {}
        self.excl = excl


def bufs(n):
    return [Buf() for _ in range(n)]


def pbufs(n):
    return [Buf(True) for _ in range(n)]


class Queue:
    def __init__(self, fw, eng, name, is_pe=False):
        self.fw = fw
        self.eng = eng
        self.name = name
        self.sem = fw.es.enter_context(fw.nc.semaphore("q_" + name))
        self.n = 0
        self.known = {}
        self.is_pe = is_pe
        self.dma_sems = []
        self.dma_cnt = []
        self.dma_k = 0

    def add_dma_sems(self, k):
        for i in range(k):
            self.dma_sems.append(self.fw.es.enter_context(self.fw.nc.semaphore("d_%s_%d" % (self.name, i))))
            self.dma_cnt.append(0)

    def _wait(self, sem, val):
        if self.known.get(sem, 0) < val:
            self.eng.wait_ge(sem, val)
            self.known[sem] = val

    def _deps(self, reads, writes, skip_same_waw=False):
        deps = {}

        def add(tok):
            if tok is None:
                return
            s, v = tok
            if deps.get(s, 0) < v:
                deps[s] = v
        for b in reads:
            add(b.w)
        for b in writes:
            if not (skip_same_waw and b.w is not None and b.w[0] is self.sem):
                add(b.w)
            for s, v in b.r.items():
                add((s, v))
        for s, v in deps.items():
            self._wait(s, v)

    def _record(self, tok, reads, writes):
        s, v = tok
        for b in reads:
            if b.r.get(s, 0) < v:
                b.r[s] = v
        for b in writes:
            b.w = tok
            b.r = {}

    def op(self, fn, reads=(), writes=(), acc=False):
        if any(b.excl for b in reads):
            writes = list(writes) + [b for b in reads if b.excl]
            reads = [b for b in reads if not b.excl]
        self._deps(reads, writes, skip_same_waw=(acc and self.is_pe))
        ins = fn(self.eng)
        self.n += 1
        ins.then_inc(self.sem, 1)
        self._record((self.sem, self.n), reads, writes)
        return ins

    def dma(self, out, in_, reads=(), writes=()):
        self._deps(reads, writes)
        i = self.dma_k % len(self.dma_sems)
        self.dma_k += 1
        s = self.dma_sems[i]
        if self.dma_cnt[i] > 0:
            self._wait(s, 16 * self.dma_cnt[i])
        self.eng.dma_start(out=out, in_=in_).then_inc(s, 16)
        self.dma_cnt[i] += 1
        self._record((s, 16 * self.dma_cnt[i]), reads, writes)


class FW:
    def __init__(self, nc):
        self.nc = nc
        self.es = ExitStack()
        self.pe = Queue(self, nc.tensor, "pe", is_pe=True)
        self.dve = Queue(self, nc.vector, "dve")
        self.act = Queue(self, nc.scalar, "act")
        self.pool = Queue(self, nc.gpsimd, "pool")
        self.sp = Queue(self, nc.sync, "sp")
        self.sp.add_dma_sems(8)
        self.pool.add_dma_sems(8)
        self.queues = [self.pe, self.dve, self.act, self.pool, self.sp]

    def all_tokens(self):
        toks = []
        for q in self.queues:
            if q.n > 0:
                toks.append((q.sem, q.n))
            for s, c in zip(q.dma_sems, q.dma_cnt):
                if c > 0:
                    toks.append((s, 16 * c))
        return toks

    def barrier(self):
        toks = self.all_tokens()
        for q in self.queues:
            for s, v in toks:
                q._wait(s, v)


PV = {}
_o = 0
for _n, _w in [("g0", 8), ("g1", 8), ("w0f", 8), ("w0b", 8), ("a0f", 8), ("a0b", 8), ("kk", 8), ("ka", 8),
               ("rk", 8), ("lnxg", 8), ("lnxb", 8), ("bdw", 8), ("lng", 8), ("lnb", 8), ("mu", 26),
               ("bin", 24), ("dw", 248)]:
    PV[_n] = _o
    _o += _w
NPV = _o


def build(T, stop=None):
    NT = T // 128
    NG = T // 512
    nc = bass.Bass("TRN2", target_bir_lowering=False)
    fw = FW(nc)
    pe, dve, act, pool, sp = fw.pe, fw.dve, fw.act, fw.pool, fw.sp

    def dram(name, shape, dt, kind="Internal"):
        return nc.dram_tensor(name, shape, dt, kind=kind).ap()

    x_t = dram("x_t", [NT, 128, 1024], F32, "ExternalInput")
    x_h = dram("x_h", [NT, 2, 1024], F32, "ExternalInput")
    smask_d = dram("smask", [128, NT * 2], F32, "ExternalInput")
    cmask_d = dram("cmask", [128, NG * 2], F32, "ExternalInput")
    w_rwin = dram("w_rwin", [128, 8, 4352], F32, "ExternalInput")
    w_rwout = dram("w_rwout", [128, 8, 1024], F32, "ExternalInput")
    w_cvin = dram("w_cvin", [128, 8, 3072], F32, "ExternalInput")
    w_cvout = dram("w_cvout", [128, 8, 1024], F32, "ExternalInput")
    w2a2_d = dram("w2a2", [128, 2, 1024], F32, "ExternalInput")
    pvec_d = dram("pvec", [128, NPV], F32, "ExternalInput")
    bc_d = dram("bc", [128, 2, 1024], F32, "ExternalInput")
    cbf_d = dram("cbf", [128, 5, 512], BF16, "ExternalInput")
    cf32_d = dram("cf32", [128, 3, 128], F32, "ExternalInput")
    y_out = dram("y", [NT, 128, 1024], F32, "ExternalOutput")

    fm_s = dram("fm_s", [NT, 128, 2, 4, 1024], BF16)
    tm_s = dram("tm_s", [NT, 128, 7, 1024], BF16)
    eg_s = dram("eg_s", [NT, 128, 2, 512], F32)
    bon_s = dram("bon_s", [NT, 128, 1024], F32)
    sg_s = dram("sg_s", [NT, 128, 1024], F32)
    yT_s = dram("yT_s", [2, NT, 128, 1024], F32)
    x1_s = dram("x1_s", [NT, 128, 1024], F32)
    h1T_s = dram("h1T_s", [NT, 128, 8, 128], BF16)
    z_s = dram("z_s", [8, 128, T + 30], F32)
    sg1_s = dram("sg1_s", [NG, 8, 128, 512], F32)

    es = fw.es

    def sb(st, name, shape, dt):
        return st.enter_context(nc.sbuf_tensor(name, shape, dt))

    def psum(st, name, shape, dt):
        return st.enter_context(nc.psum_tensor(name, shape, dt))

    pvec = sb(es, "pvec_sb", [128, NPV], F32)
    b_pvec = Buf()
    cbf = sb(es, "cbf_sb", [128, 5, 512], BF16)
    b_cbf = Buf()
    cf32 = sb(es, "cf32_sb", [128, 3, 128], F32)
    b_cf = Buf()
    cst = sb(es, "cst", [128, 4], F32)
    b_cst = Buf()
    ones = sb(es, "ones", [128, 128], F32)
    b_ones = Buf()
    smask = sb(es, "smask_sb", [128, NT * 2], F32)
    cmask = sb(es, "cmask_sb", [128, NG * 2], F32)
    b_mask = Buf()
    sp.dma(pvec[:], pvec_d[:, :], writes=[b_pvec])
    sp.dma(cbf[:], cbf_d[:, :, :], writes=[b_cbf])
    sp.dma(cf32[:], cf32_d[:, :, :], writes=[b_cf])
    sp.dma(smask[:], smask_d[:, :], writes=[b_mask])
    sp.dma(cmask[:], cmask_d[:, :], writes=[b_mask])
    dve.op(lambda e: e.memset(cst[:, 0:1], 1e-5), writes=[b_cst])
    dve.op(lambda e: e.memset(cst[:, 1:2], 64e-5), writes=[b_cst])
    dve.op(lambda e: e.memset(cst[:, 2:3], 0.0), writes=[b_cst])
    dve.op(lambda e: e.memset(ones[:], 1.0), writes=[b_ones])
    ident = cbf[:, 0, 0:128]
    BO = cf32[:, 1, :]
    BOm = cf32[:, 0, :]
    O1k = cf32[:, 2, :]

    def pcol(name, j):
        return pvec[:, PV[name] + j: PV[name] + j + 1]

    CONSTS = [b_pvec, b_cbf, b_cf, b_cst, b_ones, b_mask]

    def load_weight_bf16(st, name, src, ncols, gname, dst_stage, b_stage, w=None):
        if w is None:
            w = sb(st, name, [128, 8, ncols], BF16)
        bw = Buf()
        for kc in range(8):
            for c0 in range(0, ncols, 1536):
                c1 = min(ncols, c0 + 1536)
                sp.dma(dst_stage[:, 0:c1 - c0], src[:, kc, c0:c1], writes=[b_stage])
                if gname is None:
                    act.op(lambda e: e.activation(out=w[:, kc, c0:c1], in_=dst_stage[:, 0:c1 - c0], func=AF.Copy),
                           reads=[b_stage], writes=[bw])
                else:
                    act.op(lambda e: e.activation(out=w[:, kc, c0:c1], in_=dst_stage[:, 0:c1 - c0], func=AF.Copy,
                                                  scale=pcol(gname, kc)),
                           reads=[b_stage, b_pvec], writes=[bw])
        return w, bw

    def rmsnorm_rstd(xt, bx, npart, junk, bjunk, ss, bss):
        act.op(lambda e: e.activation(out=junk[0:npart, :], in_=xt[0:npart, :], func=AF.Square,
                                      accum_out=ss[0:npart, 0:1]), reads=[bx], writes=[bjunk, bss])
        act.op(lambda e: e.activation(out=ss[0:npart, 1:2], in_=ss[0:npart, 0:1], func=AF.Sqrt,
                                      scale=1.0 / 1024.0, bias=cst[0:npart, 0:1]), reads=[bss, b_cst], writes=[bss])
        dve.op(lambda e: e.reciprocal(out=ss[0:npart, 2:3], in_=ss[0:npart, 1:2]), reads=[bss], writes=[bss])

    with ExitStack() as ph:
        pro = ExitStack()
        Win = sb(ph, "Win", [128, 8, 4352], BF16)
        w2b = sb(ph, "w2b", [128, 4, 1024], BF16)
        stage = sb(pro, "stageA", [128, 1536], F32)
        b_stage = Buf()
        Win, b_Win = load_weight_bf16(ph, "Win", w_rwin, 4352, "g0", stage, b_stage, w=Win)
        w2f = sb(pro, "w2f", [128, 2, 1024], F32)
        b_w2 = Buf()
        sp.dma(w2f[:], w2a2_d[:, :, :], writes=[b_w2])
        dve.op(lambda e: e.memset(w2b[:], 0.0), writes=[b_w2])
        for wa in range(2):
            for d in range(2):
                dve.op(lambda e: e.tensor_copy(out=w2b[d * 64:(d + 1) * 64, wa * 2 + d, :],
                                               in_=w2f[d * 64:(d + 1) * 64, wa, :]), reads=[b_w2], writes=[b_w2])
        fw.barrier()
        pro.close()
        omm = sb(ph, "omm", [128, 26], F32)
        hmu = sb(ph, "hmu", [128, 26], F32)
        b_mu = Buf()
        mu_ap = pvec[:, PV["mu"]:PV["mu"] + 26]
        dve.op(lambda e: e.tensor_scalar(out=omm[:], in0=mu_ap, scalar1=-1.0, scalar2=1.0, op0=ALU.mult, op1=ALU.add),
               reads=[b_pvec], writes=[b_mu])
        dve.op(lambda e: e.tensor_scalar(out=hmu[:], in0=mu_ap, scalar1=0.5, scalar2=None, op0=ALU.mult),
               reads=[b_pvec], writes=[b_mu])

        X = [sb(ph, "xA%d" % i, [128, 1024], F32) for i in range(2)]
        bX = bufs(2)
        XH = [sb(ph, "xhA%d" % i, [2, 1024], F32) for i in range(2)]
        bXH = bufs(2)
        junk = sb(ph, "junkA", [128, 1024], F32)
        bjunk = Buf()
        ss = sb(ph, "ssA", [128, 4], F32)
        bss = Buf()
        ssh = sb(ph, "sshA", [128, 4], F32)
        bssh = Buf()
        hb = sb(ph, "hbA", [128, 1024], BF16)
        bhb = Buf()
        hhb = sb(ph, "hhbA", [2, 1024], BF16)
        bhhb = Buf()
        hT = [sb(ph, "hTA%d" % i, [128, 8, 130], BF16) for i in range(2)]
        bhT = bufs(2)
        ptr = psum(ph, "ptrA", [128, 8, 128], BF16)
        bptr = Buf(True)
        ptrh = psum(ph, "ptrhA", [128, 1024], BF16)[:, 0:16].rearrange("p (a b) -> p a b", b=2)
        bptrh = Buf(True)
        pin = [psum(ph, "pinA%d" % i, [128, 512], F32)[:, 0:390].rearrange("p (a b) -> p a b", b=130) for i in range(2)]
        bpin = pbufs(2)
        pgt = psum(ph, "pgA", [128, 4, 128], F32)
        bpg = Buf(True)
        pl = psum(ph, "plA", [128, 4, 128], F32)
        bpl = Buf(True)
        pss = psum(ph, "pssA", [128, 512], F32)[:, 0:256].rearrange("p (a b) -> p a b", b=128)
        bpss = pbufs(1) * 2
        ptq = psum(ph, "ptqA", [128, 8, 128], BF16)
        bptq = Buf(True)
        ue = [sb(ph, "ueA%d" % i, [128, 3, 130], F32) for i in range(2)]
        bue = bufs(2)
        t1 = [sb(ph, "t1A%d" % i, [128, 3, 128], F32) for i in range(2)]
        bt1 = bufs(2)
        t3 = sb(ph, "t3A", [128, 128], F32)
        bt3 = Buf()
        RKV = sb(ph, "rkvA", [128, 3, 8, 128], F32)
        bRKV = [bufs(8) for _ in range(3)]
        LO = sb(ph, "loA", [128, 2, 128], F32)
        bLO = Buf()
        twb = sb(ph, "twA", [128, 2, 128], BF16)
        btw = Buf()
        sgt = [sb(ph, "sgA%d" % i, [128, 8, 128], F32) for i in range(1)] * 2
        bsgt = bufs(1) * 2
        FM = [sb(ph, "fmA%d" % i, [128, 2, 4, 1024], BF16) for i in range(1)] * 2
        bFM = bufs(1) * 2
        TMb = [sb(ph, "tmA%d" % i, [128, 7, 1024], BF16) for i in range(1)] * 2
        bTM = bufs(1) * 2
        EG = [sb(ph, "egA%d" % i, [128, 2, 512], F32) for i in range(1)] * 2
        bEG = bufs(1) * 2
        BON = [sb(ph, "bonA%d" % i, [128, 8, 128], F32) for i in range(1)] * 2
        bBON = bufs(1) * 2
        vbf = sb(ph, "vbfA", [128, 128], BF16)
        bvbf = Buf()
        NW = 2
        WK = []
        for s in range(NW):
            d = {}
            for nm in ["sgw0", "sgw1", "a0", "a1", "cs0", "cs1", "u0", "u1", "E1_0", "E1_1", "E2_0", "E2_1",
                       "E3_0", "E3_1", "kk", "kk2", "nrm", "kkn", "tmp", "tmp2", "rk"]:
                d[nm] = sb(ph, "wk%d_%s" % (s, nm), [128, 128], F32)
            d["sc"] = sb(ph, "wk%d_sc" % s, [128, 8], F32)
            d["b"] = Buf()
            WK.append(d)
        if stop == "A0":
            fw.barrier()
            return nc

        for i in range(NT):
            par = i % 2
            xm, bxm, xh, bxh = X[par], bX[par], XH[par], bXH[par]
            sp.dma(xm[:], x_t[i, :, :], writes=[bxm])
            sp.dma(xh[:], x_h[i, :, :], writes=[bxh])
            rmsnorm_rstd(xm, bxm, 128, junk, bjunk, ss, bss)
            act.op(lambda e: e.activation(out=hb[:], in_=xm[:], func=AF.Copy, scale=ss[:, 2:3]),
                   reads=[bxm, bss], writes=[bhb])
            rmsnorm_rstd(xh, bxh, 2, junk, bjunk, ssh, bssh)
            act.op(lambda e: e.activation(out=hhb[:], in_=xh[:], func=AF.Copy, scale=ssh[0:2, 2:3]),
                   reads=[bxh, bssh], writes=[bhhb])
            hTt, bhTt = hT[par], bhT[par]
            for kc in range(8):
                pe.op(lambda e: e.transpose(out=ptr[:, kc, :], in_=hb[:, kc * 128:(kc + 1) * 128], identity=ident),
                      reads=[bhb, b_cbf], writes=[bptr], acc=True)
            dve.op(lambda e: e.tensor_copy(out=hTt[:, :, 1:129], in_=ptr[:]), reads=[bptr], writes=[bhTt])
            for kc in range(8):
                pe.op(lambda e: e.transpose(out=ptrh[:, kc, :], in_=hhb[0:2, kc * 128:(kc + 1) * 128],
                                            identity=cbf[0:2, 0, 0:2]),
                      reads=[bhhb, b_cbf], writes=[bptrh], acc=True)
            dve.op(lambda e: e.tensor_copy(out=hTt[:, :, 0:130:129], in_=ptrh[:]), reads=[bptrh], writes=[bhTt])
            if stop == "A1":
                fw.barrier()
                return nc

            for bg in range(9):
                pp = bg % 2
                cbs = [cb for cb in range(bg * 3, min(26, bg * 3 + 3))]
                for q, cb in enumerate(cbs):
                    for kc in range(8):
                        pe.op(lambda e: e.matmul(out=pin[pp][:, q, :], lhsT=Win[:, kc, cb * 128:(cb + 1) * 128],
                                                 rhs=hTt[:, kc, :], start=(kc == 0), stop=(kc == 7)),
                              reads=[b_Win, bhTt], writes=[bpin[pp]], acc=True)
                nq = len(cbs)
                act.op(lambda e: e.activation(out=ue[pp][:, 0:nq, :], in_=pin[pp][:, 0:nq, :], func=AF.Copy),
                       reads=[bpin[pp]], writes=[bue[pp]])
                dve.op(lambda e: e.tensor_tensor(out=t1[pp][:, 0:nq, :], in0=ue[pp][:, 0:nq, 0:128],
                                                 in1=ue[pp][:, 0:nq, 2:130], op=ALU.add),
                       reads=[bue[pp]], writes=[bt1[pp]])
                for q, cb in enumerate(cbs):
                    if cb < 24:
                        dst = RKV[:, cb // 8, cb % 8, :]
                        bd = bRKV[cb // 8][cb % 8]
                    else:
                        dst = LO[:, cb - 24, :]
                        bd = bLO
                    dve.op(lambda e: e.tensor_scalar(out=t3[:], in0=t1[pp][:, q, :], scalar1=hmu[:, cb:cb + 1],
                                                     scalar2=None, op0=ALU.mult),
                           reads=[bt1[pp], b_mu], writes=[bt3])
                    dve.op(lambda e: e.scalar_tensor_tensor(out=dst, in0=ue[pp][:, q, 1:129], scalar=omm[:, cb:cb + 1],
                                                            in1=t3[:], op0=ALU.mult, op1=ALU.add),
                           reads=[bue[pp], bt3, b_mu], writes=[bd])
            for gb in range(2):
                for q in range(4):
                    cb = gb * 4 + q
                    for kc in range(8):
                        pe.op(lambda e: e.matmul(out=pgt[:, q, :],
                                                 lhsT=Win[:, kc, 3328 + cb * 128:3328 + (cb + 1) * 128],
                                                 rhs=hTt[:, kc, 1:129], start=(kc == 0), stop=(kc == 7)),
                              reads=[b_Win, bhTt], writes=[bpg], acc=True)
                act.op(lambda e: e.activation(out=sgt[par][:, gb * 4:(gb + 1) * 4, :], in_=pgt[:], func=AF.Silu),
                       reads=[bpg], writes=[bsgt[par]])
            pool.dma(sg_s[i, :, :], sgt[par][:].rearrange("p a b -> p (a b)"), reads=[bsgt[par]])
            if stop == "A2":
                fw.barrier()
                return nc

            act.op(lambda e: e.activation(out=twb[:, 0, :], in_=LO[:, 0, :], func=AF.Tanh), reads=[bLO], writes=[btw])
            act.op(lambda e: e.activation(out=twb[:, 1, :], in_=LO[:, 1, :], func=AF.Copy), reads=[bLO], writes=[btw])

            fmt, bfmt = FM[par], bFM[par]
            tmt, btmt = TMb[par], bTM[par]
            egt, begt = EG[par], bEG[par]
            bont, bbont = BON[par], bBON[par]
            for eb in range(8):
                W = WK[eb % NW]
                bW = W["b"]
                r_ap, k_ap, v_ap = RKV[:, 0, eb, :], RKV[:, 1, eb, :], RKV[:, 2, eb, :]
                br, bk, bv = bRKV[0][eb], bRKV[1][eb], bRKV[2][eb]
                es_ = slice(eb * 128, (eb + 1) * 128)
                for slot in range(4):
                    wa, d = slot // 2, slot % 2
                    pe.op(lambda e: e.matmul(out=pl[:, slot, :], lhsT=w2b[:, slot, es_],
                                             rhs=twb[:, wa, :], start=True, stop=True),
                          reads=[b_w2, btw], writes=[bpl], acc=True)
                for d in range(2):
                    act.op(lambda e: e.activation(out=W["sgw%d" % d][:], in_=pl[:, d, :], func=AF.Sigmoid,
                                                  bias=pcol("w0f" if d == 0 else "w0b", eb)),
                           reads=[bpl, b_pvec], writes=[bW])
                    act.op(lambda e: e.activation(out=W["a%d" % d][:], in_=pl[:, 2 + d, :], func=AF.Sigmoid,
                                                  bias=pcol("a0f" if d == 0 else "a0b", eb)),
                           reads=[bpl, b_pvec], writes=[bW])
                    dve.op(lambda e: e.tensor_tensor_scan(out=W["cs%d" % d][:], data0=ones[:], data1=W["sgw%d" % d][:],
                                                          initial=0.0, op0=ALU.mult, op1=ALU.add),
                           reads=[bW, b_ones], writes=[bW])
                    dve.op(lambda e: e.tensor_tensor(out=W["u%d" % d][:], in0=W["cs%d" % d][:], in1=W["sgw%d" % d][:],
                                                     op=ALU.subtract), reads=[bW], writes=[bW])
                sc = W["sc"]
                act.op(lambda e: e.activation(out=W["E1_0"][:], in_=W["cs0"][:], func=AF.Exp, scale=-CDEC),
                       reads=[bW], writes=[bW])
                act.op(lambda e: e.activation(out=W["E2_0"][:], in_=W["cs0"][:], func=AF.Exp, scale=CDEC),
                       reads=[bW], writes=[bW])
                act.op(lambda e: e.activation(out=W["E3_0"][:], in_=W["u0"][:], func=AF.Exp, scale=-CDEC),
                       reads=[bW], writes=[bW])
                act.op(lambda e: e.activation(out=sc[:, 0:1], in_=W["cs0"][:, 127:128], func=AF.Exp, scale=-CDEC),
                       reads=[bW], writes=[bW])
                dve.op(lambda e: e.tensor_scalar(out=sc[:, 1:2], in0=W["cs1"][:, 127:128], scalar1=-CDEC, scalar2=None,
                                                 op0=ALU.mult), reads=[bW], writes=[bW])
                dve.op(lambda e: e.tensor_scalar(out=sc[:, 2:3], in0=W["cs1"][:, 127:128], scalar1=CDEC, scalar2=None,
                                                 op0=ALU.mult), reads=[bW], writes=[bW])
                act.op(lambda e: e.activation(out=W["E1_1"][:], in_=W["u1"][:], func=AF.Exp, scale=CDEC, bias=sc[:, 1:2]),
                       reads=[bW], writes=[bW])
                act.op(lambda e: e.activation(out=W["E2_1"][:], in_=W["u1"][:], func=AF.Exp, scale=-CDEC, bias=sc[:, 2:3]),
                       reads=[bW], writes=[bW])
                act.op(lambda e: e.activation(out=W["E3_1"][:], in_=W["cs1"][:], func=AF.Exp, scale=CDEC, bias=sc[:, 1:2]),
                       reads=[bW], writes=[bW])
                act.op(lambda e: e.activation(out=sc[:, 3:4], in_=sc[:, 1:2], func=AF.Exp), reads=[bW], writes=[bW])
                for d in range(2):
                    col = 0 if d == 0 else 3
                    dve.op(lambda e: e.tensor_scalar(out=sc[:, 4 + d:5 + d], in0=sc[:, col:col + 1],
                                                     scalar1=smask[:, 2 * i + d:2 * i + d + 1], scalar2=None, op0=ALU.mult),
                           reads=[bW, b_mask], writes=[bW])
                    dve.op(lambda e: e.tensor_scalar(out=egt[:, d, eb * 64:(eb + 1) * 64], in0=ones[:, 0:64],
                                                     scalar1=sc[:, 4 + d:5 + d], scalar2=None, op0=ALU.mult),
                           reads=[bW, b_ones], writes=[begt])
                dve.op(lambda e: e.tensor_scalar(out=W["kk"][:], in0=k_ap, scalar1=pcol("kk", eb), scalar2=None,
                                                 op0=ALU.mult), reads=[bk, b_pvec], writes=[bW])
                dve.op(lambda e: e.tensor_tensor(out=W["kk2"][:], in0=W["kk"][:], in1=W["kk"][:], op=ALU.mult),
                       reads=[bW], writes=[bW])
                pe.op(lambda e: e.matmul(out=pss[:, 0, :], lhsT=BO, rhs=W["kk2"][:], start=True, stop=True),
                      reads=[bW, b_cf], writes=[bpss[0]])
                act.op(lambda e: e.activation(out=W["nrm"][:], in_=pss[:, 0, :], func=AF.Sqrt), reads=[bpss[0]], writes=[bW])
                dve.op(lambda e: e.tensor_scalar(out=W["nrm"][:], in0=W["nrm"][:], scalar1=1e-12, scalar2=None,
                                                 op0=ALU.max), reads=[bW], writes=[bW])
                dve.op(lambda e: e.reciprocal(out=W["nrm"][:], in_=W["nrm"][:]), reads=[bW], writes=[bW])
                dve.op(lambda e: e.tensor_tensor(out=W["kkn"][:], in0=W["kk"][:], in1=W["nrm"][:], op=ALU.mult),
                       reads=[bW], writes=[bW])
                for d in range(2):
                    E1, E2, E3, a_t = W["E1_%d" % d], W["E2_%d" % d], W["E3_%d" % d], W["a%d" % d]
                    dve.op(lambda e: e.tensor_tensor(out=fmt[:, d, 0, es_], in0=r_ap, in1=E1[:], op=ALU.mult),
                           reads=[br, bW], writes=[bfmt])
                    dve.op(lambda e: e.tensor_scalar(out=W["tmp"][:], in0=a_t[:], scalar1=-1.0, scalar2=pcol("ka", eb),
                                                     op0=ALU.add, op1=ALU.mult), reads=[bW, b_pvec], writes=[bW])
                    dve.op(lambda e: e.scalar_tensor_tensor(out=W["tmp"][:], in0=W["tmp"][:], scalar=1.0, in1=k_ap,
                                                            op0=ALU.add, op1=ALU.mult), reads=[bW, bk], writes=[bW])
                    dve.op(lambda e: e.tensor_tensor(out=fmt[:, d, 1, es_], in0=W["tmp"][:], in1=E2[:], op=ALU.mult),
                           reads=[bW], writes=[bfmt])
                    dve.op(lambda e: e.scalar_tensor_tensor(out=fmt[:, d, 2, es_], in0=W["kkn"][:], scalar=-1.0, in1=E3[:],
                                                            op0=ALU.mult, op1=ALU.mult), reads=[bW], writes=[bfmt])
                    dve.op(lambda e: e.tensor_tensor(out=W["tmp2"][:], in0=W["kkn"][:], in1=a_t[:], op=ALU.mult),
                           reads=[bW], writes=[bW])
                    dve.op(lambda e: e.tensor_tensor(out=fmt[:, d, 3, es_], in0=W["tmp2"][:], in1=E2[:], op=ALU.mult),
                           reads=[bW], writes=[bfmt])
                dve.op(lambda e: e.scalar_tensor_tensor(out=W["rk"][:], in0=r_ap, scalar=pcol("rk", eb), in1=k_ap,
                                                        op0=ALU.mult, op1=ALU.mult), reads=[br, bk, b_pvec], writes=[bW])
                pe.op(lambda e: e.matmul(out=pss[:, 1, :], lhsT=BO, rhs=W["rk"][:], start=True, stop=True),
                      reads=[bW, b_cf], writes=[bpss[1]])
                dve.op(lambda e: e.tensor_tensor(out=bont[:, eb, :], in0=pss[:, 1, :], in1=v_ap, op=ALU.mult),
                       reads=[bpss[1], bv], writes=[bbont])
                act.op(lambda e: e.activation(out=vbf[:], in_=v_ap, func=AF.Copy), reads=[bv], writes=[bvbf])
                srcs = [fmt[:, 0, 1, es_], fmt[:, 0, 3, es_], fmt[:, 0, 2, es_],
                        fmt[:, 1, 1, es_], fmt[:, 1, 3, es_], fmt[:, 1, 2, es_], vbf[:]]
                for qi, s_ap in enumerate(srcs):
                    pe.op(lambda e: e.transpose(out=ptq[:, qi, :], in_=s_ap, identity=ident),
                          reads=[bfmt, bvbf, b_cbf], writes=[bptq], acc=True)
                act.op(lambda e: e.activation(out=tmt[:, :, es_], in_=ptq[:, 0:7, :], func=AF.Copy),
                       reads=[bptq], writes=[btmt])
            pool.dma(fm_s[i, :, :, :, :], fmt[:], reads=[bfmt])
            pool.dma(tm_s[i, :, :, :], tmt[:], reads=[btmt])
            pool.dma(eg_s[i, :, :, :], egt[:], reads=[begt])
            pool.dma(bon_s[i, :, :], bont[:].rearrange("p a b -> p (a b)"), reads=[bbont])
            if stop == "A3":
                fw.barrier()
                return nc
        fw.barrier()
    if stop == "A":
        return nc

    with ExitStack() as ph:
        I4, SU4, U4, SL4, L4 = [cbf[:, m, :].rearrange("p (a b) -> p a b", b=128) for m in range(5)]
        banks = [psum(ph, "bkB%d" % i, [128, 512], F32) for i in range(8)]
        bbank = pbufs(8)
        bank_ctr = [0]

        def next_bank():
            k = bank_ctr[0] % 8
            bank_ctr[0] += 1
            return banks[k], bbank[k]

        FMd = [sb(ph, "fmB%d" % i, [128, 4, 1024], BF16) for i in range(2)]
        bFMd = bufs(2)
        TMd = [sb(ph, "tmB%d" % i, [128, 3, 1024], BF16) for i in range(2)]
        bTMd = bufs(2)
        TMv = [sb(ph, "tvB%d" % i, [128, 1024], BF16) for i in range(2)]
        bTMv = bufs(2)
        EGd = [sb(ph, "egB%d" % i, [128, 512], F32) for i in range(2)]
        bEGd = bufs(2)

        def mat16(name):
            return sb(ph, name, [128, 16, 128], BF16), bufs(4)
        Pm, bP = mat16("PmB")
        PTm, bPT = mat16("PTmB")
        P2m, bP2 = mat16("P2mB")
        PT2m, bPT2 = mat16("PT2mB")
        Xm, bXm = mat16("XmB")
        XTm, bXTm = mat16("XTmB")
        Aak, bAak = mat16("AakB")
        Arb, bArb = mat16("ArbB")
        Ark, bArk = mat16("ArkB")
        M1 = sb(ph, "M1B", [128, 16, 64], BF16)
        bM1 = bufs(2)
        AtT = sb(ph, "AtTB", [128, 8, 128], BF16)
        bAtT = bufs(2)
        Usb = sb(ph, "UsbB", [128, 16, 64], BF16)
        bUsb = bufs(2)
        h32 = sb(ph, "h32B", [128, 512], F32)
        hbf = sb(ph, "hbfB", [128, 8, 64], BF16)
        bh = Buf()
        OT = [sb(ph, "OTB%d" % i, [128, 8, 128], F32) for i in range(2)]
        bOT = bufs(2)

        def load(step, i, d):
            p = step % 2
            sp.dma(FMd[p][:], fm_s[i, :, d, :, :], writes=[bFMd[p]])
            sp.dma(TMd[p][:], tm_s[i, :, 3 * d:3 * d + 3, :], writes=[bTMd[p]])
            sp.dma(TMv[p][:], tm_s[i, :, 6, :], writes=[bTMv[p]])
            sp.dma(EGd[p][:], eg_s[i, :, d, :], writes=[bEGd[p]])

        step = 0
        order = [(i, 0) for i in range(NT)] + [(i, 1) for i in range(NT - 1, -1, -1)]
        load(0, order[0][0], order[0][1])
        for idx, (i, d) in enumerate(order):
            p = step % 2
            if idx + 1 < len(order):
                load(step + 1, order[idx + 1][0], order[idx + 1][1])
            if idx == 0 or idx == NT:
                dve.op(lambda e: e.memset(h32[:], 0.0), writes=[bh])
                dve.op(lambda e: e.memset(hbf[:], 0.0), writes=[bh])
            fm, bfm, tmd, btmd, tv, btv, eg, beg = FMd[p], bFMd[p], TMd[p], bTMd[p], TMv[p], bTMv[p], EGd[p], bEGd[p]
            m_su, m_u, m_sl = (SU4, U4, SL4) if d == 0 else (SL4, L4, SU4)

            def fmh(q, h):
                eb, j = h // 2, h % 2
                return fm[j * 64:(j + 1) * 64, q, eb * 128:(eb + 1) * 128]

            def tmh(q, h):
                return tmd[:, q, h * 64:(h + 1) * 64]

            def vh(h):
                return tv[:, h * 64:(h + 1) * 64]

            def prod(lq, rq, mask, dst, bdst):
                for hb8 in range(2):
                    bkj = [next_bank(), next_bank()]
                    for e4 in range(4):
                        for j in range(2):
                            h = hb8 * 8 + e4 * 2 + j
                            bk, bbk = bkj[j]
                            pe.op(lambda e: e.matmul(out=bk[:, e4 * 128:(e4 + 1) * 128], lhsT=fmh(lq, h), rhs=fmh(rq, h),
                                                     start=True, stop=True), reads=[bfm], writes=[bbk], acc=True)
                    for j in range(2):
                        bk, bbk = bkj[j]
                        dve.op(lambda e: e.tensor_tensor(out=dst[:, hb8 * 8 + j:hb8 * 8 + 8:2, :],
                                                         in0=bk[:].rearrange("p (a b) -> p a b", b=128), in1=mask, op=ALU.mult),
                               reads=[bbk, b_cbf], writes=[bdst[hb8 * 2], bdst[hb8 * 2 + 1]])
            prod(3, 2, m_su, Pm, bP)
            prod(2, 3, m_sl, PTm, bPT)
            prod(1, 2, m_su, Aak, bAak)
            prod(3, 0, m_u, Arb, bArb)
            prod(1, 0, m_u, Ark, bArk)
            for hbk in range(4):
                hs = slice(hbk * 4, (hbk + 1) * 4)
                dve.op(lambda e: e.tensor_tensor(out=Xm[:, hs, :], in0=Pm[:, hs, :], in1=I4, op=ALU.add),
                       reads=[bP[hbk], b_cbf], writes=[bXm[hbk]])
                dve.op(lambda e: e.tensor_tensor(out=XTm[:, hs, :], in0=PTm[:, hs, :], in1=I4, op=ALU.add),
                       reads=[bPT[hbk], b_cbf], writes=[bXTm[hbk]])
            cur = (Pm, bP, PTm, bPT)
            nxt = (P2m, bP2, PT2m, bPT2)
            for lev in range(6):
                Pc, bPc, PTc, bPTc = cur
                Pn, bPn, PTn, bPTn = nxt
                last = (lev == 5)

                def mm_batch(lhs, blhs, rhs, brhs, evac):
                    for hbk in range(4):
                        bk, bbk = next_bank()
                        for hh in range(4):
                            h = hbk * 4 + hh
                            pe.op(lambda e: e.matmul(out=bk[:, hh * 128:(hh + 1) * 128], lhsT=lhs[:, h, :], rhs=rhs[:, h, :],
                                                     start=True, stop=True),
                                  reads=[blhs[hbk], brhs[hbk]], writes=[bbk], acc=True)
                        evac(hbk, bk, bbk)

                def ev_copy(dst, bdst):
                    def f(hbk, bk, bbk):
                        act.op(lambda e: e.activation(out=dst[:, hbk * 4:(hbk + 1) * 4, :],
                                                      in_=bk[:].rearrange("p (a b) -> p a b", b=128), func=AF.Copy),
                               reads=[bbk], writes=[bdst[hbk]])
                    return f

                def ev_add(dst, bdst):
                    def f(hbk, bk, bbk):
                        hs = slice(hbk * 4, (hbk + 1) * 4)
                        dve.op(lambda e: e.tensor_tensor(out=dst[:, hs, :], in0=bk[:].rearrange("p (a b) -> p a b", b=128),
                                                         in1=dst[:, hs, :], op=ALU.add),
                               reads=[bbk, bdst[hbk]], writes=[bdst[hbk]])
                    return f
                mm_batch(PTc, bPTc, Pc, bPc, ev_copy(Pn, bPn))
                if not last:
                    mm_batch(Pc, bPc, PTc, bPTc, ev_copy(PTn, bPTn))
                mm_batch(XTm, bXTm, Pn, bPn, ev_add(Xm, bXm))
                if not last:
                    mm_batch(Pn, bPn, XTm, bXTm, ev_add(XTm, bXTm))
                cur, nxt = nxt, cur
            for g8 in range(2):
                bk, bbk = next_bank()
                for hh in range(8):
                    h = g8 * 8 + hh
                    pe.op(lambda e: e.matmul(out=bk[:, hh * 64:(hh + 1) * 64], lhsT=Aak[:, h, :], rhs=vh(h),
                                             start=True, stop=True), reads=[bAak[h // 4], btv], writes=[bbk], acc=True)
                act.op(lambda e: e.activation(out=M1[:, g8 * 8:(g8 + 1) * 8, :],
                                              in_=bk[:].rearrange("p (a b) -> p a b", b=64), func=AF.Copy),
                       reads=[bbk], writes=[bM1[g8]])
            for g4 in range(2):
                bk, bbk = next_bank()
                for e4 in range(4):
                    eb = g4 * 4 + e4
                    for j in range(2):
                        h = eb * 2 + j
                        pe.op(lambda e: e.matmul(out=bk[j * 64:(j + 1) * 64, e4 * 128:(e4 + 1) * 128], lhsT=tmh(2, h),
                                                 rhs=Xm[:, h, :], start=True, stop=True),
                              reads=[btmd, bXm[h // 4]], writes=[bbk], acc=True)
                act.op(lambda e: e.activation(out=AtT[:, g4 * 4:(g4 + 1) * 4, :],
                                              in_=bk[:].rearrange("p (a b) -> p a b", b=128), func=AF.Copy),
                       reads=[bbk], writes=[bAtT[g4]])
            for g8 in range(2):
                bk, bbk = next_bank()
                for hh in range(8):
                    h = g8 * 8 + hh
                    eb, j = h // 2, h % 2
                    js = slice(j * 64, (j + 1) * 64)
                    pe.op(lambda e: e.matmul(out=bk[:, hh * 64:(hh + 1) * 64], lhsT=AtT[js, eb, :], rhs=hbf[js, eb, :],
                                             start=True, stop=False), reads=[bAtT[eb // 4], bh], writes=[bbk], acc=True)
                    pe.op(lambda e: e.matmul(out=bk[:, hh * 64:(hh + 1) * 64], lhsT=Xm[:, h, :], rhs=M1[:, h, :],
                                             start=False, stop=True), reads=[bXm[h // 4], bM1[g8]], writes=[bbk], acc=True)
                dve.op(lambda e: e.tensor_copy(out=Usb[:, g8 * 8:(g8 + 1) * 8, :],
                                               in_=bk[:].rearrange("p (a b) -> p a b", b=64)),
                       reads=[bbk], writes=[bUsb[g8]])
            ot, bot = OT[p], bOT[p]
            for g4 in range(2):
                bk, bbk = next_bank()
                for e4 in range(4):
                    eb = g4 * 4 + e4
                    for j in range(2):
                        h = eb * 2 + j
                        js = slice(j * 64, (j + 1) * 64)
                        o_ap = bk[js, e4 * 128:(e4 + 1) * 128]
                        pe.op(lambda e: e.matmul(out=o_ap, lhsT=hbf[js, eb, :], rhs=fmh(0, h), start=True, stop=False),
                              reads=[bh, bfm], writes=[bbk], acc=True)
                        pe.op(lambda e: e.matmul(out=o_ap, lhsT=Usb[:, h, :], rhs=Arb[:, h, :], start=False, stop=False),
                              reads=[bUsb[h // 8], bArb[h // 4]], writes=[bbk], acc=True)
                        pe.op(lambda e: e.matmul(out=o_ap, lhsT=vh(h), rhs=Ark[:, h, :], start=False, stop=True),
                              reads=[btv, bArk[h // 4]], writes=[bbk], acc=True)
                act.op(lambda e: e.activation(out=ot[:, g4 * 4:(g4 + 1) * 4, :],
                                              in_=bk[:].rearrange("p (a b) -> p a b", b=128), func=AF.Copy),
                       reads=[bbk], writes=[bot])
            pool.dma(yT_s[d, i, :, :], ot[:].rearrange("p a b -> p (a b)"), reads=[bot])
            bk, bbk = next_bank()
            for eb in range(8):
                for j in range(2):
                    h = eb * 2 + j
                    js = slice(j * 64, (j + 1) * 64)
                    o_ap = bk[js, eb * 64:(eb + 1) * 64]
                    pe.op(lambda e: e.matmul(out=o_ap, lhsT=tmh(1, h), rhs=Usb[:, h, :], start=True, stop=False),
                          reads=[btmd, bUsb[h // 8]], writes=[bbk], acc=True)
                    pe.op(lambda e: e.matmul(out=o_ap, lhsT=tmh(0, h), rhs=vh(h), start=False, stop=True),
                          reads=[btmd, btv], writes=[bbk], acc=True)
            dve.op(lambda e: e.tensor_tensor(out=h32[:], in0=bk[:], in1=h32[:], op=ALU.add), reads=[bbk, bh], writes=[bh])
            dve.op(lambda e: e.tensor_tensor(out=h32[:], in0=h32[:], in1=eg[:], op=ALU.mult), reads=[bh, beg], writes=[bh])
            act.op(lambda e: e.activation(out=hbf[:].rearrange("p a b -> p (a b)"), in_=h32[:], func=AF.Copy),
                   reads=[bh], writes=[bh])
            step += 1
        fw.barrier()
    if stop == "B":
        return nc

    with ExitStack() as ph:
        stage = sb(ph, "stageC", [128, 1536], F32)
        b_stage = Buf()
        Wout, b_Wout = load_weight_bf16(ph, "Wout", w_rwout, 1024, None, stage, b_stage)
        YF = [sb(ph, "yfC%d" % i, [128, 8, 128], F32) for i in range(2)]
        YB = [sb(ph, "ybC%d" % i, [128, 8, 128], F32) for i in range(2)]
        BN = [sb(ph, "bnC%d" % i, [128, 8, 128], F32) for i in range(2)]
        SG = [sb(ph, "sgC%d" % i, [128, 8, 128], F32) for i in range(2)]
        XC = [sb(ph, "xC%d" % i, [128, 1024], F32) for i in range(2)]
        bIN = bufs(2)
        yg = sb(ph, "ygC", [128, 8, 128], BF16)
        byg = Buf()
        pst = psum(ph, "pstC", [128, 512], F32)[:, 0:256].rearrange("p (a b) -> p a b", b=128)
        bpst = pbufs(1) * 2
        pout = [psum(ph, "poutC%d" % i, [128, 512], F32) for i in range(2)]
        bpout = pbufs(2)
        ptr = psum(ph, "ptrC", [128, 8, 128], BF16)
        bptr = Buf(True)
        x1 = [sb(ph, "x1C%d" % i, [128, 1024], F32) for i in range(2)]
        bx1 = bufs(2)
        junk = sb(ph, "junkC", [128, 1024], F32)
        bjunk = Buf()
        ss = sb(ph, "ssC", [128, 4], F32)
        bss = Buf()
        hb = sb(ph, "hbC", [128, 1024], BF16)
        bhb = Buf()
        h1T = [sb(ph, "h1TC%d" % i, [128, 8, 128], BF16) for i in range(2)]
        bh1T = bufs(2)
        WKc = []
        for s in range(2):
            dct = {nm: sb(ph, "wc%d_%s" % (s, nm), [128, 128], F32) for nm in ["y", "yc", "sq", "sd", "yn"]}
            dct["b"] = Buf()
            WKc.append(dct)

        def loadC(i):
            p = i % 2
            sp.dma(YF[p][:].rearrange("p a b -> p (a b)"), yT_s[0, i, :, :], writes=[bIN[p]])
            sp.dma(YB[p][:].rearrange("p a b -> p (a b)"), yT_s[1, i, :, :], writes=[bIN[p]])
            sp.dma(BN[p][:].rearrange("p a b -> p (a b)"), bon_s[i, :, :], writes=[bIN[p]])
            sp.dma(SG[p][:].rearrange("p a b -> p (a b)"), sg_s[i, :, :], writes=[bIN[p]])
            sp.dma(XC[p][:], x_t[i, :, :], writes=[bIN[p]])
        loadC(0)
        for i in range(NT):
            p = i % 2
            if i + 1 < NT:
                loadC(i + 1)
            bin_ = bIN[p]
            for eb in range(8):
                W = WKc[eb % 2]
                bW = W["b"]
                dve.op(lambda e: e.tensor_tensor(out=W["y"][:], in0=YF[p][:, eb, :], in1=YB[p][:, eb, :], op=ALU.add),
                       reads=[bin_], writes=[bW])
                pe.op(lambda e: e.matmul(out=pst[:, 0, :], lhsT=BOm, rhs=W["y"][:], start=True, stop=True),
                      reads=[bW, b_cf], writes=[bpst[0]])
                dve.op(lambda e: e.tensor_tensor(out=W["yc"][:], in0=W["y"][:], in1=pst[:, 0, :], op=ALU.subtract),
                       reads=[bW, bpst[0]], writes=[bW])
                dve.op(lambda e: e.tensor_tensor(out=W["sq"][:], in0=W["yc"][:], in1=W["yc"][:], op=ALU.mult),
                       reads=[bW], writes=[bW])
                pe.op(lambda e: e.matmul(out=pst[:, 1, :], lhsT=BOm, rhs=W["sq"][:], start=True, stop=True),
                      reads=[bW, b_cf], writes=[bpst[1]])
                act.op(lambda e: e.activation(out=W["sd"][:], in_=pst[:, 1, :], func=AF.Sqrt, bias=cst[:, 1:2]),
                       reads=[bpst[1], b_cst], writes=[bW])
                dve.op(lambda e: e.reciprocal(out=W["sd"][:], in_=W["sd"][:]), reads=[bW], writes=[bW])
                dve.op(lambda e: e.tensor_tensor(out=W["yn"][:], in0=W["yc"][:], in1=W["sd"][:], op=ALU.mult),
                       reads=[bW], writes=[bW])
                act.op(lambda e: e.activation(out=W["yn"][:], in_=W["yn"][:], func=AF.Identity,
                                              scale=pcol("lnxg", eb), bias=pcol("lnxb", eb)),
                       reads=[bW, b_pvec], writes=[bW])
                dve.op(lambda e: e.tensor_tensor(out=W["yn"][:], in0=W["yn"][:], in1=BN[p][:, eb, :], op=ALU.add),
                       reads=[bW, bin_], writes=[bW])
                dve.op(lambda e: e.tensor_tensor(out=yg[:, eb, :], in0=W["yn"][:], in1=SG[p][:, eb, :], op=ALU.mult),
                       reads=[bW, bin_], writes=[byg])
            for half in range(2):
                for eb in range(8):
                    pe.op(lambda e: e.matmul(out=pout[half][:], lhsT=yg[:, eb, :], rhs=Wout[:, eb, half * 512:(half + 1) * 512],
                                             start=(eb == 0), stop=(eb == 7)),
                          reads=[byg, b_Wout], writes=[bpout[half]], acc=True)
                dve.op(lambda e: e.tensor_tensor(out=x1[p][:, half * 512:(half + 1) * 512], in0=pout[half][:],
                                                 in1=XC[p][:, half * 512:(half + 1) * 512], op=ALU.add),
                       reads=[bpout[half], bin_], writes=[bx1[p]])
            pool.dma(x1_s[i, :, :], x1[p][:], reads=[bx1[p]])
            rmsnorm_rstd(x1[p], bx1[p], 128, junk, bjunk, ss, bss)
            act.op(lambda e: e.activation(out=hb[:], in_=x1[p][:], func=AF.Copy, scale=ss[:, 2:3]),
                   reads=[bx1[p], bss], writes=[bhb])
            for kc in range(8):
                pe.op(lambda e: e.transpose(out=ptr[:, kc, :], in_=hb[:, kc * 128:(kc + 1) * 128], identity=ident),
                      reads=[bhb, b_cbf], writes=[bptr], acc=True)
            dve.op(lambda e: e.tensor_copy(out=h1T[p][:], in_=ptr[:]), reads=[bptr], writes=[bh1T[p]])
            pool.dma(h1T_s[i, :, :, :], h1T[p][:], reads=[bh1T[p]])
        fw.barrier()
    if stop == "C":
        return nc

    with ExitStack() as ph:
        stage = sb(ph, "stageD", [128, 1536], F32)
        b_stage = Buf()
        Wc, b_Wc = load_weight_bf16(ph, "Wc", w_cvin, 3072, "g1", stage, b_stage)
        H1 = [sb(ph, "H1D%d" % i, [128, 8, 512], BF16) for i in range(2)]
        bH1 = bufs(2)
        pb = [psum(ph, "pbD%d" % i, [128, 512], F32) for i in range(6)]
        bpb = pbufs(6)
        sgl = [sb(ph, "sglD%d" % i, [128, 512], F32) for i in range(2)]
        bsgl = bufs(2)
        zt = [sb(ph, "ztD%d" % i, [128, 512], F32) for i in range(2)]
        bzt = bufs(2)
        sg1 = [sb(ph, "sg1D%d" % i, [128, 512], F32) for i in range(2)]
        bsg1 = bufs(2)
        zero = sb(ph, "zeroD", [128, 16], F32)
        bzero = Buf()
        dve.op(lambda e: e.memset(zero[:], 0.0), writes=[bzero])
        for cbk in range(8):
            pool.dma(z_s[cbk, :, 0:15], zero[:, 0:15], reads=[bzero])
            pool.dma(z_s[cbk, :, T + 15:T + 30], zero[:, 0:15], reads=[bzero])

        def loadD(g):
            p = g % 2
            for tt in range(4):
                sp.dma(H1[p][:, :, tt * 128:(tt + 1) * 128], h1T_s[4 * g + tt, :, :, :], writes=[bH1[p]])
        loadD(0)
        cnt = 0
        for g in range(NG):
            p = g % 2
            if g + 1 < NG:
                loadD(g + 1)
            for cbk in range(8):
                q = cnt % 2
                cnt += 1
                pbs = [pb[q * 3 + k] for k in range(3)]
                bpbs = [bpb[q * 3 + k] for k in range(3)]
                for k in range(3):
                    c0 = k * 1024 + cbk * 128
                    for kc in range(8):
                        pe.op(lambda e: e.matmul(out=pbs[k][:], lhsT=Wc[:, kc, c0:c0 + 128], rhs=H1[p][:, kc, :],
                                                 start=(kc == 0), stop=(kc == 7)),
                              reads=[b_Wc, bH1[p]], writes=[bpbs[k]], acc=True)
                act.op(lambda e: e.activation(out=sgl[q][:], in_=pbs[1][:], func=AF.Sigmoid, bias=pcol("bin", 8 + cbk)),
                       reads=[bpbs[1], b_pvec], writes=[bsgl[q]])
                dve.op(lambda e: e.scalar_tensor_tensor(out=zt[q][:], in0=pbs[0][:], scalar=pcol("bin", cbk), in1=sgl[q][:],
                                                        op0=ALU.add, op1=ALU.mult),
                       reads=[bpbs[0], bsgl[q], b_pvec], writes=[bzt[q]])
                pool.dma(z_s[cbk, :, 15 + g * 512:15 + (g + 1) * 512], zt[q][:], reads=[bzt[q]])
                act.op(lambda e: e.activation(out=sg1[q][:], in_=pbs[2][:], func=AF.Silu, bias=pcol("bin", 16 + cbk)),
                       reads=[bpbs[2], b_pvec], writes=[bsg1[q]])
                pool.dma(sg1_s[g, cbk, :, :], sg1[q][:], reads=[bsg1[q]])
        fw.barrier()
    if stop == "D1":
        return nc

    with ExitStack() as ph:
        stage = sb(ph, "stageE", [128, 1536], F32)
        b_stage = Buf()
        Wo, b_Wo = load_weight_bf16(ph, "Wo", w_cvout, 1024, None, stage, b_stage)
        bcs = sb(ph, "bcsE", [128, 2, 1024], F32)
        b_bc = Buf()
        sp.dma(bcs[:], bc_d[:, :, :], writes=[b_bc])
        zw = [sb(ph, "zwE%d" % i, [128, 542], F32) for i in range(3)]
        bzw = bufs(3)
        zc = sb(ph, "zcE", [128, 8, 512], F32)
        bzc = bufs(8)
        sq = [sb(ph, "sqE%d" % i, [128, 512], F32) for i in range(2)]
        bsq = bufs(2)
        pmean = psum(ph, "pmeanE", [128, 512], F32)
        bpmean = Buf(True)
        pvar = psum(ph, "pvarE", [128, 512], F32)
        bpvar = Buf(True)
        rs = sb(ph, "rsE", [128, 512], F32)
        brs = Buf()
        sg1 = [sb(ph, "sg1E%d" % i, [128, 512], F32) for i in range(2)]
        bsg1 = bufs(2)
        s1 = [sb(ph, "s1E%d" % i, [128, 512], F32) for i in range(2)]
        bs1 = bufs(2)
        ZF = sb(ph, "ZFE", [128, 8, 512], BF16)
        bZF = Buf()
        pout = [psum(ph, "poutE%d" % i, [128, 512], F32) for i in range(4)]
        bpout = pbufs(4)
        x1 = [sb(ph, "x1E%d" % i, [128, 1024], F32) for i in range(2)]
        bx1 = bufs(2)
        x2 = [sb(ph, "x2E%d" % i, [128, 1024], F32) for i in range(2)]
        bx2 = bufs(2)
        yo = [sb(ph, "yoE%d" % i, [128, 1024], F32) for i in range(2)]
        byo = bufs(2)
        junk = sb(ph, "junkE", [128, 1024], F32)
        bjunk = Buf()
        ss = sb(ph, "ssE", [128, 4], F32)
        bss = Buf()
        lc = 0
        oc = 0
        for g in range(NG):
            for cbk in range(8):
                q = lc % 3
                lc += 1
                sp.dma(zw[q][:], z_s[cbk, :, g * 512:g * 512 + 542], writes=[bzw[q]])
                dve.op(lambda e: e.tensor_scalar(out=zw[q][:, 0:15], in0=zw[q][:, 0:15], scalar1=cmask[:, 2 * g:2 * g + 1],
                                                 scalar2=None, op0=ALU.mult), reads=[bzw[q], b_mask], writes=[bzw[q]])
                dve.op(lambda e: e.tensor_scalar(out=zw[q][:, 527:542], in0=zw[q][:, 527:542],
                                                 scalar1=cmask[:, 2 * g + 1:2 * g + 2], scalar2=None, op0=ALU.mult),
                       reads=[bzw[q], b_mask], writes=[bzw[q]])
                dwc = PV["dw"] + cbk * 31
                dve.op(lambda e: e.tensor_scalar(out=zc[:, cbk, :], in0=zw[q][:, 0:512], scalar1=pvec[:, dwc:dwc + 1],
                                                 scalar2=pcol("bdw", cbk), op0=ALU.mult, op1=ALU.add),
                       reads=[bzw[q], b_pvec], writes=[bzc[cbk]])
                for j in range(1, 31):
                    dve.op(lambda e: e.scalar_tensor_tensor(out=zc[:, cbk, :], in0=zw[q][:, j:j + 512],
                                                            scalar=pvec[:, dwc + j:dwc + j + 1], in1=zc[:, cbk, :],
                                                            op0=ALU.mult, op1=ALU.add),
                           reads=[bzw[q], b_pvec, bzc[cbk]], writes=[bzc[cbk]])
                pe.op(lambda e: e.matmul(out=pmean[:], lhsT=O1k, rhs=zc[:, cbk, :], start=(cbk == 0), stop=(cbk == 7)),
                      reads=[bzc[cbk], b_cf], writes=[bpmean], acc=True)
            for cbk in range(8):
                q = cbk % 2
                dve.op(lambda e: e.tensor_tensor(out=zc[:, cbk, :], in0=zc[:, cbk, :], in1=pmean[:], op=ALU.subtract),
                       reads=[bzc[cbk], bpmean], writes=[bzc[cbk]])
                act.op(lambda e: e.activation(out=sq[q][:], in_=zc[:, cbk, :], func=AF.Square),
                       reads=[bzc[cbk]], writes=[bsq[q]])
                pe.op(lambda e: e.matmul(out=pvar[:], lhsT=O1k, rhs=sq[q][:], start=(cbk == 0), stop=(cbk == 7)),
                      reads=[bsq[q], b_cf], writes=[bpvar], acc=True)
            act.op(lambda e: e.activation(out=rs[:], in_=pvar[:], func=AF.Sqrt, bias=cst[:, 0:1]),
                   reads=[bpvar, b_cst], writes=[brs])
            dve.op(lambda e: e.reciprocal(out=rs[:], in_=rs[:]), reads=[brs], writes=[brs])
            for cbk in range(8):
                q = cbk % 2
                sp.dma(sg1[q][:], sg1_s[g, cbk, :, :], writes=[bsg1[q]])
                dve.op(lambda e: e.tensor_tensor(out=s1[q][:], in0=zc[:, cbk, :], in1=rs[:], op=ALU.mult),
                       reads=[bzc[cbk], brs], writes=[bs1[q]])
                act.op(lambda e: e.activation(out=s1[q][:], in_=s1[q][:], func=AF.Silu, scale=pcol("lng", cbk),
                                              bias=pcol("lnb", cbk)), reads=[bs1[q], b_pvec], writes=[bs1[q]])
                dve.op(lambda e: e.tensor_tensor(out=ZF[:, cbk, :], in0=s1[q][:], in1=sg1[q][:], op=ALU.mult),
                       reads=[bs1[q], bsg1[q]], writes=[bZF])
            for tt in range(4):
                i = 4 * g + tt
                p = oc % 2
                oc += 1
                sp.dma(x1[p][:], x1_s[i, :, :], writes=[bx1[p]])
                for half in range(2):
                    pk = p * 2 + half
                    hs = slice(half * 512, (half + 1) * 512)
                    for cbk in range(8):
                        pe.op(lambda e: e.matmul(out=pout[pk][:], lhsT=ZF[:, cbk, tt * 128:(tt + 1) * 128],
                                                 rhs=Wo[:, cbk, hs], start=(cbk == 0), stop=(cbk == 7)),
                              reads=[bZF, b_Wo], writes=[bpout[pk]], acc=True)
                    dve.op(lambda e: e.tensor_tensor(out=x2[p][:, hs], in0=pout[pk][:], in1=bcs[:, 0, hs], op=ALU.add),
                           reads=[bpout[pk], b_bc], writes=[bx2[p]])
                dve.op(lambda e: e.tensor_tensor(out=x2[p][:], in0=x2[p][:], in1=x1[p][:], op=ALU.add),
                       reads=[bx2[p], bx1[p]], writes=[bx2[p]])
                rmsnorm_rstd(x2[p], bx2[p], 128, junk, bjunk, ss, bss)
                dve.op(lambda e: e.scalar_tensor_tensor(out=yo[p][:], in0=x2[p][:], scalar=ss[:, 2:3], in1=bcs[:, 1, :],
                                                        op0=ALU.mult, op1=ALU.mult),
                       reads=[bx2[p], bss, b_bc], writes=[byo[p]])
                pool.dma(y_out[i, :, :], yo[p][:], reads=[byo[p]])
        fw.barrier()
    return nc


def _fm(v):
    v = np.asarray(v, np.float32).reshape(-1)
    return np.ascontiguousarray(v.reshape(-1, 128).T)


def _wrows(w):
    w = np.asarray(w, np.float32)
    return np.ascontiguousarray(w.reshape(8, 128, w.shape[1]).transpose(1, 0, 2))


def shared_inputs(norm_g, final_g, rw_in, rw_mu, rw_w0, rw_w2, rw_a0, rw_a2, rw_kk, rw_ka, rw_rk,
                  rw_lnx_g, rw_lnx_b, rw_out, cv_in, cv_b_in, cv_dw, cv_b_dw, cv_ln_g, cv_ln_b, cv_out, cv_b_out):
    pv = np.zeros((128, NPV), np.float32)

    def put(name, arr):
        pv[:, PV[name]:PV[name] + arr.shape[1]] = arr
    put("g0", _fm(norm_g[0]))
    put("g1", _fm(norm_g[1]))
    put("w0f", _fm(rw_w0[0, 0]))
    put("w0b", _fm(rw_w0[0, 1]))
    put("a0f", _fm(rw_a0[0, 0]))
    put("a0b", _fm(rw_a0[0, 1]))
    put("kk", _fm(rw_kk[0]))
    put("ka", _fm(rw_ka[0]))
    put("rk", _fm(rw_rk[0]))
    put("lnxg", _fm(rw_lnx_g[0]))
    put("lnxb", _fm(rw_lnx_b[0]))
    put("bdw", _fm(cv_b_dw[0]))
    put("lng", _fm(cv_ln_g[0]))
    put("lnb", _fm(cv_ln_b[0]))
    put("mu", _fm(rw_mu[0]))
    put("bin", _fm(cv_b_in[0]))
    dw = np.asarray(cv_dw[0], np.float32)
    dwf = dw.T.reshape(8, 128, 31).transpose(1, 0, 2).reshape(128, 248)
    put("dw", dwf)
    w2a2 = np.stack([np.asarray(rw_w2[0], np.float32).reshape(128, 1024),
                     np.asarray(rw_a2[0], np.float32).reshape(128, 1024)], axis=1)
    bc = np.stack([np.broadcast_to(np.asarray(cv_b_out[0], np.float32), (128, 1024)),
                   np.broadcast_to(np.asarray(final_g, np.float32), (128, 1024))], axis=1)
    r = np.arange(128)
    eye = (r[:, None] == r[None, :])
    su = (r[:, None] < r[None, :])
    u = (r[:, None] <= r[None, :])
    sl = (r[:, None] > r[None, :])
    l = (r[:, None] >= r[None, :])
    cbf = np.stack([np.tile(m.astype(np.float32), (1, 4)) for m in (eye, su, u, sl, l)], axis=1).astype(ml_dtypes.bfloat16)
    blk = ((r[:, None] // 64) == (r[None, :] // 64)).astype(np.float32)
    cf32 = np.stack([blk / 64.0, blk, np.full((128, 128), 1.0 / 1024.0, np.float32)], axis=1).astype(np.float32)
    return {
        "w_rwin": _wrows(rw_in[0]), "w_rwout": _wrows(rw_out[0]), "w_cvin": _wrows(cv_in[0]),
        "w_cvout": _wrows(cv_out[0]), "w2a2": np.ascontiguousarray(w2a2), "pvec": pv,
        "bc": np.ascontiguousarray(bc), "cbf": np.ascontiguousarray(cbf), "cf32": np.ascontiguousarray(cf32),
    }


def core_inputs(seqs):
    xs = np.concatenate([np.asarray(s, np.float32) for s in seqs], axis=0)
    T = xs.shape[0]
    NT, NG = T // 128, T // 512
    starts = set()
    o = 0
    for s in seqs:
        starts.add(o)
        o += s.shape[0]
    starts.add(T)
    x_t = xs.reshape(NT, 128, 1024)
    x_h = np.zeros((NT, 2, 1024), np.float32)
    sm = np.ones((NT, 2), np.float32)
    for i in range(NT):
        t0, t1 = i * 128, (i + 1) * 128
        if t0 not in starts:
            x_h[i, 0] = xs[t0 - 1]
        if t1 not in starts:
            x_h[i, 1] = xs[t1]
        if t1 in starts:
            sm[i, 0] = 0.0
        if t0 in starts:
            sm[i, 1] = 0.0
    cm = np.ones((NG, 2), np.float32)
    for g in range(NG):
        if g * 512 in starts:
            cm[g, 0] = 0.0
        if (g + 1) * 512 in starts:
            cm[g, 1] = 0.0
    return {
        "x_t": np.ascontiguousarray(x_t), "x_h": x_h,
        "smask": np.ascontiguousarray(np.broadcast_to(sm.reshape(1, -1), (128, NT * 2))),
        "cmask": np.ascontiguousarray(np.broadcast_to(cm.reshape(1, -1), (128, NG * 2))),
    }


def kernel(x_prompt, x_sample, norm_g, final_g, rw_in, rw_mu, rw_w0, rw_w2, rw_a0, rw_a2, rw_kk, rw_ka, rw_rk,
           rw_lnx_g, rw_lnx_b, rw_out, cv_in, cv_b_in, cv_dw, cv_b_dw, cv_ln_g, cv_ln_b, cv_out, cv_b_out):
    x_prompt = np.asarray(x_prompt, np.float32)
    x_sample = np.asarray(x_sample, np.float32)
    T = 16384
    sh = shared_inputs(norm_g, final_g, rw_in, rw_mu, rw_w0, rw_w2, rw_a0, rw_a2, rw_kk, rw_ka, rw_rk,
                       rw_lnx_g, rw_lnx_b, rw_out, cv_in, cv_b_in, cv_dw, cv_b_dw, cv_ln_g, cv_ln_b, cv_out, cv_b_out)
    cores = [core_inputs([x_prompt[0]]), core_inputs([x_prompt[1]]),
             core_inputs([x_sample[b] for b in range(8)])]
    in_maps = []
    for c in range(8):
        m = dict(sh)
        m.update(cores[min(c, 2)])
        in_maps.append(m)
    nc = build(T)
    res = run_bass_kernel_spmd(nc, in_maps, core_ids=list(range(8)))
    ys = [np.asarray(res.results[c]["y"], np.float32).reshape(T, 1024) for c in range(3)]
    y_prompt = np.stack([ys[0], ys[1]], axis=0)
    y_sample = ys[2].reshape(8, 2048, 1024)
    return (y_prompt, y_sample)
```

```python
import bisect
import numpy as np
import ml_dtypes
from contextlib import ExitStack
import concourse.bass as bass
import concourse.mybir as mybir
from concourse.bass_utils import run_bass_kernel_spmd

F32 = mybir.dt.float32
BF16 = mybir.dt.bfloat16
ALU = mybir.AluOpType
AF = mybir.ActivationFunctionType
CDEC = 0.6065306597126334


class Buf:
    __slots__ = ("w", "r", "excl")

    def __init__(self, excl=False):
        self.w = None
        self.r = {}
        self.excl = excl


def bufs(n):
    return [Buf() for _ in range(n)]


def pbufs(n):
    return [Buf(True) for _ in range(n)]


class Queue:
    def __init__(self, fw, eng, name, is_pe=False):
        self.fw = fw
        self.eng = eng
        self.name = name
        self.sem = fw.es.enter_context(fw.nc.semaphore("q_" + name))
        self.n = 0
        self.known = {}
        self.is_pe = is_pe
        self.dma_sems = []
        self.dma_cnt = []
        self.dma_k = 0
        self.waited = set()
        self.plan = None if fw.plan is None else fw.plan[name]
        self.planset = None if self.plan is None else set(self.plan)

    def add_dma_sems(self, k):
        for i in range(k):
            self.dma_sems.append(self.fw.es.enter_context(self.fw.nc.semaphore("d_%s_%d" % (self.name, i))))
            self.dma_cnt.append(0)

    def _val(self, n):
        if self.plan is None:
            return n
        return bisect.bisect_right(self.plan, n)

    def _wait(self, tok):
        key, n = tok
        if self.known.get(key, 0) >= n:
            return
        self.known[key] = n
        if isinstance(key, Queue):
            key.waited.add(n)
            self.eng.wait_ge(key.sem, key._val(n))
        else:
            self.eng.wait_ge(key, n)

    def _deps(self, reads, writes, skip_same_waw=False):
        deps = {}

        def add(tok):
            if tok is None:
                return
            s, v = tok
            if deps.get(s, 0) < v:
                deps[s] = v
        for b in reads:
            add(b.w)
        for b in writes:
            if not (skip_same_waw and b.w is not None and b.w[0] is self):
                add(b.w)
            for s, v in b.r.items():
                add((s, v))
        for s, v in deps.items():
            self._wait((s, v))

    def _record(self, tok, reads, writes):
        s, v = tok
        for b in reads:
            if b.r.get(s, 0) < v:
                b.r[s] = v
        for b in writes:
            b.w = tok
            b.r = {}

    def op(self, fn, reads=(), writes=(), acc=False):
        if any(b.excl for b in reads):
            writes = list(writes) + [b for b in reads if b.excl]
            reads = [b for b in reads if not b.excl]
        self._deps(reads, writes, skip_same_waw=(acc and self.is_pe))
        ins = fn(self.eng)
        self.n += 1
        if self.planset is None or self.n in self.planset:
            ins.then_inc(self.sem, 1)
        self._record((self, self.n), reads, writes)
        return ins

    def dma(self, out, in_, reads=(), writes=()):
        self._deps(reads, writes)
        i = self.dma_k % len(self.dma_sems)
        self.dma_k += 1
        s = self.dma_sems[i]
        if self.dma_cnt[i] > 0:
            self._wait((s, 16 * self.dma_cnt[i]))
        self.eng.dma_start(out=out, in_=in_).then_inc(s, 16)
        self.dma_cnt[i] += 1
        self._record((s, 16 * self.dma_cnt[i]), reads, writes)


class FW:
    def __init__(self, nc, plan=None):
        self.nc = nc
        self.plan = plan
        self.es = ExitStack()
        self.pe = Queue(self, nc.tensor, "pe", is_pe=True)
        self.dve = Queue(self, nc.vector, "dve")
        self.act = Queue(self, nc.scalar, "act")
        self.pool = Queue(self, nc.gpsimd, "pool")
        self.sp = Queue(self, nc.sync, "sp")
        self.sp.add_dma_sems(8)
        self.pool.add_dma_sems(8)
        self.queues = [self.pe, self.dve, self.act, self.pool, self.sp]

    def all_tokens(self):
        toks = []
        for q in self.queues:
            if q.n > 0:
                toks.append((q, q.n))
            for s, c in zip(q.dma_sems, q.dma_cnt):
                if c > 0:
                    toks.append((s, 16 * c))
        return toks

    def barrier(self):
        toks = self.all_tokens()
        for q in self.queues:
            for tok in toks:
                q._wait(tok)

    def get_plan(self):
        return {q.name: sorted(q.waited) for q in self.queues}


PV = {}
_o = 0
for _n, _w in [("g0", 8), ("g1", 8), ("w0f", 8), ("w0b", 8), ("a0f", 8), ("a0b", 8), ("kk", 8), ("ka", 8),
               ("rk", 8), ("lnxg", 8), ("lnxb", 8), ("bdw", 8), ("lng", 8), ("lnb", 8), ("mu", 26),
               ("bin", 24), ("dw", 248)]:
    PV[_n] = _o
    _o += _w
NPV = _o


def _build(T, stop=None, plan=None):
    NT = T // 128
    NG = T // 512
    nc = bass.Bass("TRN2", target_bir_lowering=False)
    fw = FW(nc, plan)
    nc._fw = fw
    pe, dve, act, pool, sp = fw.pe, fw.dve, fw.act, fw.pool, fw.sp

    def dram(name, shape, dt, kind="Internal"):
        return nc.dram_tensor(name, shape, dt, kind=kind).ap()

    x_t = dram("x_t", [NT, 128, 1024], F32, "ExternalInput")
    x_h = dram("x_h", [NT, 2, 1024], F32, "ExternalInput")
    smask_d = dram("smask", [128, NT * 2], F32, "ExternalInput")
    cmask_d = dram("cmask", [128, NG * 2], F32, "ExternalInput")
    w_rwin = dram("w_rwin", [128, 8, 4352], F32, "ExternalInput")
    w_rwout = dram("w_rwout", [128, 8, 1024], F32, "ExternalInput")
    w_cvin = dram("w_cvin", [128, 8, 3072], F32, "ExternalInput")
    w_cvout = dram("w_cvout", [128, 8, 1024], F32, "ExternalInput")
    w2a2_d = dram("w2a2", [128, 2, 1024], F32, "ExternalInput")
    pvec_d = dram("pvec", [128, NPV], F32, "ExternalInput")
    bc_d = dram("bc", [128, 2, 1024], F32, "ExternalInput")
    cbf_d = dram("cbf", [128, 5, 512], BF16, "ExternalInput")
    cf32_d = dram("cf32", [128, 3, 128], F32, "ExternalInput")
    y_out = dram("y", [NT, 128, 1024], F32, "ExternalOutput")

    fm_s = dram("fm_s", [NT, 128, 2, 4, 1024], BF16)
    tm_s = dram("tm_s", [NT, 128, 7, 1024], BF16)
    eg_s = dram("eg_s", [NT, 128, 2, 512], F32)
    bon_s = dram("bon_s", [NT, 128, 1024], F32)
    sg_s = dram("sg_s", [NT, 128, 1024], F32)
    yT_s = dram("yT_s", [2, NT, 128, 1024], F32)
    x1_s = dram("x1_s", [NT, 128, 1024], F32)
    h1T_s = dram("h1T_s", [NT, 128, 8, 128], BF16)
    z_s = dram("z_s", [8, 128, T + 30], F32)
    sg1_s = dram("sg1_s", [NG, 8, 128, 512], F32)

    es = fw.es

    def sb(st, name, shape, dt):
        return st.enter_context(nc.sbuf_tensor(name, shape, dt))

    def psum(st, name, shape, dt):
        return st.enter_context(nc.psum_tensor(name, shape, dt))

    pvec = sb(es, "pvec_sb", [128, NPV], F32)
    b_pvec = Buf()
    cbf = sb(es, "cbf_sb", [128, 5, 512], BF16)
    b_cbf = Buf()
    cf32 = sb(es, "cf32_sb", [128, 3, 128], F32)
    b_cf = Buf()
    cst = sb(es, "cst", [128, 4], F32)
    b_cst = Buf()
    ones = sb(es, "ones", [128, 128], F32)
    b_ones = Buf()
    smask = sb(es, "smask_sb", [128, NT * 2], F32)
    cmask = sb(es, "cmask_sb", [128, NG * 2], F32)
    b_mask = Buf()
    sp.dma(pvec[:], pvec_d[:, :], writes=[b_pvec])
    sp.dma(cbf[:], cbf_d[:, :, :], writes=[b_cbf])
    sp.dma(cf32[:], cf32_d[:, :, :], writes=[b_cf])
    sp.dma(smask[:], smask_d[:, :], writes=[b_mask])
    sp.dma(cmask[:], cmask_d[:, :], writes=[b_mask])
    dve.op(lambda e: e.memset(cst[:, 0:1], 1e-5), writes=[b_cst])
    dve.op(lambda e: e.memset(cst[:, 1:2], 64e-5), writes=[b_cst])
    dve.op(lambda e: e.memset(cst[:, 2:3], 0.0), writes=[b_cst])
    dve.op(lambda e: e.memset(ones[:], 1.0), writes=[b_ones])
    ident = cbf[:, 0, 0:128]
    BO = cf32[:, 1, :]
    BOm = cf32[:, 0, :]
    O1k = cf32[:, 2, :]

    def pcol(name, j):
        return pvec[:, PV[name] + j: PV[name] + j + 1]

    CONSTS = [b_pvec, b_cbf, b_cf, b_cst, b_ones, b_mask]

    def load_weight_bf16(st, name, src, ncols, gname, dst_stage, b_stage, w=None):
        if w is None:
            w = sb(st, name, [128, 8, ncols], BF16)
        bw = Buf()
        for kc in range(8):
            for c0 in range(0, ncols, 1536):
                c1 = min(ncols, c0 + 1536)
                sp.dma(dst_stage[:, 0:c1 - c0], src[:, kc, c0:c1], writes=[b_stage])
                if gname is None:
                    act.op(lambda e: e.activation(out=w[:, kc, c0:c1], in_=dst_stage[:, 0:c1 - c0], func=AF.Copy),
                           reads=[b_stage], writes=[bw])
                else:
                    act.op(lambda e: e.activation(out=w[:, kc, c0:c1], in_=dst_stage[:, 0:c1 - c0], func=AF.Copy,
                                                  scale=pcol(gname, kc)),
                           reads=[b_stage, b_pvec], writes=[bw])
        return w, bw

    def rmsnorm_rstd(xt, bx, npart, junk, bjunk, ss, bss):
        act.op(lambda e: e.activation(out=junk[0:npart, :], in_=xt[0:npart, :], func=AF.Square,
                                      accum_out=ss[0:npart, 0:1]), reads=[bx], writes=[bjunk, bss])
        act.op(lambda e: e.activation(out=ss[0:npart, 1:2], in_=ss[0:npart, 0:1], func=AF.Sqrt,
                                      scale=1.0 / 1024.0, bias=cst[0:npart, 0:1]), reads=[bss, b_cst], writes=[bss])
        dve.op(lambda e: e.reciprocal(out=ss[0:npart, 2:3], in_=ss[0:npart, 1:2]), reads=[bss], writes=[bss])

    with ExitStack() as ph:
        pro = ExitStack()
        Win = sb(ph, "Win", [128, 8, 4352], BF16)
        w2b = sb(ph, "w2b", [128, 4, 1024], BF16)
        stage = sb(pro, "stageA", [128, 1536], F32)
        b_stage = Buf()
        Win, b_Win = load_weight_bf16(ph, "Win", w_rwin, 4352, "g0", stage, b_stage, w=Win)
        w2f = sb(pro, "w2f", [128, 2, 1024], F32)
        b_w2 = Buf()
        sp.dma(w2f[:], w2a2_d[:, :, :], writes=[b_w2])
        dve.op(lambda e: e.memset(w2b[:], 0.0), writes=[b_w2])
        for wa in range(2):
            for d in range(2):
                dve.op(lambda e: e.tensor_copy(out=w2b[d * 64:(d + 1) * 64, wa * 2 + d, :],
                                               in_=w2f[d * 64:(d + 1) * 64, wa, :]), reads=[b_w2], writes=[b_w2])
        fw.barrier()
        pro.close()
        omm = sb(ph, "omm", [128, 26], F32)
        hmu = sb(ph, "hmu", [128, 26], F32)
        b_mu = Buf()
        mu_ap = pvec[:, PV["mu"]:PV["mu"] + 26]
        dve.op(lambda e: e.tensor_scalar(out=omm[:], in0=mu_ap, scalar1=-1.0, scalar2=1.0, op0=ALU.mult, op1=ALU.add),
               reads=[b_pvec], writes=[b_mu])
        dve.op(lambda e: e.tensor_scalar(out=hmu[:], in0=mu_ap, scalar1=0.5, scalar2=None, op0=ALU.mult),
               reads=[b_pvec], writes=[b_mu])

        X = [sb(ph, "xA%d" % i, [128, 1024], F32) for i in range(2)]
        bX = bufs(2)
        XH = [sb(ph, "xhA%d" % i, [2, 1024], F32) for i in range(2)]
        bXH = bufs(2)
        junk = sb(ph, "junkA", [128, 1024], BF16)
        bjunk = Buf()
        ss = sb(ph, "ssA", [128, 4], F32)
        bss = Buf()
        ssh = sb(ph, "sshA", [128, 4], F32)
        bssh = Buf()
        hb = sb(ph, "hbA", [128, 1024], BF16)
        bhb = Buf()
        hhb = sb(ph, "hhbA", [2, 1024], BF16)
        bhhb = Buf()
        hT = [sb(ph, "hTA%d" % i, [128, 8, 130], BF16) for i in range(2)]
        bhT = bufs(2)
        ptr = psum(ph, "ptrA", [128, 8, 128], BF16)
        bptr = Buf(True)
        ptrh = psum(ph, "ptrhA", [128, 1024], BF16)[:, 0:16].rearrange("p (a b) -> p a b", b=2)
        bptrh = Buf(True)
        pin = [psum(ph, "pinA%d" % i, [128, 512], F32)[:, 0:390].rearrange("p (a b) -> p a b", b=130) for i in range(2)]
        bpin = pbufs(2)
        PL = [psum(ph, "plA%d" % k, [128, 4, 128], F32) for k in range(2)]
        bPL = pbufs(2)
        pss = psum(ph, "pssA", [128, 512], F32)[:, 0:256].rearrange("p (a b) -> p a b", b=128)
        bpss = pbufs(1) * 2
        ptq = psum(ph, "ptqA", [128, 8, 128], BF16)
        bptq = Buf(True)
        ue = [sb(ph, "ueA%d" % i, [128, 3, 130], F32) for i in range(2)]
        bue = bufs(2)
        t1 = [sb(ph, "t1A%d" % i, [128, 3, 128], F32) for i in range(2)]
        bt1 = bufs(2)
        t3 = sb(ph, "t3A", [128, 128], F32)
        bt3 = Buf()
        RKV = sb(ph, "rkvA", [128, 3, 8, 128], F32)
        bRKV = [bufs(8) for _ in range(3)]
        LO = sb(ph, "loA", [128, 2, 128], F32)
        bLO = Buf()
        twb = sb(ph, "twA", [128, 2, 128], BF16)
        btw = Buf()
        sgt = [sb(ph, "sgA%d" % i, [128, 8, 128], F32) for i in range(1)] * 2
        bsgt = bufs(1) * 2
        FM = [sb(ph, "fmA%d" % i, [128, 2, 4, 1024], BF16) for i in range(1)] * 2
        bFM = bufs(1) * 2
        TMb = [sb(ph, "tmA%d" % i, [128, 7, 1024], BF16) for i in range(1)] * 2
        bTM = bufs(1) * 2
        EG = [sb(ph, "egA%d" % i, [128, 2, 512], F32) for i in range(1)] * 2
        bEG = bufs(1) * 2
        BON = [sb(ph, "bonA%d" % i, [128, 8, 128], F32) for i in range(1)] * 2
        bBON = bufs(1) * 2
        GT = {}
        for nm in ["sgw0", "sgw1", "a0", "a1", "cs0", "cs1", "u0", "u1", "kk", "kk2", "kkn",
                   "tmp0", "tmp1", "tmq0", "tmq1", "rk"]:
            GT[nm] = (sb(ph, "g_" + nm, [128, 4, 128], F32), bufs(4))
        SC = sb(ph, "g_sc", [128, 4, 8], F32)
        bSC = bufs(4)
        VBF = [sb(ph, "vbfA%d" % k, [128, 128], BF16) for k in range(2)]
        bVBF = bufs(2)
        bFMe = bufs(8)
        if stop == "A0":
            fw.barrier()
            return nc

        for i in range(NT):
            par = i % 2
            xm, bxm, xh, bxh = X[par], bX[par], XH[par], bXH[par]
            sp.dma(xm[:], x_t[i, :, :], writes=[bxm])
            sp.dma(xh[:], x_h[i, :, :], writes=[bxh])
            rmsnorm_rstd(xm, bxm, 128, junk, bjunk, ss, bss)
            act.op(lambda e: e.activation(out=hb[:], in_=xm[:], func=AF.Copy, scale=ss[:, 2:3]),
                   reads=[bxm, bss], writes=[bhb])
            rmsnorm_rstd(xh, bxh, 2, junk, bjunk, ssh, bssh)
            act.op(lambda e: e.activation(out=hhb[:], in_=xh[:], func=AF.Copy, scale=ssh[0:2, 2:3]),
                   reads=[bxh, bssh], writes=[bhhb])
            hTt, bhTt = hT[par], bhT[par]
            for kc in range(8):
                pe.op(lambda e: e.transpose(out=ptr[:, kc, :], in_=hb[:, kc * 128:(kc + 1) * 128], identity=ident),
                      reads=[bhb, b_cbf], writes=[bptr], acc=True)
            dve.op(lambda e: e.tensor_copy(out=hTt[:, :, 1:129], in_=ptr[:]), reads=[bptr], writes=[bhTt])
            for kc in range(8):
                pe.op(lambda e: e.transpose(out=ptrh[:, kc, :], in_=hhb[0:2, kc * 128:(kc + 1) * 128],
                                            identity=cbf[0:2, 0, 0:2]),
                      reads=[bhhb, b_cbf], writes=[bptrh], acc=True)
            dve.op(lambda e: e.tensor_copy(out=hTt[:, :, 0:130:129], in_=ptrh[:]), reads=[bptrh], writes=[bhTt])
            if stop == "A1":
                fw.barrier()
                return nc

            for bg in range(9):
                pp = bg % 2
                cbs = [cb for cb in range(bg * 3, min(26, bg * 3 + 3))]
                for q, cb in enumerate(cbs):
                    for kc in range(8):
                        pe.op(lambda e: e.matmul(out=pin[pp][:, q, :], lhsT=Win[:, kc, cb * 128:(cb + 1) * 128],
                                                 rhs=hTt[:, kc, :], start=(kc == 0), stop=(kc == 7)),
                              reads=[b_Win, bhTt], writes=[bpin[pp]], acc=True)
                nq = len(cbs)
                act.op(lambda e: e.activation(out=ue[pp][:, 0:nq, :], in_=pin[pp][:, 0:nq, :], func=AF.Copy),
                       reads=[bpin[pp]], writes=[bue[pp]])
                dve.op(lambda e: e.tensor_tensor(out=t1[pp][:, 0:nq, :], in0=ue[pp][:, 0:nq, 0:128],
                                                 in1=ue[pp][:, 0:nq, 2:130], op=ALU.add),
                       reads=[bue[pp]], writes=[bt1[pp]])
                for q, cb in enumerate(cbs):
                    if cb < 24:
                        dst = RKV[:, cb // 8, cb % 8, :]
                        bd = bRKV[cb // 8][cb % 8]
                    else:
                        dst = LO[:, cb - 24, :]
                        bd = bLO
                    dve.op(lambda e: e.tensor_scalar(out=t3[:], in0=t1[pp][:, q, :], scalar1=hmu[:, cb:cb + 1],
                                                     scalar2=None, op0=ALU.mult),
                           reads=[bt1[pp], b_mu], writes=[bt3])
                    dve.op(lambda e: e.scalar_tensor_tensor(out=dst, in0=ue[pp][:, q, 1:129], scalar=omm[:, cb:cb + 1],
                                                            in1=t3[:], op0=ALU.mult, op1=ALU.add),
                           reads=[bue[pp], bt3, b_mu], writes=[bd])
            for gb, gcbs in enumerate([[0, 1, 2], [3, 4, 5], [6, 7]]):
                pp = (9 + gb) % 2
                for q, cb in enumerate(gcbs):
                    for kc in range(8):
                        pe.op(lambda e: e.matmul(out=pin[pp][:, q, 0:128],
                                                 lhsT=Win[:, kc, 3328 + cb * 128:3328 + (cb + 1) * 128],
                                                 rhs=hTt[:, kc, 1:129], start=(kc == 0), stop=(kc == 7)),
                              reads=[b_Win, bhTt], writes=[bpin[pp]], acc=True)
                ng = len(gcbs)
                act.op(lambda e: e.activation(out=sgt[par][:, gcbs[0]:gcbs[0] + ng, :], in_=pin[pp][:, 0:ng, 0:128],
                                              func=AF.Silu), reads=[bpin[pp]], writes=[bsgt[par]])
            pool.dma(sg_s[i, :, :], sgt[par][:].rearrange("p a b -> p (a b)"), reads=[bsgt[par]])
            if stop == "A2":
                fw.barrier()
                return nc

            act.op(lambda e: e.activation(out=twb[:, 0, :], in_=LO[:, 0, :], func=AF.Tanh), reads=[bLO], writes=[btw])
            act.op(lambda e: e.activation(out=twb[:, 1, :], in_=LO[:, 1, :], func=AF.Copy), reads=[bLO], writes=[btw])

            fmt = FM[par]
            tmt, btmt = TMb[par], bTM[par]
            egt, begt = EG[par], bEG[par]
            bont, bbont = BON[par], bBON[par]
            for hg in range(2):
                ebs = [4 * hg + e4 for e4 in range(4)]

                def T_(nm, e4):
                    return GT[nm][0][:, e4, :]

                def B_(nm, e4):
                    return GT[nm][1][e4]
                for e4, eb in enumerate(ebs):
                    plb, bplb = PL[e4 % 2], bPL[e4 % 2]
                    es_ = slice(eb * 128, (eb + 1) * 128)
                    for slot in range(4):
                        wa = slot // 2
                        pe.op(lambda e: e.matmul(out=plb[:, slot, :], lhsT=w2b[:, slot, es_], rhs=twb[:, wa, :],
                                                 start=True, stop=True), reads=[b_w2, btw], writes=[bplb], acc=True)
                    for d in range(2):
                        act.op(lambda e: e.activation(out=T_("sgw%d" % d, e4), in_=plb[:, d, :], func=AF.Sigmoid,
                                                      bias=pcol("w0f" if d == 0 else "w0b", eb)),
                               reads=[bplb, b_pvec], writes=[B_("sgw%d" % d, e4)])
                        act.op(lambda e: e.activation(out=T_("a%d" % d, e4), in_=plb[:, 2 + d, :], func=AF.Sigmoid,
                                                      bias=pcol("a0f" if d == 0 else "a0b", eb)),
                               reads=[bplb, b_pvec], writes=[B_("a%d" % d, e4)])
                for e4 in range(4):
                    for d in range(2):
                        dve.op(lambda e: e.tensor_tensor_scan(out=T_("cs%d" % d, e4), data0=ones[:], data1=T_("sgw%d" % d, e4),
                                                              initial=0.0, op0=ALU.mult, op1=ALU.add),
                               reads=[B_("sgw%d" % d, e4), b_ones], writes=[B_("cs%d" % d, e4)])
                for e4 in range(4):
                    for d in range(2):
                        pool.op(lambda e: e.tensor_tensor(out=T_("u%d" % d, e4), in0=T_("cs%d" % d, e4), in1=T_("sgw%d" % d, e4),
                                                          op=ALU.subtract),
                                reads=[B_("cs%d" % d, e4), B_("sgw%d" % d, e4)], writes=[B_("u%d" % d, e4)])
                    dve.op(lambda e: e.tensor_scalar(out=SC[:, e4, 0:1], in0=GT["cs1"][0][:, e4, 127:128], scalar1=-CDEC,
                                                     scalar2=None, op0=ALU.mult), reads=[B_("cs1", e4)], writes=[bSC[e4]])
                    dve.op(lambda e: e.tensor_scalar(out=SC[:, e4, 1:2], in0=GT["cs1"][0][:, e4, 127:128], scalar1=CDEC,
                                                     scalar2=None, op0=ALU.mult), reads=[B_("cs1", e4)], writes=[bSC[e4]])
                for e4 in range(4):
                    act.op(lambda e: e.activation(out=SC[:, e4, 2:3], in_=GT["cs0"][0][:, e4, 127:128], func=AF.Exp, scale=-CDEC),
                           reads=[B_("cs0", e4)], writes=[bSC[e4]])
                    act.op(lambda e: e.activation(out=SC[:, e4, 3:4], in_=SC[:, e4, 0:1], func=AF.Exp),
                           reads=[bSC[e4]], writes=[bSC[e4]])
                    act.op(lambda e: e.activation(out=T_("sgw0", e4), in_=T_("cs0", e4), func=AF.Exp, scale=-CDEC),
                           reads=[B_("cs0", e4)], writes=[B_("sgw0", e4)])
                    act.op(lambda e: e.activation(out=T_("cs0", e4), in_=T_("cs0", e4), func=AF.Exp, scale=CDEC),
                           reads=[B_("cs0", e4)], writes=[B_("cs0", e4)])
                    act.op(lambda e: e.activation(out=T_("u0", e4), in_=T_("u0", e4), func=AF.Exp, scale=-CDEC),
                           reads=[B_("u0", e4)], writes=[B_("u0", e4)])
                    act.op(lambda e: e.activation(out=T_("sgw1", e4), in_=T_("u1", e4), func=AF.Exp, scale=CDEC,
                                                  bias=SC[:, e4, 0:1]),
                           reads=[B_("u1", e4), bSC[e4]], writes=[B_("sgw1", e4)])
                    act.op(lambda e: e.activation(out=T_("cs1", e4), in_=T_("cs1", e4), func=AF.Exp, scale=CDEC,
                                                  bias=SC[:, e4, 0:1]),
                           reads=[B_("cs1", e4), bSC[e4]], writes=[B_("cs1", e4)])
                    act.op(lambda e: e.activation(out=T_("u1", e4), in_=T_("u1", e4), func=AF.Exp, scale=-CDEC,
                                                  bias=SC[:, e4, 1:2]),
                           reads=[B_("u1", e4), bSC[e4]], writes=[B_("u1", e4)])
                E1n, E2n, E3n = ("sgw0", "sgw1"), ("cs0", "u1"), ("u0", "cs1")
                for e4, eb in enumerate(ebs):
                    for d in range(2):
                        pool.op(lambda e: e.tensor_scalar(out=egt[:, d, eb * 64:(eb + 1) * 64], in0=ones[:, 0:64],
                                                          scalar1=SC[:, e4, 2 + d:3 + d],
                                                          scalar2=smask[:, 2 * i + d:2 * i + d + 1], op0=ALU.mult, op1=ALU.mult),
                                reads=[bSC[e4], b_ones, b_mask], writes=[begt])
                for e4, eb in enumerate(ebs):
                    act.op(lambda e: e.activation(out=T_("kk", e4), in_=RKV[:, 1, eb, :], func=AF.Copy, scale=pcol("kk", eb)),
                           reads=[bRKV[1][eb], b_pvec], writes=[B_("kk", e4)])
                    act.op(lambda e: e.activation(out=T_("kk2", e4), in_=T_("kk", e4), func=AF.Square),
                           reads=[B_("kk", e4)], writes=[B_("kk2", e4)])
                for e4 in range(4):
                    pe.op(lambda e: e.matmul(out=pss[:, 0, :], lhsT=BO, rhs=T_("kk2", e4), start=True, stop=True),
                          reads=[B_("kk2", e4), b_cf], writes=[bpss[0]])
                    act.op(lambda e: e.activation(out=T_("kk2", e4), in_=pss[:, 0, :], func=AF.Sqrt),
                           reads=[bpss[0]], writes=[B_("kk2", e4)])
                for e4 in range(4):
                    dve.op(lambda e: e.tensor_scalar(out=T_("kk2", e4), in0=T_("kk2", e4), scalar1=1e-12, scalar2=None,
                                                     op0=ALU.max), reads=[B_("kk2", e4)], writes=[B_("kk2", e4)])
                    dve.op(lambda e: e.reciprocal(out=T_("kk2", e4), in_=T_("kk2", e4)),
                           reads=[B_("kk2", e4)], writes=[B_("kk2", e4)])
                    dve.op(lambda e: e.tensor_tensor(out=T_("kkn", e4), in0=T_("kk", e4), in1=T_("kk2", e4), op=ALU.mult),
                           reads=[B_("kk", e4), B_("kk2", e4)], writes=[B_("kkn", e4)])
                for e4, eb in enumerate(ebs):
                    for d in range(2):
                        pool.op(lambda e: e.tensor_scalar(out=T_("tmp%d" % d, e4), in0=T_("a%d" % d, e4), scalar1=-1.0,
                                                          scalar2=pcol("ka", eb), op0=ALU.add, op1=ALU.mult),
                                reads=[B_("a%d" % d, e4), b_pvec], writes=[B_("tmp%d" % d, e4)])
                        pool.op(lambda e: e.tensor_tensor(out=T_("tmq%d" % d, e4), in0=T_("kkn", e4), in1=T_("a%d" % d, e4),
                                                          op=ALU.mult),
                                reads=[B_("kkn", e4), B_("a%d" % d, e4)], writes=[B_("tmq%d" % d, e4)])
                for e4, eb in enumerate(ebs):
                    es_ = slice(eb * 128, (eb + 1) * 128)
                    r_ap, k_ap = RKV[:, 0, eb, :], RKV[:, 1, eb, :]
                    br, bk = bRKV[0][eb], bRKV[1][eb]
                    for d in range(2):
                        E1, E2, E3 = E1n[d], E2n[d], E3n[d]
                        dve.op(lambda e: e.tensor_tensor(out=fmt[:, d, 0, es_], in0=r_ap, in1=T_(E1, e4), op=ALU.mult),
                               reads=[br, B_(E1, e4)], writes=[bFMe[eb]])
                        dve.op(lambda e: e.scalar_tensor_tensor(out=T_("tmp%d" % d, e4), in0=T_("tmp%d" % d, e4), scalar=1.0,
                                                                in1=k_ap, op0=ALU.add, op1=ALU.mult),
                               reads=[B_("tmp%d" % d, e4), bk], writes=[B_("tmp%d" % d, e4)])
                        dve.op(lambda e: e.tensor_tensor(out=fmt[:, d, 1, es_], in0=T_("tmp%d" % d, e4), in1=T_(E2, e4),
                                                         op=ALU.mult),
                               reads=[B_("tmp%d" % d, e4), B_(E2, e4)], writes=[bFMe[eb]])
                        dve.op(lambda e: e.scalar_tensor_tensor(out=fmt[:, d, 2, es_], in0=T_("kkn", e4), scalar=-1.0,
                                                                in1=T_(E3, e4), op0=ALU.mult, op1=ALU.mult),
                               reads=[B_("kkn", e4), B_(E3, e4)], writes=[bFMe[eb]])
                        dve.op(lambda e: e.tensor_tensor(out=fmt[:, d, 3, es_], in0=T_("tmq%d" % d, e4), in1=T_(E2, e4),
                                                         op=ALU.mult),
                               reads=[B_("tmq%d" % d, e4), B_(E2, e4)], writes=[bFMe[eb]])
                for e4, eb in enumerate(ebs):
                    dve.op(lambda e: e.scalar_tensor_tensor(out=T_("rk", e4), in0=RKV[:, 0, eb, :], scalar=pcol("rk", eb),
                                                            in1=RKV[:, 1, eb, :], op0=ALU.mult, op1=ALU.mult),
                           reads=[bRKV[0][eb], bRKV[1][eb], b_pvec], writes=[B_("rk", e4)])
                    pe.op(lambda e: e.matmul(out=pss[:, 1, :], lhsT=BO, rhs=T_("rk", e4), start=True, stop=True),
                          reads=[B_("rk", e4), b_cf], writes=[bpss[1]])
                    dve.op(lambda e: e.tensor_tensor(out=bont[:, eb, :], in0=pss[:, 1, :], in1=RKV[:, 2, eb, :], op=ALU.mult),
                           reads=[bpss[1], bRKV[2][eb]], writes=[bbont])
                for e4, eb in enumerate(ebs):
                    es_ = slice(eb * 128, (eb + 1) * 128)
                    vb, bvb = VBF[e4 % 2], bVBF[e4 % 2]
                    act.op(lambda e: e.activation(out=vb[:], in_=RKV[:, 2, eb, :], func=AF.Copy),
                           reads=[bRKV[2][eb]], writes=[bvb])
                    srcs = [fmt[:, 0, 1, es_], fmt[:, 0, 3, es_], fmt[:, 0, 2, es_],
                            fmt[:, 1, 1, es_], fmt[:, 1, 3, es_], fmt[:, 1, 2, es_], vb[:]]
                    for qi, s_ap in enumerate(srcs):
                        pe.op(lambda e: e.transpose(out=ptq[:, qi, :], in_=s_ap, identity=ident),
                              reads=[bFMe[eb], bvb, b_cbf], writes=[bptq], acc=True)
                    act.op(lambda e: e.activation(out=tmt[:, :, es_], in_=ptq[:, 0:7, :], func=AF.Copy),
                           reads=[bptq], writes=[btmt])
            pool.dma(fm_s[i, :, :, :, :], fmt[:], reads=bFMe)
            pool.dma(tm_s[i, :, :, :], tmt[:], reads=[btmt])
            pool.dma(eg_s[i, :, :, :], egt[:], reads=[begt])
            pool.dma(bon_s[i, :, :], bont[:].rearrange("p a b -> p (a b)"), reads=[bbont])
            if stop == "A3":
                fw.barrier()
                return nc
        fw.barrier()
    if stop == "A":
        return nc

    with ExitStack() as ph:
        I4, SU4, U4, SL4, L4 = [cbf[:, m, :].rearrange("p (a b) -> p a b", b=128) for m in range(5)]
        banks = [psum(ph, "bkB%d" % i, [128, 512], F32) for i in range(8)]
        bbank = pbufs(8)
        bank_ctr = [0]

        def next_bank():
            k = bank_ctr[0] % 8
            bank_ctr[0] += 1
            return banks[k], bbank[k]

        FMd = [sb(ph, "fmB%d" % i, [128, 4, 1024], BF16) for i in range(2)]
        bFMd = bufs(2)
        TMd = [sb(ph, "tmB%d" % i, [128, 3, 1024], BF16) for i in range(2)]
        bTMd = bufs(2)
        TMv = [sb(ph, "tvB%d" % i, [128, 1024], BF16) for i in range(2)]
        bTMv = bufs(2)
        EGd = [sb(ph, "egB%d" % i, [128, 512], F32) for i in range(2)]
        bEGd = bufs(2)

        def mat16(name):
            return sb(ph, name, [128, 16, 128], BF16), bufs(4)
        Pm, bP = mat16("PmB")
        PTm, bPT = mat16("PTmB")
        P2m, bP2 = mat16("P2mB")
        PT2m, bPT2 = mat16("PT2mB")
        Xm, bXm = mat16("XmB")
        XTm, bXTm = mat16("XTmB")
        Aak, bAak = mat16("AakB")
        Arb, bArb = mat16("ArbB")
        Ark, bArk = mat16("ArkB")
        M1 = sb(ph, "M1B", [128, 16, 64], BF16)
        bM1 = bufs(2)
        AtT = sb(ph, "AtTB", [128, 8, 128], BF16)
        bAtT = bufs(2)
        Usb = sb(ph, "UsbB", [128, 16, 64], BF16)
        bUsb = bufs(2)
        h32 = sb(ph, "h32B", [128, 512], F32)
        hbf = sb(ph, "hbfB", [128, 8, 64], BF16)
        bh = Buf()
        OT = [sb(ph, "OTB%d" % i, [128, 8, 128], F32) for i in range(2)]
        bOT = bufs(2)

        def load(step, i, d):
            p = step % 2
            sp.dma(FMd[p][:], fm_s[i, :, d, :, :], writes=[bFMd[p]])
            sp.dma(TMd[p][:], tm_s[i, :, 3 * d:3 * d + 3, :], writes=[bTMd[p]])
            sp.dma(TMv[p][:], tm_s[i, :, 6, :], writes=[bTMv[p]])
            sp.dma(EGd[p][:], eg_s[i, :, d, :], writes=[bEGd[p]])

        step = 0
        order = [(i, 0) for i in range(NT)] + [(i, 1) for i in range(NT - 1, -1, -1)]
        load(0, order[0][0], order[0][1])
        for idx, (i, d) in enumerate(order):
            p = step % 2
            if idx + 1 < len(order):
                load(step + 1, order[idx + 1][0], order[idx + 1][1])
            if idx == 0 or idx == NT:
                dve.op(lambda e: e.memset(h32[:], 0.0), writes=[bh])
                dve.op(lambda e: e.memset(hbf[:], 0.0), writes=[bh])
            fm, bfm, tmd, btmd, tv, btv, eg, beg = FMd[p], bFMd[p], TMd[p], bTMd[p], TMv[p], bTMv[p], EGd[p], bEGd[p]
            m_su, m_u, m_sl = (SU4, U4, SL4) if d == 0 else (SL4, L4, SU4)

            def fmh(q, h):
                eb, j = h // 2, h % 2
                return fm[j * 64:(j + 1) * 64, q, eb * 128:(eb + 1) * 128]

            def tmh(q, h):
                return tmd[:, q, h * 64:(h + 1) * 64]

            def vh(h):
                return tv[:, h * 64:(h + 1) * 64]

            def prod(lq, rq, mask, dst, bdst):
                for hb8 in range(2):
                    bkj = [next_bank(), next_bank()]
                    for e4 in range(4):
                        for j in range(2):
                            h = hb8 * 8 + e4 * 2 + j
                            bk, bbk = bkj[j]
                            pe.op(lambda e: e.matmul(out=bk[:, e4 * 128:(e4 + 1) * 128], lhsT=fmh(lq, h), rhs=fmh(rq, h),
                                                     start=True, stop=True), reads=[bfm], writes=[bbk], acc=True)
                    for j in range(2):
                        bk, bbk = bkj[j]
                        dve.op(lambda e: e.tensor_tensor(out=dst[:, hb8 * 8 + j:hb8 * 8 + 8:2, :],
                                                         in0=bk[:].rearrange("p (a b) -> p a b", b=128), in1=mask, op=ALU.mult),
                               reads=[bbk, b_cbf], writes=[bdst[hb8 * 2], bdst[hb8 * 2 + 1]])
            prod(3, 2, m_su, Pm, bP)
            prod(2, 3, m_sl, PTm, bPT)
            prod(1, 2, m_su, Aak, bAak)
            prod(3, 0, m_u, Arb, bArb)
            prod(1, 0, m_u, Ark, bArk)
            for hbk in range(4):
                hs = slice(hbk * 4, (hbk + 1) * 4)
                dve.op(lambda e: e.tensor_tensor(out=Xm[:, hs, :], in0=Pm[:, hs, :], in1=I4, op=ALU.add),
                       reads=[bP[hbk], b_cbf], writes=[bXm[hbk]])
                dve.op(lambda e: e.tensor_tensor(out=XTm[:, hs, :], in0=PTm[:, hs, :], in1=I4, op=ALU.add),
                       reads=[bPT[hbk], b_cbf], writes=[bXTm[hbk]])
            cur = (Pm, bP, PTm, bPT)
            nxt = (P2m, bP2, PT2m, bPT2)
            for lev in range(6):
                Pc, bPc, PTc, bPTc = cur
                Pn, bPn, PTn, bPTn = nxt
                last = (lev == 5)

                def mm_batch(lhs, blhs, rhs, brhs, evac):
                    for hbk in range(4):
                        bk, bbk = next_bank()
                        for hh in range(4):
                            h = hbk * 4 + hh
                            pe.op(lambda e: e.matmul(out=bk[:, hh * 128:(hh + 1) * 128], lhsT=lhs[:, h, :], rhs=rhs[:, h, :],
                                                     start=True, stop=True),
                                  reads=[blhs[hbk], brhs[hbk]], writes=[bbk], acc=True)
                        evac(hbk, bk, bbk)

                def ev_copy(dst, bdst):
                    def f(hbk, bk, bbk):
                        act.op(lambda e: e.activation(out=dst[:, hbk * 4:(hbk + 1) * 4, :],
                                                      in_=bk[:].rearrange("p (a b) -> p a b", b=128), func=AF.Copy),
                               reads=[bbk], writes=[bdst[hbk]])
                    return f

                def ev_add(dst, bdst):
                    def f(hbk, bk, bbk):
                        hs = slice(hbk * 4, (hbk + 1) * 4)
                        dve.op(lambda e: e.tensor_tensor(out=dst[:, hs, :], in0=bk[:].rearrange("p (a b) -> p a b", b=128),
                                                         in1=dst[:, hs, :], op=ALU.add),
                               reads=[bbk, bdst[hbk]], writes=[bdst[hbk]])
                    return f
                mm_batch(PTc, bPTc, Pc, bPc, ev_copy(Pn, bPn))
                if not last:
                    mm_batch(Pc, bPc, PTc, bPTc, ev_copy(PTn, bPTn))
                mm_batch(XTm, bXTm, Pn, bPn, ev_add(Xm, bXm))
                if not last:
                    mm_batch(Pn, bPn, XTm, bXTm, ev_add(XTm, bXTm))
                cur, nxt = nxt, cur
            for g8 in range(2):
                bk, bbk = next_bank()
                for hh in range(8):
                    h = g8 * 8 + hh
                    pe.op(lambda e: e.matmul(out=bk[:, hh * 64:(hh + 1) * 64], lhsT=Aak[:, h, :], rhs=vh(h),
                                             start=True, stop=True), reads=[bAak[h // 4], btv], writes=[bbk], acc=True)
                act.op(lambda e: e.activation(out=M1[:, g8 * 8:(g8 + 1) * 8, :],
                                              in_=bk[:].rearrange("p (a b) -> p a b", b=64), func=AF.Copy),
                       reads=[bbk], writes=[bM1[g8]])
            for g4 in range(2):
                bk, bbk = next_bank()
                for e4 in range(4):
                    eb = g4 * 4 + e4
                    for j in range(2):
                        h = eb * 2 + j
                        pe.op(lambda e: e.matmul(out=bk[j * 64:(j + 1) * 64, e4 * 128:(e4 + 1) * 128], lhsT=tmh(2, h),
                                                 rhs=Xm[:, h, :], start=True, stop=True),
                              reads=[btmd, bXm[h // 4]], writes=[bbk], acc=True)
                act.op(lambda e: e.activation(out=AtT[:, g4 * 4:(g4 + 1) * 4, :],
                                              in_=bk[:].rearrange("p (a b) -> p a b", b=128), func=AF.Copy),
                       reads=[bbk], writes=[bAtT[g4]])
            for g8 in range(2):
                bk, bbk = next_bank()
                for hh in range(8):
                    h = g8 * 8 + hh
                    eb, j = h // 2, h % 2
                    js = slice(j * 64, (j + 1) * 64)
                    pe.op(lambda e: e.matmul(out=bk[:, hh * 64:(hh + 1) * 64], lhsT=AtT[js, eb, :], rhs=hbf[js, eb, :],
                                             start=True, stop=False), reads=[bAtT[eb // 4], bh], writes=[bbk], acc=True)
                    pe.op(lambda e: e.matmul(out=bk[:, hh * 64:(hh + 1) * 64], lhsT=Xm[:, h, :], rhs=M1[:, h, :],
                                             start=False, stop=True), reads=[bXm[h // 4], bM1[g8]], writes=[bbk], acc=True)
                dve.op(lambda e: e.tensor_copy(out=Usb[:, g8 * 8:(g8 + 1) * 8, :],
                                               in_=bk[:].rearrange("p (a b) -> p a b", b=64)),
                       reads=[bbk], writes=[bUsb[g8]])
            ot, bot = OT[p], bOT[p]
            for g4 in range(2):
                bk, bbk = next_bank()
                for e4 in range(4):
                    eb = g4 * 4 + e4
                    for j in range(2):
                        h = eb * 2 + j
                        js = slice(j * 64, (j + 1) * 64)
                        o_ap = bk[js, e4 * 128:(e4 + 1) * 128]
                        pe.op(lambda e: e.matmul(out=o_ap, lhsT=hbf[js, eb, :], rhs=fmh(0, h), start=True, stop=False),
                              reads=[bh, bfm], writes=[bbk], acc=True)
                        pe.op(lambda e: e.matmul(out=o_ap, lhsT=Usb[:, h, :], rhs=Arb[:, h, :], start=False, stop=False),
                              reads=[bUsb[h // 8], bArb[h // 4]], writes=[bbk], acc=True)
                        pe.op(lambda e: e.matmul(out=o_ap, lhsT=vh(h), rhs=Ark[:, h, :], start=False, stop=True),
                              reads=[btv, bArk[h // 4]], writes=[bbk], acc=True)
                act.op(lambda e: e.activation(out=ot[:, g4 * 4:(g4 + 1) * 4, :],
                                              in_=bk[:].rearrange("p (a b) -> p a b", b=128), func=AF.Copy),
                       reads=[bbk], writes=[bot])
            pool.dma(yT_s[d, i, :, :], ot[:].rearrange("p a b -> p (a b)"), reads=[bot])
            bk, bbk = next_bank()
            for eb in range(8):
                for j in range(2):
                    h = eb * 2 + j
                    js = slice(j * 64, (j + 1) * 64)
                    o_ap = bk[js, eb * 64:(eb + 1) * 64]
                    pe.op(lambda e: e.matmul(out=o_ap, lhsT=tmh(1, h), rhs=Usb[:, h, :], start=True, stop=False),
                          reads=[btmd, bUsb[h // 8]], writes=[bbk], acc=True)
                    pe.op(lambda e: e.matmul(out=o_ap, lhsT=tmh(0, h), rhs=vh(h), start=False, stop=True),
                          reads=[btmd, btv], writes=[bbk], acc=True)
            dve.op(lambda e: e.tensor_tensor(out=h32[:], in0=bk[:], in1=h32[:], op=ALU.add), reads=[bbk, bh], writes=[bh])
            dve.op(lambda e: e.tensor_tensor(out=h32[:], in0=h32[:], in1=eg[:], op=ALU.mult), reads=[bh, beg], writes=[bh])
            act.op(lambda e: e.activation(out=hbf[:].rearrange("p a b -> p (a b)"), in_=h32[:], func=AF.Copy),
                   reads=[bh], writes=[bh])
            step += 1
        fw.barrier()
    if stop == "B":
        return nc

    with ExitStack() as ph:
        stage = sb(ph, "stageC", [128, 1536], F32)
        b_stage = Buf()
        Wout, b_Wout = load_weight_bf16(ph, "Wout", w_rwout, 1024, None, stage, b_stage)
        YF = [sb(ph, "yfC%d" % i, [128, 8, 128], F32) for i in range(2)]
        YB = [sb(ph, "ybC%d" % i, [128, 8, 128], F32) for i in range(2)]
        BN = [sb(ph, "bnC%d" % i, [128, 8, 128], F32) for i in range(2)]
        SG = [sb(ph, "sgC%d" % i, [128, 8, 128], F32) for i in range(2)]
        XC = [sb(ph, "xC%d" % i, [128, 1024], F32) for i in range(2)]
        bIN = bufs(2)
        yg = sb(ph, "ygC", [128, 8, 128], BF16)
        byg = Buf()
        pst = psum(ph, "pstC", [128, 512], F32)[:, 0:256].rearrange("p (a b) -> p a b", b=128)
        bpst = pbufs(1) * 2
        pout = [psum(ph, "poutC%d" % i, [128, 512], F32) for i in range(2)]
        bpout = pbufs(2)
        ptr = psum(ph, "ptrC", [128, 8, 128], BF16)
        bptr = Buf(True)
        x1 = [sb(ph, "x1C%d" % i, [128, 1024], F32) for i in range(2)]
        bx1 = bufs(2)
        junk = sb(ph, "junkC", [128, 1024], F32)
        bjunk = Buf()
        ss = sb(ph, "ssC", [128, 4], F32)
        bss = Buf()
        hb = sb(ph, "hbC", [128, 1024], BF16)
        bhb = Buf()
        h1T = [sb(ph, "h1TC%d" % i, [128, 8, 128], BF16) for i in range(2)]
        bh1T = bufs(2)
        WKc = []
        for s in range(2):
            dct = {nm: sb(ph, "wc%d_%s" % (s, nm), [128, 128], F32) for nm in ["y", "yc", "sq", "sd", "yn"]}
            dct["b"] = Buf()
            WKc.append(dct)

        def loadC(i):
            p = i % 2
            sp.dma(YF[p][:].rearrange("p a b -> p (a b)"), yT_s[0, i, :, :], writes=[bIN[p]])
            sp.dma(YB[p][:].rearrange("p a b -> p (a b)"), yT_s[1, i, :, :], writes=[bIN[p]])
            sp.dma(BN[p][:].rearrange("p a b -> p (a b)"), bon_s[i, :, :], writes=[bIN[p]])
            sp.dma(SG[p][:].rearrange("p a b -> p (a b)"), sg_s[i, :, :], writes=[bIN[p]])
            sp.dma(XC[p][:], x_t[i, :, :], writes=[bIN[p]])
        loadC(0)
        for i in range(NT):
            p = i % 2
            if i + 1 < NT:
                loadC(i + 1)
            bin_ = bIN[p]
            for eb in range(8):
                W = WKc[eb % 2]
                bW = W["b"]
                dve.op(lambda e: e.tensor_tensor(out=W["y"][:], in0=YF[p][:, eb, :], in1=YB[p][:, eb, :], op=ALU.add),
                       reads=[bin_], writes=[bW])
                pe.op(lambda e: e.matmul(out=pst[:, 0, :], lhsT=BOm, rhs=W["y"][:], start=True, stop=True),
                      reads=[bW, b_cf], writes=[bpst[0]])
                dve.op(lambda e: e.tensor_tensor(out=W["yc"][:], in0=W["y"][:], in1=pst[:, 0, :], op=ALU.subtract),
                       reads=[bW, bpst[0]], writes=[bW])
                dve.op(lambda e: e.tensor_tensor(out=W["sq"][:], in0=W["yc"][:], in1=W["yc"][:], op=ALU.mult),
                       reads=[bW], writes=[bW])
                pe.op(lambda e: e.matmul(out=pst[:, 1, :], lhsT=BOm, rhs=W["sq"][:], start=True, stop=True),
                      reads=[bW, b_cf], writes=[bpst[1]])
                act.op(lambda e: e.activation(out=W["sd"][:], in_=pst[:, 1, :], func=AF.Sqrt, bias=cst[:, 1:2]),
                       reads=[bpst[1], b_cst], writes=[bW])
                dve.op(lambda e: e.reciprocal(out=W["sd"][:], in_=W["sd"][:]), reads=[bW], writes=[bW])
                dve.op(lambda e: e.tensor_tensor(out=W["yn"][:], in0=W["yc"][:], in1=W["sd"][:], op=ALU.mult),
                       reads=[bW], writes=[bW])
                act.op(lambda e: e.activation(out=W["yn"][:], in_=W["yn"][:], func=AF.Identity,
                                              scale=pcol("lnxg", eb), bias=pcol("lnxb", eb)),
                       reads=[bW, b_pvec], writes=[bW])
                dve.op(lambda e: e.tensor_tensor(out=W["yn"][:], in0=W["yn"][:], in1=BN[p][:, eb, :], op=ALU.add),
                       reads=[bW, bin_], writes=[bW])
                dve.op(lambda e: e.tensor_tensor(out=yg[:, eb, :], in0=W["yn"][:], in1=SG[p][:, eb, :], op=ALU.mult),
                       reads=[bW, bin_], writes=[byg])
            for half in range(2):
                for eb in range(8):
                    pe.op(lambda e: e.matmul(out=pout[half][:], lhsT=yg[:, eb, :], rhs=Wout[:, eb, half * 512:(half + 1) * 512],
                                             start=(eb == 0), stop=(eb == 7)),
                          reads=[byg, b_Wout], writes=[bpout[half]], acc=True)
                dve.op(lambda e: e.tensor_tensor(out=x1[p][:, half * 512:(half + 1) * 512], in0=pout[half][:],
                                                 in1=XC[p][:, half * 512:(half + 1) * 512], op=ALU.add),
                       reads=[bpout[half], bin_], writes=[bx1[p]])
            pool.dma(x1_s[i, :, :], x1[p][:], reads=[bx1[p]])
            rmsnorm_rstd(x1[p], bx1[p], 128, junk, bjunk, ss, bss)
            act.op(lambda e: e.activation(out=hb[:], in_=x1[p][:], func=AF.Copy, scale=ss[:, 2:3]),
                   reads=[bx1[p], bss], writes=[bhb])
            for kc in range(8):
                pe.op(lambda e: e.transpose(out=ptr[:, kc, :], in_=hb[:, kc * 128:(kc + 1) * 128], identity=ident),
                      reads=[bhb, b_cbf], writes=[bptr], acc=True)
            dve.op(lambda e: e.tensor_copy(out=h1T[p][:], in_=ptr[:]), reads=[bptr], writes=[bh1T[p]])
            pool.dma(h1T_s[i, :, :, :], h1T[p][:], reads=[bh1T[p]])
        fw.barrier()
    if stop == "C":
        return nc

    with ExitStack() as ph:
        stage = sb(ph, "stageD", [128, 1536], F32)
        b_stage = Buf()
        Wc, b_Wc = load_weight_bf16(ph, "Wc", w_cvin, 3072, "g1", stage, b_stage)
        H1 = [sb(ph, "H1D%d" % i, [128, 8, 512], BF16) for i in range(2)]
        bH1 = bufs(2)
        pb = [psum(ph, "pbD%d" % i, [128, 512], F32) for i in range(6)]
        bpb = pbufs(6)
        sgl = [sb(ph, "sglD%d" % i, [128, 512], F32) for i in range(2)]
        bsgl = bufs(2)
        zt = [sb(ph, "ztD%d" % i, [128, 512], F32) for i in range(2)]
        bzt = bufs(2)
        sg1 = [sb(ph, "sg1D%d" % i, [128, 512], F32) for i in range(2)]
        bsg1 = bufs(2)
        zero = sb(ph, "zeroD", [128, 16], F32)
        bzero = Buf()
        dve.op(lambda e: e.memset(zero[:], 0.0), writes=[bzero])
        for cbk in range(8):
            pool.dma(z_s[cbk, :, 0:15], zero[:, 0:15], reads=[bzero])
            pool.dma(z_s[cbk, :, T + 15:T + 30], zero[:, 0:15], reads=[bzero])

        def loadD(g):
            p = g % 2
            for tt in range(4):
                sp.dma(H1[p][:, :, tt * 128:(tt + 1) * 128], h1T_s[4 * g + tt, :, :, :], writes=[bH1[p]])
        loadD(0)
        cnt = 0
        for g in range(NG):
            p = g % 2
            if g + 1 < NG:
                loadD(g + 1)
            for cbk in range(8):
                q = cnt % 2
                cnt += 1
                pbs = [pb[q * 3 + k] for k in range(3)]
                bpbs = [bpb[q * 3 + k] for k in range(3)]
                for k in range(3):
                    c0 = k * 1024 + cbk * 128
                    for kc in range(8):
                        pe.op(lambda e: e.matmul(out=pbs[k][:], lhsT=Wc[:, kc, c0:c0 + 128], rhs=H1[p][:, kc, :],
                                                 start=(kc == 0), stop=(kc == 7)),
                              reads=[b_Wc, bH1[p]], writes=[bpbs[k]], acc=True)
                act.op(lambda e: e.activation(out=sgl[q][:], in_=pbs[1][:], func=AF.Sigmoid, bias=pcol("bin", 8 + cbk)),
                       reads=[bpbs[1], b_pvec], writes=[bsgl[q]])
                dve.op(lambda e: e.scalar_tensor_tensor(out=zt[q][:], in0=pbs[0][:], scalar=pcol("bin", cbk), in1=sgl[q][:],
                                                        op0=ALU.add, op1=ALU.mult),
                       reads=[bpbs[0], bsgl[q], b_pvec], writes=[bzt[q]])
                pool.dma(z_s[cbk, :, 15 + g * 512:15 + (g + 1) * 512], zt[q][:], reads=[bzt[q]])
                act.op(lambda e: e.activation(out=sg1[q][:], in_=pbs[2][:], func=AF.Silu, bias=pcol("bin", 16 + cbk)),
                       reads=[bpbs[2], b_pvec], writes=[bsg1[q]])
                pool.dma(sg1_s[g, cbk, :, :], sg1[q][:], reads=[bsg1[q]])
        fw.barrier()
    if stop == "D1":
        return nc

    with ExitStack() as ph:
        stage = sb(ph, "stageE", [128, 1536], F32)
        b_stage = Buf()
        Wo, b_Wo = load_weight_bf16(ph, "Wo", w_cvout, 1024, None, stage, b_stage)
        bcs = sb(ph, "bcsE", [128, 2, 1024], F32)
        b_bc = Buf()
        sp.dma(bcs[:], bc_d[:, :, :], writes=[b_bc])
        zw = [sb(ph, "zwE%d" % i, [128, 542], F32) for i in range(3)]
        bzw = bufs(3)
        zc = sb(ph, "zcE", [128, 8, 512], F32)
        bzc = bufs(8)
        sq = [sb(ph, "sqE%d" % i, [128, 512], F32) for i in range(2)]
        bsq = bufs(2)
        pmean = psum(ph, "pmeanE", [128, 512], F32)
        bpmean = Buf(True)
        pvar = psum(ph, "pvarE", [128, 512], F32)
        bpvar = Buf(True)
        rs = sb(ph, "rsE", [128, 512], F32)
        brs = Buf()
        sg1 = [sb(ph, "sg1E%d" % i, [128, 512], F32) for i in range(2)]
        bsg1 = bufs(2)
        s1 = [sb(ph, "s1E%d" % i, [128, 512], F32) for i in range(2)]
        bs1 = bufs(2)
        ZF = sb(ph, "ZFE", [128, 8, 512], BF16)
        bZF = Buf()
        pout = [psum(ph, "poutE%d" % i, [128, 512], F32) for i in range(4)]
        bpout = pbufs(4)
        x1 = [sb(ph, "x1E%d" % i, [128, 1024], F32) for i in range(2)]
        bx1 = bufs(2)
        x2 = [sb(ph, "x2E%d" % i, [128, 1024], F32) for i in range(2)]
        bx2 = bufs(2)
        yo = [sb(ph, "yoE%d" % i, [128, 1024], F32) for i in range(2)]
        byo = bufs(2)
        junk = sb(ph, "junkE", [128, 1024], F32)
        bjunk = Buf()
        ss = sb(ph, "ssE", [128, 4], F32)
        bss = Buf()
        lc = 0
        oc = 0
        for g in range(NG):
            for cbk in range(8):
                q = lc % 3
                lc += 1
                sp.dma(zw[q][:], z_s[cbk, :, g * 512:g * 512 + 542], writes=[bzw[q]])
                dve.op(lambda e: e.tensor_scalar(out=zw[q][:, 0:15], in0=zw[q][:, 0:15], scalar1=cmask[:, 2 * g:2 * g + 1],
                                                 scalar2=None, op0=ALU.mult), reads=[bzw[q], b_mask], writes=[bzw[q]])
                dve.op(lambda e: e.tensor_scalar(out=zw[q][:, 527:542], in0=zw[q][:, 527:542],
                                                 scalar1=cmask[:, 2 * g + 1:2 * g + 2], scalar2=None, op0=ALU.mult),
                       reads=[bzw[q], b_mask], writes=[bzw[q]])
                dwc = PV["dw"] + cbk * 31
                dve.op(lambda e: e.tensor_scalar(out=zc[:, cbk, :], in0=zw[q][:, 0:512], scalar1=pvec[:, dwc:dwc + 1],
                                                 scalar2=pcol("bdw", cbk), op0=ALU.mult, op1=ALU.add),
                       reads=[bzw[q], b_pvec], writes=[bzc[cbk]])
                for j in range(1, 31):
                    dve.op(lambda e: e.scalar_tensor_tensor(out=zc[:, cbk, :], in0=zw[q][:, j:j + 512],
                                                            scalar=pvec[:, dwc + j:dwc + j + 1], in1=zc[:, cbk, :],
                                                            op0=ALU.mult, op1=ALU.add),
                           reads=[bzw[q], b_pvec, bzc[cbk]], writes=[bzc[cbk]])
                pe.op(lambda e: e.matmul(out=pmean[:], lhsT=O1k, rhs=zc[:, cbk, :], start=(cbk == 0), stop=(cbk == 7)),
                      reads=[bzc[cbk], b_cf], writes=[bpmean], acc=True)
            for cbk in range(8):
                q = cbk % 2
                dve.op(lambda e: e.tensor_tensor(out=zc[:, cbk, :], in0=zc[:, cbk, :], in1=pmean[:], op=ALU.subtract),
                       reads=[bzc[cbk], bpmean], writes=[bzc[cbk]])
                act.op(lambda e: e.activation(out=sq[q][:], in_=zc[:, cbk, :], func=AF.Square),
                       reads=[bzc[cbk]], writes=[bsq[q]])
                pe.op(lambda e: e.matmul(out=pvar[:], lhsT=O1k, rhs=sq[q][:], start=(cbk == 0), stop=(cbk == 7)),
                      reads=[bsq[q], b_cf], writes=[bpvar], acc=True)
            act.op(lambda e: e.activation(out=rs[:], in_=pvar[:], func=AF.Sqrt, bias=cst[:, 0:1]),
                   reads=[bpvar, b_cst], writes=[brs])
            dve.op(lambda e: e.reciprocal(out=rs[:], in_=rs[:]), reads=[brs], writes=[brs])
            for cbk in range(8):
                q = cbk % 2
                sp.dma(sg1[q][:], sg1_s[g, cbk, :, :], writes=[bsg1[q]])
                dve.op(lambda e: e.tensor_tensor(out=s1[q][:], in0=zc[:, cbk, :], in1=rs[:], op=ALU.mult),
                       reads=[bzc[cbk], brs], writes=[bs1[q]])
                act.op(lambda e: e.activation(out=s1[q][:], in_=s1[q][:], func=AF.Silu, scale=pcol("lng", cbk),
                                              bias=pcol("lnb", cbk)), reads=[bs1[q], b_pvec], writes=[bs1[q]])
                dve.op(lambda e: e.tensor_tensor(out=ZF[:, cbk, :], in0=s1[q][:], in1=sg1[q][:], op=ALU.mult),
                       reads=[bs1[q], bsg1[q]], writes=[bZF])
            for tt in range(4):
                i = 4 * g + tt
                p = oc % 2
                oc += 1
                sp.dma(x1[p][:], x1_s[i, :, :], writes=[bx1[p]])
                for half in range(2):
                    pk = p * 2 + half
                    hs = slice(half * 512, (half + 1) * 512)
                    for cbk in range(8):
                        pe.op(lambda e: e.matmul(out=pout[pk][:], lhsT=ZF[:, cbk, tt * 128:(tt + 1) * 128],
                                                 rhs=Wo[:, cbk, hs], start=(cbk == 0), stop=(cbk == 7)),
                              reads=[bZF, b_Wo], writes=[bpout[pk]], acc=True)
                    dve.op(lambda e: e.tensor_tensor(out=x2[p][:, hs], in0=pout[pk][:], in1=bcs[:, 0, hs], op=ALU.add),
                           reads=[bpout[pk], b_bc], writes=[bx2[p]])
                dve.op(lambda e: e.tensor_tensor(out=x2[p][:], in0=x2[p][:], in1=x1[p][:], op=ALU.add),
                       reads=[bx2[p], bx1[p]], writes=[bx2[p]])
                rmsnorm_rstd(x2[p], bx2[p], 128, junk, bjunk, ss, bss)
                dve.op(lambda e: e.scalar_tensor_tensor(out=yo[p][:], in0=x2[p][:], scalar=ss[:, 2:3], in1=bcs[:, 1, :],
                                                        op0=ALU.mult, op1=ALU.mult),
                       reads=[bx2[p], bss, b_bc], writes=[byo[p]])
                pool.dma(y_out[i, :, :], yo[p][:], reads=[byo[p]])
        fw.barrier()
    return nc


def build(T, stop=None):
    dry = _build(T, stop, None)
    plan = dry._fw.get_plan()
    return _build(T, stop, plan)


def _fm(v):
    v = np.asarray(v, np.float32).reshape(-1)
    return np.ascontiguousarray(v.reshape(-1, 128).T)


def _wrows(w):
    w = np.asarray(w, np.float32)
    return np.ascontiguousarray(w.reshape(8, 128, w.shape[1]).transpose(1, 0, 2))


def shared_inputs(norm_g, final_g, rw_in, rw_mu, rw_w0, rw_w2, rw_a0, rw_a2, rw_kk, rw_ka, rw_rk,
                  rw_lnx_g, rw_lnx_b, rw_out, cv_in, cv_b_in, cv_dw, cv_b_dw, cv_ln_g, cv_ln_b, cv_out, cv_b_out):
    pv = np.zeros((128, NPV), np.float32)

    def put(name, arr):
        pv[:, PV[name]:PV[name] + arr.shape[1]] = arr
    put("g0", _fm(norm_g[0]))
    put("g1", _fm(norm_g[1]))
    put("w0f", _fm(rw_w0[0, 0]))
    put("w0b", _fm(rw_w0[0, 1]))
    put("a0f", _fm(rw_a0[0, 0]))
    put("a0b", _fm(rw_a0[0, 1]))
    put("kk", _fm(rw_kk[0]))
    put("ka", _fm(rw_ka[0]))
    put("rk", _fm(rw_rk[0]))
    put("lnxg", _fm(rw_lnx_g[0]))
    put("lnxb", _fm(rw_lnx_b[0]))
    put("bdw", _fm(cv_b_dw[0]))
    put("lng", _fm(cv_ln_g[0]))
    put("lnb", _fm(cv_ln_b[0]))
    put("mu", _fm(rw_mu[0]))
    put("bin", _fm(cv_b_in[0]))
    dw = np.asarray(cv_dw[0], np.float32)
    dwf = dw.T.reshape(8, 128, 31).transpose(1, 0, 2).reshape(128, 248)
    put("dw", dwf)
    w2a2 = np.stack([np.asarray(rw_w2[0], np.float32).reshape(128, 1024),
                     np.asarray(rw_a2[0], np.float32).reshape(128, 1024)], axis=1)
    bc = np.stack([np.broadcast_to(np.asarray(cv_b_out[0], np.float32), (128, 1024)),
                   np.broadcast_to(np.asarray(final_g, np.float32), (128, 1024))], axis=1)
    r = np.arange(128)
    eye = (r[:, None] == r[None, :])
    su = (r[:, None] < r[None, :])
    u = (r[:, None] <= r[None, :])
    sl = (r[:, None] > r[None, :])
    l = (r[:, None] >= r[None, :])
    cbf = np.stack([np.tile(m.astype(np.float32), (1, 4)) for m in (eye, su, u, sl, l)], axis=1).astype(ml_dtypes.bfloat16)
    blk = ((r[:, None] // 64) == (r[None, :] // 64)).astype(np.float32)
    cf32 = np.stack([blk / 64.0, blk, np.full((128, 128), 1.0 / 1024.0, np.float32)], axis=1).astype(np.float32)
    return {
        "w_rwin": _wrows(rw_in[0]), "w_rwout": _wrows(rw_out[0]), "w_cvin": _wrows(cv_in[0]),
        "w_cvout": _wrows(cv_out[0]), "w2a2": np.ascontiguousarray(w2a2), "pvec": pv,
        "bc": np.ascontiguousarray(bc), "cbf": np.ascontiguousarray(cbf), "cf32": np.ascontiguousarray(cf32),
    }


def core_inputs(seqs):
    xs = np.concatenate([np.asarray(s, np.float32) for s in seqs], axis=0)
    T = xs.shape[0]
    NT, NG = T // 128, T // 512
    starts = set()
    o = 0
    for s in seqs:
        starts.add(o)
        o += s.shape[0]
    starts.add(T)
    x_t = xs.reshape(NT, 128, 1024)
    x_h = np.zeros((NT, 2, 1024), np.float32)
    sm = np.ones((NT, 2), np.float32)
    for i in range(NT):
        t0, t1 = i * 128, (i + 1) * 128
        if t0 not in starts:
            x_h[i, 0] = xs[t0 - 1]
        if t1 not in starts:
            x_h[i, 1] = xs[t1]
        if t1 in starts:
            sm[i, 0] = 0.0
        if t0 in starts:
            sm[i, 1] = 0.0
    cm = np.ones((NG, 2), np.float32)
    for g in range(NG):
        if g * 512 in starts:
            cm[g, 0] = 0.0
        if (g + 1) * 512 in starts:
            cm[g, 1] = 0.0
    return {
        "x_t": np.ascontiguousarray(x_t), "x_h": x_h,
        "smask": np.ascontiguousarray(np.broadcast_to(sm.reshape(1, -1), (128, NT * 2))),
        "cmask": np.ascontiguousarray(np.broadcast_to(cm.reshape(1, -1), (128, NG * 2))),
    }


def kernel(x_prompt, x_sample, norm_g, final_g, rw_in, rw_mu, rw_w0, rw_w2, rw_a0, rw_a2, rw_kk, rw_ka, rw_rk,
           rw_lnx_g, rw_lnx_b, rw_out, cv_in, cv_b_in, cv_dw, cv_b_dw, cv_ln_g, cv_ln_b, cv_out, cv_b_out):
    x_prompt = np.asarray(x_prompt, np.float32)
    x_sample = np.asarray(x_sample, np.float32)
    T = 16384
    sh = shared_inputs(norm_g, final_g, rw_in, rw_mu, rw_w0, rw_w2, rw_a0, rw_a2, rw_kk, rw_ka, rw_rk,
                       rw_lnx_g, rw_lnx_b, rw_out, cv_in, cv_b_in, cv_dw, cv_b_dw, cv_ln_g, cv_ln_b, cv_out, cv_b_out)
    cores = [core_inputs([x_prompt[0]]), core_inputs([x_prompt[1]]),
             core_inputs([x_sample[b] for b in range(8)])]
    in_maps = []
    for c in range(8):
        m = dict(sh)
        m.update(cores[min(c, 2)])
        in_maps.append(m)
    nc = build(T)
    res = run_bass_kernel_spmd(nc, in_maps, core_ids=list(range(8)))
    ys = [np.asarray(res.results[c]["y"], np.float32).reshape(T, 1024) for c in range(3)]
    y_prompt = np.stack([ys[0], ys[1]], axis=0)
    y_sample = ys[2].reshape(8, 2048, 1024)
    return (y_prompt, y_sample)
```

```python
import bisect
import numpy as np
import ml_dtypes
from contextlib import ExitStack
import concourse.bass as bass
import concourse.mybir as mybir
from concourse.bass_utils import run_bass_kernel_spmd

F32 = mybir.dt.float32
BF16 = mybir.dt.bfloat16
ALU = mybir.AluOpType
AF = mybir.ActivationFunctionType
CDEC = 0.6065306597126334


class Buf:
    __slots__ = ("w", "r", "excl")

    def __init__(self, excl=False):
        self.w = None
        self.r = {}
        self.excl = excl


def bufs(n):
    return [Buf() for _ in range(n)]


def pbufs(n):
    return [Buf(True) for _ in range(n)]


SAME_ENGINE_SYNC = ("pool", "dve", "act", "pe", "sp")


class Queue:
    def __init__(self, fw, eng, name, is_pe=False):
        self.fw = fw
        self.eng = eng
        self.name = name
        self.sem = fw.es.enter_context(fw.nc.semaphore("q_" + name))
        self.n = 0
        self.known = {}
        self.is_pe = is_pe
        self.dma_sems = []
        self.dma_cnt = []
        self.dma_k = 0
        self.same_sync = name in SAME_ENGINE_SYNC
        self.waited = set()
        self.plan = None if fw.plan is None else fw.plan[name]
        self.planset = None if self.plan is None else set(self.plan)

    def add_dma_sems(self, k):
        for i in range(k):
            self.dma_sems.append(self.fw.es.enter_context(self.fw.nc.semaphore("d_%s_%d" % (self.name, i))))
            self.dma_cnt.append(0)

    def _val(self, n):
        if self.plan is None:
            return n
        return bisect.bisect_right(self.plan, n)

    def _wait(self, tok):
        key, n = tok
        if self.known.get(key, 0) >= n:
            return
        self.known[key] = n
        if isinstance(key, Queue):
            key.waited.add(n)
            self.eng.wait_ge(key.sem, key._val(n))
        else:
            self.eng.wait_ge(key, n)

    def _deps(self, reads, writes, skip_same_waw=False):
        deps = {}

        def add(tok):
            if tok is None:
                return
            s, v = tok
            if deps.get(s, 0) < v:
                deps[s] = v
        for b in reads:
            add(b.w)
        for b in writes:
            if not (skip_same_waw and b.w is not None and b.w[0] is self):
                add(b.w)
            for s, v in b.r.items():
                add((s, v))
        if not self.same_sync:
            deps.pop(self, None)
        for s, v in deps.items():
            self._wait((s, v))

    def _record(self, tok, reads, writes):
        s, v = tok
        for b in reads:
            if b.r.get(s, 0) < v:
                b.r[s] = v
        for b in writes:
            b.w = tok
            b.r = {}

    def op(self, fn, reads=(), writes=(), acc=False):
        xreads = [b for b in reads if b.excl]
        if xreads:
            for b in xreads:
                for q, v in b.r.items():
                    if q is not self:
                        self._wait((q, v))
        self._deps(reads, writes, skip_same_waw=(acc and self.is_pe))
        ins = fn(self.eng)
        self.n += 1
        if self.planset is None or self.n in self.planset:
            ins.then_inc(self.sem, 1)
        self._record((self, self.n), reads, writes)
        return ins

    def dma(self, out, in_, reads=(), writes=()):
        self._deps(reads, writes)
        i = self.dma_k % len(self.dma_sems)
        self.dma_k += 1
        s = self.dma_sems[i]
        if self.dma_cnt[i] > 0:
            self._wait((s, 16 * self.dma_cnt[i]))
        self.eng.dma_start(out=out, in_=in_).then_inc(s, 16)
        self.dma_cnt[i] += 1
        self._record((s, 16 * self.dma_cnt[i]), reads, writes)


class FW:
    def __init__(self, nc, plan=None):
        self.nc = nc
        self.plan = plan
        self.es = ExitStack()
        self.pe = Queue(self, nc.tensor, "pe", is_pe=True)
        self.dve = Queue(self, nc.vector, "dve")
        self.act = Queue(self, nc.scalar, "act")
        self.pool = Queue(self, nc.gpsimd, "pool")
        self.sp = Queue(self, nc.sync, "sp")
        self.sp.add_dma_sems(8)
        self.pool.add_dma_sems(8)
        self.queues = [self.pe, self.dve, self.act, self.pool, self.sp]

    def all_tokens(self):
        toks = []
        for q in self.queues:
            if q.n > 0:
                toks.append((q, q.n))
            for s, c in zip(q.dma_sems, q.dma_cnt):
                if c > 0:
                    toks.append((s, 16 * c))
        return toks

    def barrier(self):
        toks = self.all_tokens()
        for q in self.queues:
            for tok in toks:
                q._wait(tok)

    def get_plan(self):
        return {q.name: sorted(q.waited) for q in self.queues}


PV = {}
_o = 0
for _n, _w in [("g0", 8), ("g1", 8), ("w0f", 8), ("w0b", 8), ("a0f", 8), ("a0b", 8), ("kk", 8), ("ka", 8),
               ("rk", 8), ("lnxg", 8), ("lnxb", 8), ("bdw", 8), ("lng", 8), ("lnb", 8), ("mu", 26),
               ("bin", 24), ("dw", 248)]:
    PV[_n] = _o
    _o += _w
NPV = _o


def _build(T, stop=None, plan=None):
    NT = T // 128
    NG = T // 512
    nc = bass.Bass("TRN2", target_bir_lowering=False)
    fw = FW(nc, plan)
    nc._fw = fw
    pe, dve, act, pool, sp = fw.pe, fw.dve, fw.act, fw.pool, fw.sp

    def dram(name, shape, dt, kind="Internal"):
        return nc.dram_tensor(name, shape, dt, kind=kind).ap()

    x_t = dram("x_t", [NT, 128, 1024], F32, "ExternalInput")
    x_h = dram("x_h", [NT, 2, 1024], F32, "ExternalInput")
    smask_d = dram("smask", [128, NT * 2], F32, "ExternalInput")
    cmask_d = dram("cmask", [128, NG * 2], F32, "ExternalInput")
    w_rwin = dram("w_rwin", [128, 8, 4352], F32, "ExternalInput")
    w_rwout = dram("w_rwout", [128, 8, 1024], F32, "ExternalInput")
    w_cvin = dram("w_cvin", [128, 8, 3072], F32, "ExternalInput")
    w_cvout = dram("w_cvout", [128, 8, 1024], F32, "ExternalInput")
    w2a2_d = dram("w2a2", [128, 2, 1024], F32, "ExternalInput")
    pvec_d = dram("pvec", [128, NPV], F32, "ExternalInput")
    bc_d = dram("bc", [128, 2, 1024], F32, "ExternalInput")
    cbf_d = dram("cbf", [128, 5, 512], BF16, "ExternalInput")
    cf32_d = dram("cf32", [128, 3, 128], F32, "ExternalInput")
    y_out = dram("y", [NT, 128, 1024], F32, "ExternalOutput")

    fm_s = dram("fm_s", [NT, 128, 2, 4, 1024], BF16)
    tm_s = dram("tm_s", [NT, 128, 7, 1024], BF16)
    eg_s = dram("eg_s", [NT, 128, 2, 512], F32)
    bon_s = dram("bon_s", [NT, 128, 1024], F32)
    sg_s = dram("sg_s", [NT, 128, 1024], F32)
    yT_s = dram("yT_s", [2, NT, 128, 1024], F32)
    x1_s = dram("x1_s", [NT, 128, 1024], F32)
    h1T_s = dram("h1T_s", [NT, 128, 8, 128], BF16)
    z_s = dram("z_s", [8, 128, T + 32], BF16)
    sg1_s = dram("sg1_s", [NG, 8, 128, 512], F32)

    es = fw.es

    def sb(st, name, shape, dt):
        return st.enter_context(nc.sbuf_tensor(name, shape, dt))

    def psum(st, name, shape, dt):
        return st.enter_context(nc.psum_tensor(name, shape, dt))

    pvec = sb(es, "pvec_sb", [128, NPV], F32)
    b_pvec = Buf()
    cbf = sb(es, "cbf_sb", [128, 5, 512], BF16)
    b_cbf = Buf()
    cf32 = sb(es, "cf32_sb", [128, 3, 128], F32)
    b_cf = Buf()
    cst = sb(es, "cst", [128, 4], F32)
    b_cst = Buf()
    ones = sb(es, "ones", [128, 128], F32)
    b_ones = Buf()
    smask = sb(es, "smask_sb", [128, NT * 2], F32)
    cmask = sb(es, "cmask_sb", [128, NG * 2], F32)
    b_mask = Buf()
    sp.dma(pvec[:], pvec_d[:, :], writes=[b_pvec])
    sp.dma(cbf[:], cbf_d[:, :, :], writes=[b_cbf])
    sp.dma(cf32[:], cf32_d[:, :, :], writes=[b_cf])
    sp.dma(smask[:], smask_d[:, :], writes=[b_mask])
    sp.dma(cmask[:], cmask_d[:, :], writes=[b_mask])
    dve.op(lambda e: e.memset(cst[:, 0:1], 1e-5), writes=[b_cst])
    dve.op(lambda e: e.memset(cst[:, 1:2], 64e-5), writes=[b_cst])
    dve.op(lambda e: e.memset(cst[:, 2:3], 0.0), writes=[b_cst])
    dve.op(lambda e: e.memset(ones[:], 1.0), writes=[b_ones])
    ident = cbf[:, 0, 0:128]
    BO = cf32[:, 1, :]
    BOm = cf32[:, 0, :]
    O1k = cf32[:, 2, :]

    def pcol(name, j):
        return pvec[:, PV[name] + j: PV[name] + j + 1]

    CONSTS = [b_pvec, b_cbf, b_cf, b_cst, b_ones, b_mask]

    def load_weight_bf16(st, name, src, ncols, gname, dst_stage, b_stage, w=None):
        if w is None:
            w = sb(st, name, [128, 8, ncols], BF16)
        bw = Buf()
        for kc in range(8):
            for c0 in range(0, ncols, 1536):
                c1 = min(ncols, c0 + 1536)
                sp.dma(dst_stage[:, 0:c1 - c0], src[:, kc, c0:c1], writes=[b_stage])
                if gname is None:
                    act.op(lambda e: e.activation(out=w[:, kc, c0:c1], in_=dst_stage[:, 0:c1 - c0], func=AF.Copy),
                           reads=[b_stage], writes=[bw])
                else:
                    act.op(lambda e: e.activation(out=w[:, kc, c0:c1], in_=dst_stage[:, 0:c1 - c0], func=AF.Copy,
                                                  scale=pcol(gname, kc)),
                           reads=[b_stage, b_pvec], writes=[bw])
        return w, bw

    def rmsnorm_rstd(xt, bx, npart, junk, bjunk, ss, bss):
        act.op(lambda e: e.activation(out=junk[0:npart, :], in_=xt[0:npart, :], func=AF.Square,
                                      accum_out=ss[0:npart, 0:1]), reads=[bx], writes=[bjunk, bss])
        act.op(lambda e: e.activation(out=ss[0:npart, 1:2], in_=ss[0:npart, 0:1], func=AF.Ln,
                                      scale=1.0 / 1024.0, bias=cst[0:npart, 0:1]), reads=[bss, b_cst], writes=[bss])
        act.op(lambda e: e.activation(out=ss[0:npart, 2:3], in_=ss[0:npart, 1:2], func=AF.Exp, scale=-0.5),
               reads=[bss], writes=[bss])

    with ExitStack() as ph:
        pro = ExitStack()
        Win = sb(ph, "Win", [128, 8, 4352], BF16)
        w2b = sb(ph, "w2b", [128, 4, 1024], BF16)
        stage = sb(pro, "stageA", [128, 1536], F32)
        b_stage = Buf()
        Win, b_Win = load_weight_bf16(ph, "Win", w_rwin, 4352, "g0", stage, b_stage, w=Win)
        w2f = sb(pro, "w2f", [128, 2, 1024], F32)
        b_w2 = Buf()
        sp.dma(w2f[:], w2a2_d[:, :, :], writes=[b_w2])
        dve.op(lambda e: e.memset(w2b[:], 0.0), writes=[b_w2])
        for wa in range(2):
            for d in range(2):
                dve.op(lambda e: e.tensor_copy(out=w2b[d * 64:(d + 1) * 64, wa * 2 + d, :],
                                               in_=w2f[d * 64:(d + 1) * 64, wa, :]), reads=[b_w2], writes=[b_w2])
        fw.barrier()
        pro.close()
        omm = sb(ph, "omm", [128, 26], F32)
        hmu = sb(ph, "hmu", [128, 26], F32)
        b_mu = Buf()
        mu_ap = pvec[:, PV["mu"]:PV["mu"] + 26]
        dve.op(lambda e: e.tensor_scalar(out=omm[:], in0=mu_ap, scalar1=-1.0, scalar2=1.0, op0=ALU.mult, op1=ALU.add),
               reads=[b_pvec], writes=[b_mu])
        dve.op(lambda e: e.tensor_scalar(out=hmu[:], in0=mu_ap, scalar1=0.5, scalar2=None, op0=ALU.mult),
               reads=[b_pvec], writes=[b_mu])

        X = [sb(ph, "xA0", [128, 1024], F32)]
        bX = bufs(1)
        ss = sb(ph, "ssA", [128, 4], F32)
        bss = Buf()
        hb = sb(ph, "hbA", [128, 1024], BF16)
        bhb = Buf()
        hT = [sb(ph, "hTA%d" % i, [128, 8, 130], BF16) for i in range(3)]
        bhT = bufs(3)
        ptr = psum(ph, "ptrA", [128, 8, 128], BF16)
        bptr = Buf(True)
        ptrh = psum(ph, "ptrhA", [128, 1024], BF16)[:, 0:16].rearrange("p (a b) -> p a b", b=2)
        bptrh = Buf(True)
        pin = [psum(ph, "pinA%d" % i, [128, 512], F32)[:, 0:390].rearrange("p (a b) -> p a b", b=130) for i in range(2)]
        bpin = pbufs(2)
        PL = [psum(ph, "plA%d" % k, [128, 4, 128], F32) for k in range(2)]
        bPL = pbufs(2)
        pss = psum(ph, "pssA", [128, 512], F32)[:, 0:256].rearrange("p (a b) -> p a b", b=128)
        bpss = pbufs(1) * 2
        ptq = psum(ph, "ptqA", [128, 8, 128], BF16)
        bptq = Buf(True)
        ue = [sb(ph, "ueA%d" % i, [128, 3, 130], F32) for i in range(2)]
        bue = bufs(2)
        t1 = [sb(ph, "t1A%d" % i, [128, 3, 128], F32) for i in range(2)]
        bt1 = bufs(2)
        t3 = sb(ph, "t3A", [128, 128], F32)
        bt3 = Buf()
        RKVs = [sb(ph, "rkvA%d" % k, [128, 3, 8, 128], F32) for k in range(2)]
        bRKVs = [[bufs(8) for _ in range(3)] for k in range(2)]
        LOs = [sb(ph, "loA%d" % k, [128, 2, 128], F32) for k in range(2)]
        bLOs = bufs(2)
        twb = sb(ph, "twA", [128, 2, 128], BF16)
        btw = Buf()
        twt = sb(ph, "twtA", [128, 128], F32)
        btwt = Buf()
        sgt = [sb(ph, "sgA%d" % i, [128, 8, 128], F32) for i in range(1)] * 2
        bsgt = bufs(1) * 2
        FM = [sb(ph, "fmA%d" % i, [128, 2, 4, 1024], BF16) for i in range(1)] * 2
        bFM = bufs(1) * 2
        TMb = [sb(ph, "tmA%d" % i, [128, 7, 1024], BF16) for i in range(1)] * 2
        bTM = bufs(1) * 2
        EG = [sb(ph, "egA%d" % i, [128, 2, 512], F32) for i in range(1)] * 2
        bEG = bufs(1) * 2
        BON = [sb(ph, "bonA%d" % i, [128, 8, 128], F32) for i in range(1)] * 2
        bBON = bufs(1) * 2
        GT = {}
        for nm in ["sgw0", "sgw1", "a0", "a1", "cs0", "cs1", "u0", "u1", "kk", "kk2", "kkn",
                   "tmp0", "tmp1", "tmq0", "tmq1", "rk"]:
            GT[nm] = (sb(ph, "g_" + nm, [128, 4, 128], F32), bufs(4))
        SC = sb(ph, "g_sc", [128, 4, 8], F32)
        bSC = bufs(4)
        VBF = [sb(ph, "vbfA%d" % k, [128, 128], BF16) for k in range(2)]
        bVBF = bufs(2)
        bFMe = bufs(8)
        if stop == "A0":
            fw.barrier()
            return nc

        def H_stage(i):
            xm, bxm = X[0], bX[0]
            sp.dma(xm[:], x_t[i, :, :], writes=[bxm])
            rmsnorm_rstd(xm, bxm, 128, hb, bhb, ss, bss)
            act.op(lambda e: e.activation(out=hb[:], in_=xm[:], func=AF.Copy, scale=ss[:, 2:3]),
                   reads=[bxm, bss], writes=[bhb])
            hTt, bhTt = hT[i % 3], bhT[i % 3]
            for kc in range(8):
                pe.op(lambda e: e.transpose(out=ptr[:, kc, :], in_=hb[:, kc * 128:(kc + 1) * 128], identity=ident),
                      reads=[bhb, b_cbf], writes=[bptr], acc=True)
            dve.op(lambda e: e.tensor_copy(out=hTt[:, :, 1:129], in_=ptr[:]), reads=[bptr], writes=[bhTt])
            if i == 0:
                dve.op(lambda e: e.memset(hTt[:, :, 0:1], 0.0), writes=[bhTt])
            else:
                hTp, bhTp = hT[(i - 1) % 3], bhT[(i - 1) % 3]
                dve.op(lambda e: e.tensor_scalar(out=hTt[:, :, 0:1], in0=hTp[:, :, 128:129],
                                                 scalar1=smask[:, 2 * i + 1:2 * i + 2], scalar2=None, op0=ALU.mult),
                       reads=[bhTp, b_mask], writes=[bhTt])
                dve.op(lambda e: e.tensor_scalar(out=hTp[:, :, 129:130], in0=hTt[:, :, 1:2],
                                                 scalar1=smask[:, 2 * (i - 1):2 * (i - 1) + 1], scalar2=None, op0=ALU.mult),
                       reads=[bhTt, b_mask], writes=[bhTp])
            if i == NT - 1:
                dve.op(lambda e: e.memset(hTt[:, :, 129:130], 0.0), writes=[bhTt])

        def P_gen(i):
            par = i % 2
            hTt, bhTt = hT[i % 3], bhT[i % 3]
            RKV, bRKV, LO, bLO = RKVs[par], bRKVs[par], LOs[par], bLOs[par]
            for bg in range(9):
                pp = bg % 2
                cbs = [cb for cb in range(bg * 3, min(26, bg * 3 + 3))]
                for q, cb in enumerate(cbs):
                    for kc in range(8):
                        pe.op(lambda e: e.matmul(out=pin[pp][:, q, :], lhsT=Win[:, kc, cb * 128:(cb + 1) * 128],
                                                 rhs=hTt[:, kc, :], start=(kc == 0), stop=(kc == 7)),
                              reads=[b_Win, bhTt], writes=[bpin[pp]], acc=True)
                nq = len(cbs)
                act.op(lambda e: e.activation(out=ue[pp][:, 0:nq, :], in_=pin[pp][:, 0:nq, :], func=AF.Copy),
                       reads=[bpin[pp]], writes=[bue[pp]])
                dve.op(lambda e: e.tensor_tensor(out=t1[pp][:, 0:nq, :], in0=ue[pp][:, 0:nq, 0:128],
                                                 in1=ue[pp][:, 0:nq, 2:130], op=ALU.add),
                       reads=[bue[pp]], writes=[bt1[pp]])
                for q, cb in enumerate(cbs):
                    if cb < 24:
                        dst = RKV[:, cb // 8, cb % 8, :]
                        bd = bRKV[cb // 8][cb % 8]
                    else:
                        dst = LO[:, cb - 24, :]
                        bd = bLO
                    dve.op(lambda e: e.tensor_scalar(out=t3[:], in0=t1[pp][:, q, :], scalar1=hmu[:, cb:cb + 1],
                                                     scalar2=None, op0=ALU.mult),
                           reads=[bt1[pp], b_mu], writes=[bt3])
                    dve.op(lambda e: e.scalar_tensor_tensor(out=dst, in0=ue[pp][:, q, 1:129], scalar=omm[:, cb:cb + 1],
                                                            in1=t3[:], op0=ALU.mult, op1=ALU.add),
                           reads=[bue[pp], bt3, b_mu], writes=[bd])
                yield
            for gb, gcbs in enumerate([[0, 1, 2], [3, 4, 5], [6, 7]]):
                pp = (9 + gb) % 2
                for q, cb in enumerate(gcbs):
                    for kc in range(8):
                        pe.op(lambda e: e.matmul(out=pin[pp][:, q, 0:128],
                                                 lhsT=Win[:, kc, 3328 + cb * 128:3328 + (cb + 1) * 128],
                                                 rhs=hTt[:, kc, 1:129], start=(kc == 0), stop=(kc == 7)),
                              reads=[b_Win, bhTt], writes=[bpin[pp]], acc=True)
                ng = len(gcbs)
                act.op(lambda e: e.activation(out=sgt[par][:, gcbs[0]:gcbs[0] + ng, :], in_=pin[pp][:, 0:ng, 0:128],
                                              func=AF.Sigmoid), reads=[bpin[pp]], writes=[bsgt[par]])
                dve.op(lambda e: e.tensor_tensor(out=sgt[par][:, gcbs[0]:gcbs[0] + ng, :], in0=pin[pp][:, 0:ng, 0:128],
                                                 in1=sgt[par][:, gcbs[0]:gcbs[0] + ng, :], op=ALU.mult),
                       reads=[bpin[pp], bsgt[par]], writes=[bsgt[par]])
                yield
            pool.dma(sg_s[i, :, :], sgt[par][:].rearrange("p a b -> p (a b)"), reads=[bsgt[par]])

        def E_gen(i):
            par = i % 2
            RKV, bRKV, LO, bLO = RKVs[par], bRKVs[par], LOs[par], bLOs[par]
            act.op(lambda e: e.activation(out=twt[:], in_=LO[:, 0, :], func=AF.Sigmoid, scale=2.0), reads=[bLO], writes=[btwt])
            pool.op(lambda e: e.tensor_scalar(out=twb[:, 0, :], in0=twt[:], scalar1=2.0, scalar2=-1.0, op0=ALU.mult, op1=ALU.add),
                    reads=[btwt], writes=[btw])
            act.op(lambda e: e.activation(out=twb[:, 1, :], in_=LO[:, 1, :], func=AF.Copy), reads=[bLO], writes=[btw])

            fmt = FM[par]
            tmt, btmt = TMb[par], bTM[par]
            egt, begt = EG[par], bEG[par]
            bont, bbont = BON[par], bBON[par]
            for hg in range(2):
                ebs = [4 * hg + e4 for e4 in range(4)]

                def T_(nm, e4):
                    return GT[nm][0][:, e4, :]

                def B_(nm, e4):
                    return GT[nm][1][e4]
                for e4, eb in enumerate(ebs):
                    plb, bplb = PL[e4 % 2], bPL[e4 % 2]
                    es_ = slice(eb * 128, (eb + 1) * 128)
                    for slot in range(4):
                        wa = slot // 2
                        pe.op(lambda e: e.matmul(out=plb[:, slot, :], lhsT=w2b[:, slot, es_], rhs=twb[:, wa, :],
                                                 start=True, stop=True), reads=[b_w2, btw], writes=[bplb], acc=True)
                    for d in range(2):
                        act.op(lambda e: e.activation(out=T_("sgw%d" % d, e4), in_=plb[:, d, :], func=AF.Sigmoid,
                                                      bias=pcol("w0f" if d == 0 else "w0b", eb)),
                               reads=[bplb, b_pvec], writes=[B_("sgw%d" % d, e4)])
                        act.op(lambda e: e.activation(out=T_("a%d" % d, e4), in_=plb[:, 2 + d, :], func=AF.Sigmoid,
                                                      bias=pcol("a0f" if d == 0 else "a0b", eb)),
                               reads=[bplb, b_pvec], writes=[B_("a%d" % d, e4)])
                yield
                for e4 in range(4):
                    for d in range(2):
                        dve.op(lambda e: e.tensor_tensor_scan(out=T_("cs%d" % d, e4), data0=ones[:], data1=T_("sgw%d" % d, e4),
                                                              initial=0.0, op0=ALU.mult, op1=ALU.add),
                               reads=[B_("sgw%d" % d, e4), b_ones], writes=[B_("cs%d" % d, e4)])
                yield
                for e4 in range(4):
                    for d in range(2):
                        pool.op(lambda e: e.tensor_tensor(out=T_("u%d" % d, e4), in0=T_("cs%d" % d, e4), in1=T_("sgw%d" % d, e4),
                                                          op=ALU.subtract),
                                reads=[B_("cs%d" % d, e4), B_("sgw%d" % d, e4)], writes=[B_("u%d" % d, e4)])
                    dve.op(lambda e: e.tensor_scalar(out=SC[:, e4, 0:1], in0=GT["cs1"][0][:, e4, 127:128], scalar1=-CDEC,
                                                     scalar2=None, op0=ALU.mult), reads=[B_("cs1", e4)], writes=[bSC[e4]])
                    dve.op(lambda e: e.tensor_scalar(out=SC[:, e4, 1:2], in0=GT["cs1"][0][:, e4, 127:128], scalar1=CDEC,
                                                     scalar2=None, op0=ALU.mult), reads=[B_("cs1", e4)], writes=[bSC[e4]])
                yield
                for e4 in range(4):
                    act.op(lambda e: e.activation(out=SC[:, e4, 2:3], in_=GT["cs0"][0][:, e4, 127:128], func=AF.Exp, scale=-CDEC),
                           reads=[B_("cs0", e4)], writes=[bSC[e4]])
                    act.op(lambda e: e.activation(out=SC[:, e4, 3:4], in_=SC[:, e4, 0:1], func=AF.Exp),
                           reads=[bSC[e4]], writes=[bSC[e4]])
                    act.op(lambda e: e.activation(out=T_("sgw0", e4), in_=T_("cs0", e4), func=AF.Exp, scale=-CDEC),
                           reads=[B_("cs0", e4)], writes=[B_("sgw0", e4)])
                    act.op(lambda e: e.activation(out=T_("cs0", e4), in_=T_("cs0", e4), func=AF.Exp, scale=CDEC),
                           reads=[B_("cs0", e4)], writes=[B_("cs0", e4)])
                    act.op(lambda e: e.activation(out=T_("u0", e4), in_=T_("u0", e4), func=AF.Exp, scale=-CDEC),
                           reads=[B_("u0", e4)], writes=[B_("u0", e4)])
                    act.op(lambda e: e.activation(out=T_("sgw1", e4), in_=T_("u1", e4), func=AF.Exp, scale=CDEC,
                                                  bias=SC[:, e4, 0:1]),
                           reads=[B_("u1", e4), bSC[e4]], writes=[B_("sgw1", e4)])
                    act.op(lambda e: e.activation(out=T_("cs1", e4), in_=T_("cs1", e4), func=AF.Exp, scale=CDEC,
                                                  bias=SC[:, e4, 0:1]),
                           reads=[B_("cs1", e4), bSC[e4]], writes=[B_("cs1", e4)])
                    act.op(lambda e: e.activation(out=T_("u1", e4), in_=T_("u1", e4), func=AF.Exp, scale=-CDEC,
                                                  bias=SC[:, e4, 1:2]),
                           reads=[B_("u1", e4), bSC[e4]], writes=[B_("u1", e4)])
                E1n, E2n, E3n = ("sgw0", "sgw1"), ("cs0", "u1"), ("u0", "cs1")
                yield
                for e4, eb in enumerate(ebs):
                    for d in range(2):
                        pool.op(lambda e: e.tensor_scalar(out=egt[:, d, eb * 64:(eb + 1) * 64], in0=ones[:, 0:64],
                                                          scalar1=SC[:, e4, 2 + d:3 + d],
                                                          scalar2=smask[:, 2 * i + d:2 * i + d + 1], op0=ALU.mult, op1=ALU.mult),
                                reads=[bSC[e4], b_ones, b_mask], writes=[begt])
                yield
                for e4, eb in enumerate(ebs):
                    act.op(lambda e: e.activation(out=T_("kk", e4), in_=RKV[:, 1, eb, :], func=AF.Copy, scale=pcol("kk", eb)),
                           reads=[bRKV[1][eb], b_pvec], writes=[B_("kk", e4)])
                for e4, eb in enumerate(ebs):
                    act.op(lambda e: e.activation(out=T_("kk2", e4), in_=T_("kk", e4), func=AF.Square),
                           reads=[B_("kk", e4)], writes=[B_("kk2", e4)])
                for e4 in range(4):
                    pe.op(lambda e: e.matmul(out=pss[:, 0, :], lhsT=BO, rhs=T_("kk2", e4), start=True, stop=True),
                          reads=[B_("kk2", e4), b_cf], writes=[bpss[0]])
                    dve.op(lambda e: e.tensor_scalar(out=T_("kk2", e4), in0=pss[:, 0, :], scalar1=1e-19, scalar2=None,
                                                     op0=ALU.max), reads=[bpss[0]], writes=[B_("kk2", e4)])
                for e4 in range(4):
                    act.op(lambda e: e.activation(out=T_("kk2", e4), in_=T_("kk2", e4), func=AF.Ln),
                           reads=[B_("kk2", e4)], writes=[B_("kk2", e4)])
                for e4 in range(4):
                    act.op(lambda e: e.activation(out=T_("kk2", e4), in_=T_("kk2", e4), func=AF.Exp, scale=-0.5),
                           reads=[B_("kk2", e4)], writes=[B_("kk2", e4)])
                for e4 in range(4):
                    dve.op(lambda e: e.tensor_tensor(out=T_("kkn", e4), in0=T_("kk", e4), in1=T_("kk2", e4), op=ALU.mult),
                           reads=[B_("kk", e4), B_("kk2", e4)], writes=[B_("kkn", e4)])
                yield
                for e4, eb in enumerate(ebs):
                    for d in range(2):
                        dve.op(lambda e: e.tensor_scalar(out=T_("tmp%d" % d, e4), in0=T_("a%d" % d, e4), scalar1=-1.0,
                                                         scalar2=pcol("ka", eb), op0=ALU.add, op1=ALU.mult),
                                reads=[B_("a%d" % d, e4), b_pvec], writes=[B_("tmp%d" % d, e4)])
                        dve.op(lambda e: e.tensor_tensor(out=T_("tmq%d" % d, e4), in0=T_("kkn", e4), in1=T_("a%d" % d, e4),
                                                         op=ALU.mult),
                                reads=[B_("kkn", e4), B_("a%d" % d, e4)], writes=[B_("tmq%d" % d, e4)])
                yield
                def s11(kind, e4, eb, d):
                    es_ = slice(eb * 128, (eb + 1) * 128)
                    r_ap, k_ap = RKV[:, 0, eb, :], RKV[:, 1, eb, :]
                    br, bk = bRKV[0][eb], bRKV[1][eb]
                    E1, E2, E3 = E1n[d], E2n[d], E3n[d]
                    if kind == 0:
                        dve.op(lambda e: e.scalar_tensor_tensor(out=T_("tmp%d" % d, e4), in0=T_("tmp%d" % d, e4), scalar=1.0,
                                                                in1=k_ap, op0=ALU.add, op1=ALU.mult),
                               reads=[B_("tmp%d" % d, e4), bk], writes=[B_("tmp%d" % d, e4)])
                    elif kind == 1:
                        dve.op(lambda e: e.tensor_tensor(out=fmt[:, d, 0, es_], in0=r_ap, in1=T_(E1, e4), op=ALU.mult),
                               reads=[br, B_(E1, e4)], writes=[bFMe[eb]])
                    elif kind == 2:
                        dve.op(lambda e: e.scalar_tensor_tensor(out=fmt[:, d, 2, es_], in0=T_("kkn", e4), scalar=-1.0,
                                                                in1=T_(E3, e4), op0=ALU.mult, op1=ALU.mult),
                               reads=[B_("kkn", e4), B_(E3, e4)], writes=[bFMe[eb]])
                    elif kind == 3:
                        dve.op(lambda e: e.tensor_tensor(out=fmt[:, d, 3, es_], in0=T_("tmq%d" % d, e4), in1=T_(E2, e4),
                                                         op=ALU.mult),
                               reads=[B_("tmq%d" % d, e4), B_(E2, e4)], writes=[bFMe[eb]])
                    else:
                        dve.op(lambda e: e.tensor_tensor(out=fmt[:, d, 1, es_], in0=T_("tmp%d" % d, e4), in1=T_(E2, e4),
                                                         op=ALU.mult),
                               reads=[B_("tmp%d" % d, e4), B_(E2, e4)], writes=[bFMe[eb]])
                for kind in range(5):
                    for e4, eb in enumerate(ebs):
                        for d in range(2):
                            s11(kind, e4, eb, d)
                yield
                for e4, eb in enumerate(ebs):
                    dve.op(lambda e: e.scalar_tensor_tensor(out=T_("rk", e4), in0=RKV[:, 0, eb, :], scalar=pcol("rk", eb),
                                                            in1=RKV[:, 1, eb, :], op0=ALU.mult, op1=ALU.mult),
                           reads=[bRKV[0][eb], bRKV[1][eb], b_pvec], writes=[B_("rk", e4)])
                for e4, eb in enumerate(ebs):
                    pe.op(lambda e: e.matmul(out=pss[:, 1, :], lhsT=BO, rhs=T_("rk", e4), start=True, stop=True),
                          reads=[B_("rk", e4), b_cf], writes=[bpss[1]])
                    dve.op(lambda e: e.tensor_tensor(out=bont[:, eb, :], in0=pss[:, 1, :], in1=RKV[:, 2, eb, :], op=ALU.mult),
                           reads=[bpss[1], bRKV[2][eb]], writes=[bbont])
                yield
                for e4, eb in enumerate(ebs):
                    es_ = slice(eb * 128, (eb + 1) * 128)
                    vb, bvb = VBF[e4 % 2], bVBF[e4 % 2]
                    act.op(lambda e: e.activation(out=vb[:], in_=RKV[:, 2, eb, :], func=AF.Copy),
                           reads=[bRKV[2][eb]], writes=[bvb])
                    srcs = [fmt[:, 0, 1, es_], fmt[:, 0, 3, es_], fmt[:, 0, 2, es_],
                            fmt[:, 1, 1, es_], fmt[:, 1, 3, es_], fmt[:, 1, 2, es_], vb[:]]
                    for qi, s_ap in enumerate(srcs):
                        pe.op(lambda e: e.transpose(out=ptq[:, qi, :], in_=s_ap, identity=ident),
                              reads=[bFMe[eb], bvb, b_cbf], writes=[bptq], acc=True)
                    act.op(lambda e: e.activation(out=tmt[:, :, es_], in_=ptq[:, 0:7, :], func=AF.Copy),
                           reads=[bptq], writes=[btmt])
            pool.dma(fm_s[i, :, :, :, :], fmt[:], reads=bFMe)
            pool.dma(tm_s[i, :, :, :], tmt[:], reads=[btmt])
            pool.dma(eg_s[i, :, :, :], egt[:], reads=[begt])
            pool.dma(bon_s[i, :, :], bont[:].rearrange("p a b -> p (a b)"), reads=[bbont])

        def run_interleaved(gens):
            gens = [g for g in gens if g is not None]
            while gens:
                for g in list(gens):
                    try:
                        next(g)
                    except StopIteration:
                        gens.remove(g)

        H_stage(0)
        for i in range(NT):
            if i + 1 < NT:
                H_stage(i + 1)
            run_interleaved([P_gen(i), E_gen(i - 1) if i > 0 else None])
        run_interleaved([E_gen(NT - 1)])
        fw.barrier()
    if stop == "A":
        return nc

    with ExitStack() as ph:
        I4, SU4, U4, SL4, L4 = [cbf[:, m, :].rearrange("p (a b) -> p a b", b=128) for m in range(5)]
        banks = [psum(ph, "bkB%d" % i, [128, 512], F32) for i in range(8)]
        bbank = pbufs(8)
        bank_ctr = [0]

        def next_bank():
            k = bank_ctr[0] % 8
            bank_ctr[0] += 1
            return banks[k], bbank[k]

        FMd = [sb(ph, "fmB%d" % i, [128, 4, 1024], BF16) for i in range(2)]
        bFMd = bufs(2)
        TMd = [sb(ph, "tmB%d" % i, [128, 3, 1024], BF16) for i in range(2)]
        bTMd = bufs(2)
        TMv = [sb(ph, "tvB%d" % i, [128, 1024], BF16) for i in range(2)]
        bTMv = bufs(2)
        EGd = [sb(ph, "egB%d" % i, [128, 512], F32) for i in range(2)]
        bEGd = bufs(2)

        def mat16(name):
            return sb(ph, name, [128, 16, 128], BF16), bufs(4)
        Pm, bP = mat16("PmB")
        PTm, bPT = mat16("PTmB")
        P2m, bP2 = mat16("P2mB")
        PT2m, bPT2 = mat16("PT2mB")
        Xm, bXm = mat16("XmB")
        XTm, bXTm = mat16("XTmB")
        Aak, bAak = mat16("AakB")
        Arb, bArb = mat16("ArbB")
        Ark, bArk = mat16("ArkB")
        M1 = sb(ph, "M1B", [128, 16, 64], BF16)
        bM1 = bufs(2)
        AtT = sb(ph, "AtTB", [128, 8, 128], BF16)
        bAtT = bufs(2)
        Usb = sb(ph, "UsbB", [128, 16, 64], BF16)
        bUsb = bufs(2)
        h32 = sb(ph, "h32B", [128, 512], F32)
        hbf = sb(ph, "hbfB", [128, 8, 64], BF16)
        bh = Buf()
        OT = [sb(ph, "OTB%d" % i, [128, 8, 128], F32) for i in range(2)]
        bOT = bufs(2)

        def load(step, i, d):
            p = step % 2
            sp.dma(FMd[p][:], fm_s[i, :, d, :, :], writes=[bFMd[p]])
            sp.dma(TMd[p][:], tm_s[i, :, 3 * d:3 * d + 3, :], writes=[bTMd[p]])
            sp.dma(TMv[p][:], tm_s[i, :, 6, :], writes=[bTMv[p]])
            sp.dma(EGd[p][:], eg_s[i, :, d, :], writes=[bEGd[p]])

        step = 0
        order = [(i, 0) for i in range(NT)] + [(i, 1) for i in range(NT - 1, -1, -1)]
        load(0, order[0][0], order[0][1])
        for idx, (i, d) in enumerate(order):
            p = step % 2
            if idx + 1 < len(order):
                load(step + 1, order[idx + 1][0], order[idx + 1][1])
            if idx == 0 or idx == NT:
                dve.op(lambda e: e.memset(h32[:], 0.0), writes=[bh])
                dve.op(lambda e: e.memset(hbf[:], 0.0), writes=[bh])
            fm, bfm, tmd, btmd, tv, btv, eg, beg = FMd[p], bFMd[p], TMd[p], bTMd[p], TMv[p], bTMv[p], EGd[p], bEGd[p]
            m_su, m_u, m_sl = (SU4, U4, SL4) if d == 0 else (SL4, L4, SU4)

            def fmh(q, h):
                eb, j = h // 2, h % 2
                return fm[j * 64:(j + 1) * 64, q, eb * 128:(eb + 1) * 128]

            def tmh(q, h):
                return tmd[:, q, h * 64:(h + 1) * 64]

            def vh(h):
                return tv[:, h * 64:(h + 1) * 64]

            def prod(lq, rq, mask, dst, bdst):
                for hb8 in range(2):
                    bkj = [next_bank(), next_bank()]
                    for e4 in range(4):
                        for j in range(2):
                            h = hb8 * 8 + e4 * 2 + j
                            bk, bbk = bkj[j]
                            pe.op(lambda e: e.matmul(out=bk[:, e4 * 128:(e4 + 1) * 128], lhsT=fmh(lq, h), rhs=fmh(rq, h),
                                                     start=True, stop=True), reads=[bfm], writes=[bbk], acc=True)
                    for j in range(2):
                        bk, bbk = bkj[j]
                        dve.op(lambda e: e.tensor_tensor(out=dst[:, hb8 * 8 + j:hb8 * 8 + 8:2, :],
                                                         in0=bk[:].rearrange("p (a b) -> p a b", b=128), in1=mask, op=ALU.mult),
                               reads=[bbk, b_cbf], writes=[bdst[hb8 * 2], bdst[hb8 * 2 + 1]])
            prod(3, 2, m_su, Pm, bP)
            prod(2, 3, m_sl, PTm, bPT)
            prod(1, 2, m_su, Aak, bAak)
            prod(3, 0, m_u, Arb, bArb)
            prod(1, 0, m_u, Ark, bArk)
            for hbk in range(4):
                hs = slice(hbk * 4, (hbk + 1) * 4)
                dve.op(lambda e: e.tensor_tensor(out=Xm[:, hs, :], in0=Pm[:, hs, :], in1=I4, op=ALU.add),
                       reads=[bP[hbk], b_cbf], writes=[bXm[hbk]])
                dve.op(lambda e: e.tensor_tensor(out=XTm[:, hs, :], in0=PTm[:, hs, :], in1=I4, op=ALU.add),
                       reads=[bPT[hbk], b_cbf], writes=[bXTm[hbk]])
            cur = (Pm, bP, PTm, bPT)
            nxt = (P2m, bP2, PT2m, bPT2)
            for lev in range(6):
                Pc, bPc, PTc, bPTc = cur
                Pn, bPn, PTn, bPTn = nxt
                last = (lev == 5)

                def mm_batch(lhs, blhs, rhs, brhs, evac):
                    for hbk in range(4):
                        bk, bbk = next_bank()
                        for hh in range(4):
                            h = hbk * 4 + hh
                            pe.op(lambda e: e.matmul(out=bk[:, hh * 128:(hh + 1) * 128], lhsT=lhs[:, h, :], rhs=rhs[:, h, :],
                                                     start=True, stop=True),
                                  reads=[blhs[hbk], brhs[hbk]], writes=[bbk], acc=True)
                        evac(hbk, bk, bbk)

                def ev_copy(dst, bdst):
                    def f(hbk, bk, bbk):
                        act.op(lambda e: e.activation(out=dst[:, hbk * 4:(hbk + 1) * 4, :],
                                                      in_=bk[:].rearrange("p (a b) -> p a b", b=128), func=AF.Copy),
                               reads=[bbk], writes=[bdst[hbk]])
                    return f

                def ev_add(dst, bdst):
                    def f(hbk, bk, bbk):
                        hs = slice(hbk * 4, (hbk + 1) * 4)
                        dve.op(lambda e: e.tensor_tensor(out=dst[:, hs, :], in0=bk[:].rearrange("p (a b) -> p a b", b=128),
                                                         in1=dst[:, hs, :], op=ALU.add),
                               reads=[bbk, bdst[hbk]], writes=[bdst[hbk]])
                    return f
                mm_batch(PTc, bPTc, Pc, bPc, ev_copy(Pn, bPn))
                if not last:
                    mm_batch(Pc, bPc, PTc, bPTc, ev_copy(PTn, bPTn))
                mm_batch(XTm, bXTm, Pn, bPn, ev_add(Xm, bXm))
                if not last:
                    mm_batch(Pn, bPn, XTm, bXTm, ev_add(XTm, bXTm))
                cur, nxt = nxt, cur
            for g8 in range(2):
                bk, bbk = next_bank()
                for hh in range(8):
                    h = g8 * 8 + hh
                    pe.op(lambda e: e.matmul(out=bk[:, hh * 64:(hh + 1) * 64], lhsT=Aak[:, h, :], rhs=vh(h),
                                             start=True, stop=True), reads=[bAak[h // 4], btv], writes=[bbk], acc=True)
                act.op(lambda e: e.activation(out=M1[:, g8 * 8:(g8 + 1) * 8, :],
                                              in_=bk[:].rearrange("p (a b) -> p a b", b=64), func=AF.Copy),
                       reads=[bbk], writes=[bM1[g8]])
            for g4 in range(2):
                bk, bbk = next_bank()
                for e4 in range(4):
                    eb = g4 * 4 + e4
                    for j in range(2):
                        h = eb * 2 + j
                        pe.op(lambda e: e.matmul(out=bk[j * 64:(j + 1) * 64, e4 * 128:(e4 + 1) * 128], lhsT=tmh(2, h),
                                                 rhs=Xm[:, h, :], start=True, stop=True),
                              reads=[btmd, bXm[h // 4]], writes=[bbk], acc=True)
                act.op(lambda e: e.activation(out=AtT[:, g4 * 4:(g4 + 1) * 4, :],
                                              in_=bk[:].rearrange("p (a b) -> p a b", b=128), func=AF.Copy),
                       reads=[bbk], writes=[bAtT[g4]])
            for g8 in range(2):
                bk, bbk = next_bank()
                for hh in range(8):
                    h = g8 * 8 + hh
                    eb, j = h // 2, h % 2
                    js = slice(j * 64, (j + 1) * 64)
                    pe.op(lambda e: e.matmul(out=bk[:, hh * 64:(hh + 1) * 64], lhsT=AtT[js, eb, :], rhs=hbf[js, eb, :],
                                             start=True, stop=False), reads=[bAtT[eb // 4], bh], writes=[bbk], acc=True)
                    pe.op(lambda e: e.matmul(out=bk[:, hh * 64:(hh + 1) * 64], lhsT=Xm[:, h, :], rhs=M1[:, h, :],
                                             start=False, stop=True), reads=[bXm[h // 4], bM1[g8]], writes=[bbk], acc=True)
                dve.op(lambda e: e.tensor_copy(out=Usb[:, g8 * 8:(g8 + 1) * 8, :],
                                               in_=bk[:].rearrange("p (a b) -> p a b", b=64)),
                       reads=[bbk], writes=[bUsb[g8]])
            ot, bot = OT[p], bOT[p]
            for g4 in range(2):
                bk, bbk = next_bank()
                for e4 in range(4):
                    eb = g4 * 4 + e4
                    for j in range(2):
                        h = eb * 2 + j
                        js = slice(j * 64, (j + 1) * 64)
                        o_ap = bk[js, e4 * 128:(e4 + 1) * 128]
                        pe.op(lambda e: e.matmul(out=o_ap, lhsT=hbf[js, eb, :], rhs=fmh(0, h), start=True, stop=False),
                              reads=[bh, bfm], writes=[bbk], acc=True)
                        pe.op(lambda e: e.matmul(out=o_ap, lhsT=Usb[:, h, :], rhs=Arb[:, h, :], start=False, stop=False),
                              reads=[bUsb[h // 8], bArb[h // 4]], writes=[bbk], acc=True)
                        pe.op(lambda e: e.matmul(out=o_ap, lhsT=vh(h), rhs=Ark[:, h, :], start=False, stop=True),
                              reads=[btv, bArk[h // 4]], writes=[bbk], acc=True)
                act.op(lambda e: e.activation(out=ot[:, g4 * 4:(g4 + 1) * 4, :],
                                              in_=bk[:].rearrange("p (a b) -> p a b", b=128), func=AF.Copy),
                       reads=[bbk], writes=[bot])
            pool.dma(yT_s[d, i, :, :], ot[:].rearrange("p a b -> p (a b)"), reads=[bot])
            bk, bbk = next_bank()
            for eb in range(8):
                for j in range(2):
                    h = eb * 2 + j
                    js = slice(j * 64, (j + 1) * 64)
                    o_ap = bk[js, eb * 64:(eb + 1) * 64]
                    pe.op(lambda e: e.matmul(out=o_ap, lhsT=tmh(1, h), rhs=Usb[:, h, :], start=True, stop=False),
                          reads=[btmd, bUsb[h // 8]], writes=[bbk], acc=True)
                    pe.op(lambda e: e.matmul(out=o_ap, lhsT=tmh(0, h), rhs=vh(h), start=False, stop=True),
                          reads=[btmd, btv], writes=[bbk], acc=True)
            dve.op(lambda e: e.tensor_tensor(out=h32[:], in0=bk[:], in1=h32[:], op=ALU.add), reads=[bbk, bh], writes=[bh])
            dve.op(lambda e: e.tensor_tensor(out=h32[:], in0=h32[:], in1=eg[:], op=ALU.mult), reads=[bh, beg], writes=[bh])
            act.op(lambda e: e.activation(out=hbf[:].rearrange("p a b -> p (a b)"), in_=h32[:], func=AF.Copy),
                   reads=[bh], writes=[bh])
            step += 1
        fw.barrier()
    if stop == "B":
        return nc

    with ExitStack() as ph:
        stage = sb(ph, "stageC", [128, 1536], F32)
        b_stage = Buf()
        Wout, b_Wout = load_weight_bf16(ph, "Wout", w_rwout, 1024, None, stage, b_stage)
        YF = [sb(ph, "yfC%d" % i, [128, 8, 128], F32) for i in range(2)]
        YB = [sb(ph, "ybC%d" % i, [128, 8, 128], F32) for i in range(2)]
        BN = [sb(ph, "bnC%d" % i, [128, 8, 128], F32) for i in range(2)]
        SG = [sb(ph, "sgC%d" % i, [128, 8, 128], F32) for i in range(2)]
        XC = [sb(ph, "xC%d" % i, [128, 1024], F32) for i in range(2)]
        bIN = bufs(2)
        yg = sb(ph, "ygC", [128, 8, 128], BF16)
        byg = Buf()
        pst = psum(ph, "pstC", [128, 512], F32)[:, 0:256].rearrange("p (a b) -> p a b", b=128)
        bpst = pbufs(1) * 2
        pout = [psum(ph, "poutC%d" % i, [128, 512], F32) for i in range(2)]
        bpout = pbufs(2)
        ptr = psum(ph, "ptrC", [128, 8, 128], BF16)
        bptr = Buf(True)
        x1 = [sb(ph, "x1C%d" % i, [128, 1024], F32) for i in range(2)]
        bx1 = bufs(2)
        junk = sb(ph, "junkC", [128, 1024], F32)
        bjunk = Buf()
        ss = sb(ph, "ssC", [128, 4], F32)
        bss = Buf()
        hb = sb(ph, "hbC", [128, 1024], BF16)
        bhb = Buf()
        h1T = [sb(ph, "h1TC%d" % i, [128, 8, 128], BF16) for i in range(2)]
        bh1T = bufs(2)
        Yw = sb(ph, "YwC", [128, 8, 128], F32)
        bYw = bufs(2)
        SQ = sb(ph, "SQC", [128, 8, 128], F32)
        bSQ = bufs(2)
        pm = [psum(ph, "pmC%d" % k, [128, 4, 128], F32) for k in range(2)]
        bpm = pbufs(2)

        def loadC(i):
            p = i % 2
            sp.dma(YF[p][:].rearrange("p a b -> p (a b)"), yT_s[0, i, :, :], writes=[bIN[p]])
            sp.dma(YB[p][:].rearrange("p a b -> p (a b)"), yT_s[1, i, :, :], writes=[bIN[p]])
            sp.dma(BN[p][:].rearrange("p a b -> p (a b)"), bon_s[i, :, :], writes=[bIN[p]])
            sp.dma(SG[p][:].rearrange("p a b -> p (a b)"), sg_s[i, :, :], writes=[bIN[p]])
            sp.dma(XC[p][:], x_t[i, :, :], writes=[bIN[p]])
        loadC(0)
        for i in range(NT):
            p = i % 2
            if i + 1 < NT:
                loadC(i + 1)
            bin_ = bIN[p]
            for hf in range(2):
                hs = slice(hf * 4, (hf + 1) * 4)
                dve.op(lambda e: e.tensor_tensor(out=Yw[:, hs, :], in0=YF[p][:, hs, :], in1=YB[p][:, hs, :], op=ALU.add),
                       reads=[bin_], writes=[bYw[hf]])
                for e4 in range(4):
                    pe.op(lambda e: e.matmul(out=pm[hf][:, e4, :], lhsT=BOm, rhs=Yw[:, hf * 4 + e4, :], start=True, stop=True),
                          reads=[bYw[hf], b_cf], writes=[bpm[hf]], acc=True)
                dve.op(lambda e: e.tensor_tensor(out=Yw[:, hs, :], in0=Yw[:, hs, :], in1=pm[hf][:], op=ALU.subtract),
                       reads=[bYw[hf], bpm[hf]], writes=[bYw[hf]])
                act.op(lambda e: e.activation(out=SQ[:, hs, :], in_=Yw[:, hs, :], func=AF.Square),
                       reads=[bYw[hf]], writes=[bSQ[hf]])
                for e4 in range(4):
                    pe.op(lambda e: e.matmul(out=pm[hf][:, e4, :], lhsT=BOm, rhs=SQ[:, hf * 4 + e4, :], start=True, stop=True),
                          reads=[bSQ[hf], b_cf], writes=[bpm[hf]], acc=True)
                act.op(lambda e: e.activation(out=SQ[:, hs, :], in_=pm[hf][:], func=AF.Ln, bias=cst[:, 1:2]),
                       reads=[bpm[hf], b_cst], writes=[bSQ[hf]])
                act.op(lambda e: e.activation(out=SQ[:, hs, :], in_=SQ[:, hs, :], func=AF.Exp, scale=-0.5),
                       reads=[bSQ[hf]], writes=[bSQ[hf]])
                dve.op(lambda e: e.tensor_tensor(out=Yw[:, hs, :], in0=Yw[:, hs, :], in1=SQ[:, hs, :], op=ALU.mult),
                       reads=[bYw[hf], bSQ[hf]], writes=[bYw[hf]])
                for e4 in range(4):
                    eb = hf * 4 + e4
                    act.op(lambda e: e.activation(out=Yw[:, eb, :], in_=Yw[:, eb, :], func=AF.Identity,
                                                  scale=pcol("lnxg", eb), bias=pcol("lnxb", eb)),
                           reads=[bYw[hf], b_pvec], writes=[bYw[hf]])
                dve.op(lambda e: e.tensor_tensor(out=Yw[:, hs, :], in0=Yw[:, hs, :], in1=BN[p][:, hs, :], op=ALU.add),
                       reads=[bYw[hf], bin_], writes=[bYw[hf]])
                dve.op(lambda e: e.tensor_tensor(out=yg[:, hs, :], in0=Yw[:, hs, :], in1=SG[p][:, hs, :], op=ALU.mult),
                       reads=[bYw[hf], bin_], writes=[byg])
            for half in range(2):
                for eb in range(8):
                    pe.op(lambda e: e.matmul(out=pout[half][:], lhsT=yg[:, eb, :], rhs=Wout[:, eb, half * 512:(half + 1) * 512],
                                             start=(eb == 0), stop=(eb == 7)),
                          reads=[byg, b_Wout], writes=[bpout[half]], acc=True)
                dve.op(lambda e: e.tensor_tensor(out=x1[p][:, half * 512:(half + 1) * 512], in0=pout[half][:],
                                                 in1=XC[p][:, half * 512:(half + 1) * 512], op=ALU.add),
                       reads=[bpout[half], bin_], writes=[bx1[p]])
            pool.dma(x1_s[i, :, :], x1[p][:], reads=[bx1[p]])
            rmsnorm_rstd(x1[p], bx1[p], 128, junk, bjunk, ss, bss)
            act.op(lambda e: e.activation(out=hb[:], in_=x1[p][:], func=AF.Copy, scale=ss[:, 2:3]),
                   reads=[bx1[p], bss], writes=[bhb])
            for kc in range(8):
                pe.op(lambda e: e.transpose(out=ptr[:, kc, :], in_=hb[:, kc * 128:(kc + 1) * 128], identity=ident),
                      reads=[bhb, b_cbf], writes=[bptr], acc=True)
            dve.op(lambda e: e.tensor_copy(out=h1T[p][:], in_=ptr[:]), reads=[bptr], writes=[bh1T[p]])
            pool.dma(h1T_s[i, :, :, :], h1T[p][:], reads=[bh1T[p]])
        fw.barrier()
    if stop == "C":
        return nc

    with ExitStack() as ph:
        stage = sb(ph, "stageD", [128, 1536], F32)
        b_stage = Buf()
        Wc, b_Wc = load_weight_bf16(ph, "Wc", w_cvin, 3072, "g1", stage, b_stage)
        H1 = [sb(ph, "H1D%d" % i, [128, 8, 512], BF16) for i in range(2)]
        bH1 = bufs(2)
        pb = [psum(ph, "pbD%d" % i, [128, 512], F32) for i in range(6)]
        bpb = pbufs(6)
        sgl = [sb(ph, "sglD%d" % i, [128, 512], F32) for i in range(2)]
        bsgl = bufs(2)
        zt = [sb(ph, "ztD%d" % i, [128, 512], BF16) for i in range(2)]
        bzt = bufs(2)
        sg1 = [sb(ph, "sg1D%d" % i, [128, 512], F32) for i in range(2)]
        bsg1 = bufs(2)
        zero = sb(ph, "zeroD", [128, 16], BF16)
        bzero = Buf()
        dve.op(lambda e: e.memset(zero[:], 0.0), writes=[bzero])
        for cbk in range(8):
            pool.dma(z_s[cbk, :, 0:16], zero[:, 0:16], reads=[bzero])
            pool.dma(z_s[cbk, :, T + 16:T + 32], zero[:, 0:16], reads=[bzero])

        def loadD(g):
            p = g % 2
            for tt in range(4):
                sp.dma(H1[p][:, :, tt * 128:(tt + 1) * 128], h1T_s[4 * g + tt, :, :, :], writes=[bH1[p]])
        loadD(0)
        cnt = 0
        for g in range(NG):
            p = g % 2
            if g + 1 < NG:
                loadD(g + 1)
            for cbk in range(8):
                q = cnt % 2
                cnt += 1
                pbs = [pb[q * 3 + k] for k in range(3)]
                bpbs = [bpb[q * 3 + k] for k in range(3)]
                for k in range(3):
                    c0 = k * 1024 + cbk * 128
                    for kc in range(8):
                        pe.op(lambda e: e.matmul(out=pbs[k][:], lhsT=Wc[:, kc, c0:c0 + 128], rhs=H1[p][:, kc, :],
                                                 start=(kc == 0), stop=(kc == 7)),
                              reads=[b_Wc, bH1[p]], writes=[bpbs[k]], acc=True)
                act.op(lambda e: e.activation(out=sgl[q][:], in_=pbs[1][:], func=AF.Sigmoid, bias=pcol("bin", 8 + cbk)),
                       reads=[bpbs[1], b_pvec], writes=[bsgl[q]])
                dve.op(lambda e: e.scalar_tensor_tensor(out=zt[q][:], in0=pbs[0][:], scalar=pcol("bin", cbk), in1=sgl[q][:],
                                                        op0=ALU.add, op1=ALU.mult),
                       reads=[bpbs[0], bsgl[q], b_pvec], writes=[bzt[q]])
                pool.dma(z_s[cbk, :, 16 + g * 512:16 + (g + 1) * 512], zt[q][:], reads=[bzt[q]])
                act.op(lambda e: e.activation(out=sg1[q][:], in_=pbs[2][:], func=AF.Sigmoid, bias=pcol("bin", 16 + cbk)),
                       reads=[bpbs[2], b_pvec], writes=[bsg1[q]])
                dve.op(lambda e: e.scalar_tensor_tensor(out=sg1[q][:], in0=pbs[2][:], scalar=pcol("bin", 16 + cbk), in1=sg1[q][:],
                                                        op0=ALU.add, op1=ALU.mult),
                       reads=[bpbs[2], bsg1[q], b_pvec], writes=[bsg1[q]])
                pool.dma(sg1_s[g, cbk, :, :], sg1[q][:], reads=[bsg1[q]])
        fw.barrier()
    if stop == "D1":
        return nc

    with ExitStack() as ph:
        stage = sb(ph, "stageE", [128, 1536], F32)
        b_stage = Buf()
        Wo, b_Wo = load_weight_bf16(ph, "Wo", w_cvout, 1024, None, stage, b_stage)
        bcs = sb(ph, "bcsE", [128, 2, 1024], F32)
        b_bc = Buf()
        sp.dma(bcs[:], bc_d[:, :, :], writes=[b_bc])
        zw = [sb(ph, "zwE%d" % i, [128, 542], BF16) for i in range(3)]
        bzw = bufs(3)
        zc = sb(ph, "zcE", [128, 8, 512], F32)
        bzc = bufs(8)
        sq = [sb(ph, "sqE%d" % i, [128, 512], F32) for i in range(2)]
        bsq = bufs(2)
        DG = sb(ph, "DGE", [128, 248, 128], BF16)
        b_DG = Buf()
        for idx in range(248):
            dve.op(lambda e: e.tensor_scalar(out=DG[:, idx, :], in0=ident, scalar1=pvec[:, PV["dw"] + idx:PV["dw"] + idx + 1],
                                             scalar2=None, op0=ALU.mult), reads=[b_cbf, b_pvec], writes=[b_DG])
        pconv = [psum(ph, "pconvE%d" % i, [128, 512], F32) for i in range(2)]
        bpconv = pbufs(2)
        pmean = psum(ph, "pmeanE", [128, 512], F32)
        bpmean = Buf(True)
        pvar = psum(ph, "pvarE", [128, 512], F32)
        bpvar = Buf(True)
        rs = sb(ph, "rsE", [128, 512], F32)
        brs = Buf()
        sg1 = [sb(ph, "sg1E%d" % i, [128, 512], F32) for i in range(2)]
        bsg1 = bufs(2)
        s1 = [sb(ph, "s1E%d" % i, [128, 512], F32) for i in range(2)]
        bs1 = bufs(2)
        ZF = sb(ph, "ZFE", [128, 8, 512], BF16)
        bZF = Buf()
        pout = [psum(ph, "poutE%d" % i, [128, 512], F32) for i in range(4)]
        bpout = pbufs(4)
        x1 = [sb(ph, "x1E%d" % i, [128, 1024], F32) for i in range(2)]
        bx1 = bufs(2)
        x2 = [sb(ph, "x2E%d" % i, [128, 1024], F32) for i in range(2)]
        bx2 = bufs(2)
        yo = [sb(ph, "yoE%d" % i, [128, 1024], F32) for i in range(2)]
        byo = bufs(2)
        junk = sb(ph, "junkE", [128, 1024], F32)
        bjunk = Buf()
        ss = sb(ph, "ssE", [128, 4], F32)
        bss = Buf()
        lc = 0
        oc = 0
        for g in range(NG):
            for cbk in range(8):
                q = lc % 3
                lc += 1
                sp.dma(zw[q][:], z_s[cbk, :, g * 512 + 1:g * 512 + 543], writes=[bzw[q]])
                dve.op(lambda e: e.tensor_scalar(out=zw[q][:, 0:15], in0=zw[q][:, 0:15], scalar1=cmask[:, 2 * g:2 * g + 1],
                                                 scalar2=None, op0=ALU.mult), reads=[bzw[q], b_mask], writes=[bzw[q]])
                dve.op(lambda e: e.tensor_scalar(out=zw[q][:, 527:542], in0=zw[q][:, 527:542],
                                                 scalar1=cmask[:, 2 * g + 1:2 * g + 2], scalar2=None, op0=ALU.mult),
                       reads=[bzw[q], b_mask], writes=[bzw[q]])
                pc, bpc = pconv[cbk % 2], bpconv[cbk % 2]
                for j in range(31):
                    pe.op(lambda e: e.matmul(out=pc[:], lhsT=DG[:, cbk * 31 + j, :], rhs=zw[q][:, j:j + 512],
                                             start=(j == 0), stop=(j == 30)),
                          reads=[b_DG, bzw[q]], writes=[bpc], acc=True)
                act.op(lambda e: e.activation(out=zc[:, cbk, :], in_=pc[:], func=AF.Identity, bias=pcol("bdw", cbk)),
                       reads=[bpc, b_pvec], writes=[bzc[cbk]])
                pe.op(lambda e: e.matmul(out=pmean[:], lhsT=O1k, rhs=zc[:, cbk, :], start=(cbk == 0), stop=(cbk == 7)),
                      reads=[bzc[cbk], b_cf], writes=[bpmean], acc=True)
            for cbk in range(8):
                q = cbk % 2
                dve.op(lambda e: e.tensor_tensor(out=zc[:, cbk, :], in0=zc[:, cbk, :], in1=pmean[:], op=ALU.subtract),
                       reads=[bzc[cbk], bpmean], writes=[bzc[cbk]])
                act.op(lambda e: e.activation(out=sq[q][:], in_=zc[:, cbk, :], func=AF.Square),
                       reads=[bzc[cbk]], writes=[bsq[q]])
                pe.op(lambda e: e.matmul(out=pvar[:], lhsT=O1k, rhs=sq[q][:], start=(cbk == 0), stop=(cbk == 7)),
                      reads=[bsq[q], b_cf], writes=[bpvar], acc=True)
            act.op(lambda e: e.activation(out=rs[:], in_=pvar[:], func=AF.Ln, bias=cst[:, 0:1]),
                   reads=[bpvar, b_cst], writes=[brs])
            act.op(lambda e: e.activation(out=rs[:], in_=rs[:], func=AF.Exp, scale=-0.5), reads=[brs], writes=[brs])
            for cbk in range(8):
                q = cbk % 2
                sp.dma(sg1[q][:], sg1_s[g, cbk, :, :], writes=[bsg1[q]])
                dve.op(lambda e: e.tensor_tensor(out=s1[q][:], in0=zc[:, cbk, :], in1=rs[:], op=ALU.mult),
                       reads=[bzc[cbk], brs], writes=[bs1[q]])
                act.op(lambda e: e.activation(out=s1[q][:], in_=s1[q][:], func=AF.Silu, scale=pcol("lng", cbk),
                                              bias=pcol("lnb", cbk)), reads=[bs1[q], b_pvec], writes=[bs1[q]])
                dve.op(lambda e: e.tensor_tensor(out=ZF[:, cbk, :], in0=s1[q][:], in1=sg1[q][:], op=ALU.mult),
                       reads=[bs1[q], bsg1[q]], writes=[bZF])
            for tt in range(4):
                i = 4 * g + tt
                p = oc % 2
                oc += 1
                sp.dma(x1[p][:], x1_s[i, :, :], writes=[bx1[p]])
                for half in range(2):
                    pk = p * 2 + half
                    hs = slice(half * 512, (half + 1) * 512)
                    for cbk in range(8):
                        pe.op(lambda e: e.matmul(out=pout[pk][:], lhsT=ZF[:, cbk, tt * 128:(tt + 1) * 128],
                                                 rhs=Wo[:, cbk, hs], start=(cbk == 0), stop=(cbk == 7)),
                              reads=[bZF, b_Wo], writes=[bpout[pk]], acc=True)
                    dve.op(lambda e: e.tensor_tensor(out=x2[p][:, hs], in0=pout[pk][:], in1=bcs[:, 0, hs], op=ALU.add),
                           reads=[bpout[pk], b_bc], writes=[bx2[p]])
                dve.op(lambda e: e.tensor_tensor(out=x2[p][:], in0=x2[p][:], in1=x1[p][:], op=ALU.add),
                       reads=[bx2[p], bx1[p]], writes=[bx2[p]])
                rmsnorm_rstd(x2[p], bx2[p], 128, junk, bjunk, ss, bss)
                dve.op(lambda e: e.scalar_tensor_tensor(out=yo[p][:], in0=x2[p][:], scalar=ss[:, 2:3], in1=bcs[:, 1, :],
                                                        op0=ALU.mult, op1=ALU.mult),
                       reads=[bx2[p], bss, b_bc], writes=[byo[p]])
                pool.dma(y_out[i, :, :], yo[p][:], reads=[byo[p]])
        fw.barrier()
    return nc


def build(T, stop=None):
    dry = _build(T, stop, None)
    plan = dry._fw.get_plan()
    return _build(T, stop, plan)


def _fm(v):
    v = np.asarray(v, np.float32).reshape(-1)
    return np.ascontiguousarray(v.reshape(-1, 128).T)


def _wrows(w):
    w = np.asarray(w, np.float32)
    return np.ascontiguousarray(w.reshape(8, 128, w.shape[1]).transpose(1, 0, 2))


def shared_inputs(norm_g, final_g, rw_in, rw_mu, rw_w0, rw_w2, rw_a0, rw_a2, rw_kk, rw_ka, rw_rk,
                  rw_lnx_g, rw_lnx_b, rw_out, cv_in, cv_b_in, cv_dw, cv_b_dw, cv_ln_g, cv_ln_b, cv_out, cv_b_out):
    pv = np.zeros((128, NPV), np.float32)

    def put(name, arr):
        pv[:, PV[name]:PV[name] + arr.shape[1]] = arr
    put("g0", _fm(norm_g[0]))
    put("g1", _fm(norm_g[1]))
    put("w0f", _fm(rw_w0[0, 0]))
    put("w0b", _fm(rw_w0[0, 1]))
    put("a0f", _fm(rw_a0[0, 0]))
    put("a0b", _fm(rw_a0[0, 1]))
    put("kk", _fm(rw_kk[0]))
    put("ka", _fm(rw_ka[0]))
    put("rk", _fm(rw_rk[0]))
    put("lnxg", _fm(rw_lnx_g[0]))
    put("lnxb", _fm(rw_lnx_b[0]))
    put("bdw", _fm(cv_b_dw[0]))
    put("lng", _fm(cv_ln_g[0]))
    put("lnb", _fm(cv_ln_b[0]))
    put("mu", _fm(rw_mu[0]))
    put("bin", _fm(cv_b_in[0]))
    dw = np.asarray(cv_dw[0], np.float32)
    dwf = dw.T.reshape(8, 128, 31).transpose(1, 0, 2).reshape(128, 248)
    put("dw", dwf)
    w2a2 = np.stack([np.asarray(rw_w2[0], np.float32).reshape(128, 1024),
                     np.asarray(rw_a2[0], np.float32).reshape(128, 1024)], axis=1)
    bc = np.stack([np.broadcast_to(np.asarray(cv_b_out[0], np.float32), (128, 1024)),
                   np.broadcast_to(np.asarray(final_g, np.float32), (128, 1024))], axis=1)
    r = np.arange(128)
    eye = (r[:, None] == r[None, :])
    su = (r[:, None] < r[None, :])
    u = (r[:, None] <= r[None, :])
    sl = (r[:, None] > r[None, :])
    l = (r[:, None] >= r[None, :])
    cbf = np.stack([np.tile(m.astype(np.float32), (1, 4)) for m in (eye, su, u, sl, l)], axis=1).astype(ml_dtypes.bfloat16)
    blk = ((r[:, None] // 64) == (r[None, :] // 64)).astype(np.float32)
    cf32 = np.stack([blk / 64.0, blk, np.full((128, 128), 1.0 / 1024.0, np.float32)], axis=1).astype(np.float32)
    return {
        "w_rwin": _wrows(rw_in[0]), "w_rwout": _wrows(rw_out[0]), "w_cvin": _wrows(cv_in[0]),
        "w_cvout": _wrows(cv_out[0]), "w2a2": np.ascontiguousarray(w2a2), "pvec": pv,
        "bc": np.ascontiguousarray(bc), "cbf": np.ascontiguousarray(cbf), "cf32": np.ascontiguousarray(cf32),
    }


def core_inputs(seqs):
    xs = np.concatenate([np.asarray(s, np.float32) for s in seqs], axis=0)
    T = xs.shape[0]
    NT, NG = T // 128, T // 512
    starts = set()
    o = 0
    for s in seqs:
        starts.add(o)
        o += s.shape[0]
    starts.add(T)
    x_t = xs.reshape(NT, 128, 1024)
    x_h = np.zeros((NT, 2, 1024), np.float32)
    sm = np.ones((NT, 2), np.float32)
    for i in range(NT):
        t0, t1 = i * 128, (i + 1) * 128
        if t0 not in starts:
            x_h[i, 0] = xs[t0 - 1]
        if t1 not in starts:
            x_h[i, 1] = xs[t1]
        if t1 in starts:
            sm[i, 0] = 0.0
        if t0 in starts:
            sm[i, 1] = 0.0
    cm = np.ones((NG, 2), np.float32)
    for g in range(NG):
        if g * 512 in starts:
            cm[g, 0] = 0.0
        if (g + 1) * 512 in starts:
            cm[g, 1] = 0.0
    return {
        "x_t": np.ascontiguousarray(x_t), "x_h": x_h,
        "smask": np.ascontiguousarray(np.broadcast_to(sm.reshape(1, -1), (128, NT * 2))),
        "cmask": np.ascontiguousarray(np.broadcast_to(cm.reshape(1, -1), (128, NG * 2))),
    }


def kernel(x_prompt, x_sample, norm_g, final_g, rw_in, rw_mu, rw_w0, rw_w2, rw_a0, rw_a2, rw_kk, rw_ka, rw_rk,
           rw_lnx_g, rw_lnx_b, rw_out, cv_in, cv_b_in, cv_dw, cv_b_dw, cv_ln_g, cv_ln_b, cv_out, cv_b_out):
    x_prompt = np.asarray(x_prompt, np.float32)
    x_sample = np.asarray(x_sample, np.float32)
    T = 16384
    sh = shared_inputs(norm_g, final_g, rw_in, rw_mu, rw_w0, rw_w2, rw_a0, rw_a2, rw_kk, rw_ka, rw_rk,
                       rw_lnx_g, rw_lnx_b, rw_out, cv_in, cv_b_in, cv_dw, cv_b_dw, cv_ln_g, cv_ln_b, cv_out, cv_b_out)
    cores = [core_inputs([x_prompt[0]]), core_inputs([x_prompt[1]]),
             core_inputs([x_sample[b] for b in range(8)])]
    in_maps = []
    for c in range(8):
        m = dict(sh)
        m.update(cores[min(c, 2)])
        in_maps.append(m)
    nc = build(T)
    res = run_bass_kernel_spmd(nc, in_maps, core_ids=list(range(8)))
    ys = [np.asarray(res.results[c]["y"], np.float32).reshape(T, 1024) for c in range(3)]
    y_prompt = np.stack([ys[0], ys[1]], axis=0)
    y_sample = ys[2].reshape(8, 2048, 1024)
    return (y_prompt, y_sample)
```

```python
import bisect
import numpy as np
import ml_dtypes
from contextlib import ExitStack
import concourse.bass as bass
import concourse.mybir as mybir
from concourse.bass_utils import run_bass_kernel_spmd

F32 = mybir.dt.float32
BF16 = mybir.dt.bfloat16
ALU = mybir.AluOpType
AF = mybir.ActivationFunctionType
CDEC = 0.6065306597126334


class Buf:
    __slots__ = ("w", "r", "excl")

    def __init__(self, excl=False):
        self.w = None
        self.r = {}
        self.excl = excl


def bufs(n):
    return [Buf() for _ in range(n)]


def pbufs(n):
    return [Buf(True) for _ in range(n)]


SAME_ENGINE_SYNC = ("pool", "dve", "act", "pe", "sp")


class Queue:
    def __init__(self, fw, eng, name, is_pe=False):
        self.fw = fw
        self.eng = eng
        self.name = name
        self.sem = fw.es.enter_context(fw.nc.semaphore("q_" + name))
        self.n = 0
        self.known = {}
        self.is_pe = is_pe
        self.dma_sems = []
        self.dma_cnt = []
        self.dma_k = 0
        self.same_sync = name in SAME_ENGINE_SYNC
        self.waited = set()
        self.plan = None if fw.plan is None else fw.plan[name]
        self.planset = None if self.plan is None else set(self.plan)

    def add_dma_sems(self, k):
        for i in range(k):
            self.dma_sems.append(self.fw.es.enter_context(self.fw.nc.semaphore("d_%s_%d" % (self.name, i))))
            self.dma_cnt.append(0)

    def _val(self, n):
        if self.plan is None:
            return n
        return bisect.bisect_right(self.plan, n)

    def _wait(self, tok):
        key, n = tok
        if self.known.get(key, 0) >= n:
            return
        self.known[key] = n
        if isinstance(key, Queue):
            key.waited.add(n)
            self.eng.wait_ge(key.sem, key._val(n))
        else:
            self.eng.wait_ge(key, n)

    def _deps(self, reads, writes, skip_same_waw=False):
        deps = {}

        def add(tok):
            if tok is None:
                return
            s, v = tok
            if deps.get(s, 0) < v:
                deps[s] = v
        for b in reads:
            add(b.w)
        for b in writes:
            if not (skip_same_waw and b.w is not None and b.w[0] is self):
                add(b.w)
            for s, v in b.r.items():
                add((s, v))
        if not self.same_sync:
            deps.pop(self, None)
        for s, v in deps.items():
            self._wait((s, v))

    def _record(self, tok, reads, writes):
        s, v = tok
        for b in reads:
            if b.r.get(s, 0) < v:
                b.r[s] = v
        for b in writes:
            b.w = tok
            b.r = {}

    def op(self, fn, reads=(), writes=(), acc=False):
        xreads = [b for b in reads if b.excl]
        if xreads:
            for b in xreads:
                for q, v in b.r.items():
                    if q is not self:
                        self._wait((q, v))
        self._deps(reads, writes, skip_same_waw=(acc and self.is_pe))
        ins = fn(self.eng)
        self.n += 1
        if self.planset is None or self.n in self.planset:
            ins.then_inc(self.sem, 1)
        self._record((self, self.n), reads, writes)
        return ins

    def dma(self, out, in_, reads=(), writes=()):
        self._deps(reads, writes)
        i = self.dma_k % len(self.dma_sems)
        self.dma_k += 1
        s = self.dma_sems[i]
        if self.dma_cnt[i] > 0:
            self._wait((s, 16 * self.dma_cnt[i]))
        self.eng.dma_start(out=out, in_=in_).then_inc(s, 16)
        self.dma_cnt[i] += 1
        self._record((s, 16 * self.dma_cnt[i]), reads, writes)


class FW:
    def __init__(self, nc, plan=None):
        self.nc = nc
        self.plan = plan
        self.es = ExitStack()
        self.pe = Queue(self, nc.tensor, "pe", is_pe=True)
        self.dve = Queue(self, nc.vector, "dve")
        self.act = Queue(self, nc.scalar, "act")
        self.pool = Queue(self, nc.gpsimd, "pool")
        self.sp = Queue(self, nc.sync, "sp")
        self.sp.add_dma_sems(8)
        self.pool.add_dma_sems(8)
        self.queues = [self.pe, self.dve, self.act, self.pool, self.sp]

    def all_tokens(self):
        toks = []
        for q in self.queues:
            if q.n > 0:
                toks.append((q, q.n))
            for s, c in zip(q.dma_sems, q.dma_cnt):
                if c > 0:
                    toks.append((s, 16 * c))
        return toks

    def barrier(self):
        toks = self.all_tokens()
        for q in self.queues:
            for tok in toks:
                q._wait(tok)

    def get_plan(self):
        return {q.name: sorted(q.waited) for q in self.queues}


PV = {}
_o = 0
for _n, _w in [("g0", 8), ("g1", 8), ("w0f", 8), ("w0b", 8), ("a0f", 8), ("a0b", 8), ("kk", 8), ("ka", 8),
               ("rk", 8), ("lnxg", 8), ("lnxb", 8), ("bdw", 8), ("lng", 8), ("lnb", 8), ("mu", 26),
               ("bin", 24), ("dw", 248)]:
    PV[_n] = _o
    _o += _w
NPV = _o


def _build(T, stop=None, plan=None):
    NT = T // 128
    NG = T // 512
    nc = bass.Bass("TRN2", target_bir_lowering=False)
    fw = FW(nc, plan)
    nc._fw = fw
    pe, dve, act, pool, sp = fw.pe, fw.dve, fw.act, fw.pool, fw.sp

    def dram(name, shape, dt, kind="Internal"):
        return nc.dram_tensor(name, shape, dt, kind=kind).ap()

    x_t = dram("x_t", [NT, 128, 1024], F32, "ExternalInput")
    x_h = dram("x_h", [NT, 2, 1024], F32, "ExternalInput")
    smask_d = dram("smask", [128, NT * 2], F32, "ExternalInput")
    cmask_d = dram("cmask", [128, NG * 2], F32, "ExternalInput")
    w_rwin = dram("w_rwin", [128, 8, 4352], F32, "ExternalInput")
    w_rwout = dram("w_rwout", [128, 8, 1024], F32, "ExternalInput")
    w_cvin = dram("w_cvin", [128, 8, 3072], F32, "ExternalInput")
    w_cvout = dram("w_cvout", [128, 8, 1024], F32, "ExternalInput")
    w2a2_d = dram("w2a2", [128, 2, 1024], F32, "ExternalInput")
    pvec_d = dram("pvec", [128, NPV], F32, "ExternalInput")
    bc_d = dram("bc", [128, 2, 1024], F32, "ExternalInput")
    cbf_d = dram("cbf", [128, 5, 512], BF16, "ExternalInput")
    cf32_d = dram("cf32", [128, 3, 128], F32, "ExternalInput")
    y_out = dram("y", [NT, 128, 1024], F32, "ExternalOutput")

    fm_s = dram("fm_s", [NT, 128, 2, 4, 1024], BF16)
    tm_s = dram("tm_s", [NT, 128, 7, 1024], BF16)
    eg_s = dram("eg_s", [NT, 128, 2, 512], F32)
    bon_s = dram("bon_s", [NT, 128, 1024], F32)
    yT_s = dram("yT_s", [2, NT, 128, 1024], F32)
    x1_s = dram("x1_s", [NT, 128, 1024], F32)
    h1T_s = dram("h1T_s", [NT, 128, 8, 128], BF16)
    z_s = dram("z_s", [8, 128, T + 32], BF16)
    sg1_s = dram("sg1_s", [NG, 8, 128, 512], F32)

    es = fw.es

    def sb(st, name, shape, dt):
        return st.enter_context(nc.sbuf_tensor(name, shape, dt))

    def psum(st, name, shape, dt):
        return st.enter_context(nc.psum_tensor(name, shape, dt))

    pvec = sb(es, "pvec_sb", [128, NPV], F32)
    b_pvec = Buf()
    cbf = sb(es, "cbf_sb", [128, 5, 512], BF16)
    b_cbf = Buf()
    cf32 = sb(es, "cf32_sb", [128, 3, 128], F32)
    b_cf = Buf()
    cst = sb(es, "cst", [128, 4], F32)
    b_cst = Buf()
    ones = sb(es, "ones", [128, 128], F32)
    b_ones = Buf()
    smask = sb(es, "smask_sb", [128, NT * 2], F32)
    cmask = sb(es, "cmask_sb", [128, NG * 2], F32)
    b_mask = Buf()
    sp.dma(pvec[:], pvec_d[:, :], writes=[b_pvec])
    sp.dma(cbf[:], cbf_d[:, :, :], writes=[b_cbf])
    sp.dma(cf32[:], cf32_d[:, :, :], writes=[b_cf])
    sp.dma(smask[:], smask_d[:, :], writes=[b_mask])
    sp.dma(cmask[:], cmask_d[:, :], writes=[b_mask])
    dve.op(lambda e: e.memset(cst[:, 0:1], 1e-5), writes=[b_cst])
    dve.op(lambda e: e.memset(cst[:, 1:2], 64e-5), writes=[b_cst])
    dve.op(lambda e: e.memset(cst[:, 2:3], 0.0), writes=[b_cst])
    dve.op(lambda e: e.memset(ones[:], 1.0), writes=[b_ones])
    ident = cbf[:, 0, 0:128]
    BO = cf32[:, 1, :]
    BOm = cf32[:, 0, :]
    O1k = cf32[:, 2, :]

    def pcol(name, j):
        return pvec[:, PV[name] + j: PV[name] + j + 1]

    CONSTS = [b_pvec, b_cbf, b_cf, b_cst, b_ones, b_mask]

    def load_weight_bf16(st, name, src, ncols, gname, dst_stage, b_stage, w=None):
        if w is None:
            w = sb(st, name, [128, 8, ncols], BF16)
        bw = Buf()
        for kc in range(8):
            for c0 in range(0, ncols, 1536):
                c1 = min(ncols, c0 + 1536)
                sp.dma(dst_stage[:, 0:c1 - c0], src[:, kc, c0:c1], writes=[b_stage])
                if gname is None:
                    act.op(lambda e: e.activation(out=w[:, kc, c0:c1], in_=dst_stage[:, 0:c1 - c0], func=AF.Copy),
                           reads=[b_stage], writes=[bw])
                else:
                    act.op(lambda e: e.activation(out=w[:, kc, c0:c1], in_=dst_stage[:, 0:c1 - c0], func=AF.Copy,
                                                  scale=pcol(gname, kc)),
                           reads=[b_stage, b_pvec], writes=[bw])
        return w, bw

    def rmsnorm_rstd(xt, bx, npart, junk, bjunk, ss, bss):
        act.op(lambda e: e.activation(out=junk[0:npart, :], in_=xt[0:npart, :], func=AF.Square,
                                      accum_out=ss[0:npart, 0:1]), reads=[bx], writes=[bjunk, bss])
        act.op(lambda e: e.activation(out=ss[0:npart, 1:2], in_=ss[0:npart, 0:1], func=AF.Ln,
                                      scale=1.0 / 1024.0, bias=cst[0:npart, 0:1]), reads=[bss, b_cst], writes=[bss])
        act.op(lambda e: e.activation(out=ss[0:npart, 2:3], in_=ss[0:npart, 1:2], func=AF.Exp, scale=-0.5),
               reads=[bss], writes=[bss])

    with ExitStack() as ph:
        pro = ExitStack()
        Win = sb(ph, "Win", [128, 8, 3328], BF16)
        w2b = sb(ph, "w2b", [128, 4, 1024], BF16)
        stage = sb(pro, "stageA", [128, 1536], F32)
        b_stage = Buf()
        Win, b_Win = load_weight_bf16(ph, "Win", w_rwin, 3328, "g0", stage, b_stage, w=Win)
        w2f = sb(pro, "w2f", [128, 2, 1024], F32)
        b_w2 = Buf()
        sp.dma(w2f[:], w2a2_d[:, :, :], writes=[b_w2])
        dve.op(lambda e: e.memset(w2b[:], 0.0), writes=[b_w2])
        for wa in range(2):
            for d in range(2):
                dve.op(lambda e: e.tensor_copy(out=w2b[d * 64:(d + 1) * 64, wa * 2 + d, :],
                                               in_=w2f[d * 64:(d + 1) * 64, wa, :]), reads=[b_w2], writes=[b_w2])
        fw.barrier()
        pro.close()
        omm = sb(ph, "omm", [128, 26], F32)
        hmu = sb(ph, "hmu", [128, 26], F32)
        b_mu = Buf()
        mu_ap = pvec[:, PV["mu"]:PV["mu"] + 26]
        dve.op(lambda e: e.tensor_scalar(out=omm[:], in0=mu_ap, scalar1=-1.0, scalar2=1.0, op0=ALU.mult, op1=ALU.add),
               reads=[b_pvec], writes=[b_mu])
        dve.op(lambda e: e.tensor_scalar(out=hmu[:], in0=mu_ap, scalar1=0.5, scalar2=None, op0=ALU.mult),
               reads=[b_pvec], writes=[b_mu])

        X = [sb(ph, "xA0", [128, 1024], F32)]
        bX = bufs(1)
        ss = sb(ph, "ssA", [128, 4], F32)
        bss = Buf()
        hb = sb(ph, "hbA", [128, 1024], BF16)
        bhb = Buf()
        hT = [sb(ph, "hTA%d" % i, [128, 8, 130], BF16) for i in range(3)]
        bhT = bufs(3)
        ptr = psum(ph, "ptrA", [128, 8, 128], BF16)
        bptr = Buf(True)
        pin = [psum(ph, "pinA%d" % i, [128, 512], F32)[:, 0:390].rearrange("p (a b) -> p a b", b=130) for i in range(2)]
        bpin = pbufs(2)
        PL = [psum(ph, "plA%d" % k, [128, 4, 128], F32) for k in range(2)]
        bPL = pbufs(2)
        pss = psum(ph, "pssA", [128, 512], F32)[:, 0:256].rearrange("p (a b) -> p a b", b=128)
        bpss = pbufs(1) * 2
        PTQ = [psum(ph, "ptqA%d" % k, [128, 8, 128], BF16) for k in range(2)]
        bPTQ = pbufs(2)
        ue = [sb(ph, "ueA%d" % i, [128, 3, 130], F32) for i in range(2)]
        bue = bufs(2)
        t1 = [sb(ph, "t1A%d" % i, [128, 3, 128], F32) for i in range(1)] * 2
        bt1 = bufs(1) * 2
        t3 = sb(ph, "t3A", [128, 128], F32)
        bt3 = Buf()
        RKVs = [sb(ph, "rkvA%d" % k, [128, 3, 8, 128], F32) for k in range(2)]
        bRKVs = [[bufs(8) for _ in range(3)] for k in range(2)]
        LOs = [sb(ph, "loA%d" % k, [128, 2, 128], F32) for k in range(2)]
        bLOs = bufs(2)
        twb = sb(ph, "twA", [128, 2, 128], BF16)
        btw = Buf()
        twt = sb(ph, "twtA", [128, 128], F32)
        btwt = Buf()
        FM = [sb(ph, "fmA%d" % i, [128, 2, 4, 1024], BF16) for i in range(1)] * 2
        bFM = bufs(1) * 2
        TMb = [sb(ph, "tmA%d" % i, [128, 7, 1024], BF16) for i in range(1)] * 2
        bTM = bufs(1) * 2
        EG = [sb(ph, "egA%d" % i, [128, 2, 512], F32) for i in range(1)] * 2
        bEG = bufs(1) * 2
        BON = [sb(ph, "bonA%d" % i, [128, 8, 128], F32) for i in range(1)] * 2
        bBON = bufs(1) * 2
        GT = {}
        for nm in ["sgw0", "sgw1", "a0", "a1", "cs0", "cs1", "u0", "u1", "kk", "kk2", "kkn"]:
            for hg in range(2):
                GT[(nm, hg)] = (sb(ph, "g%d_%s" % (hg, nm), [128, 4, 128], F32), bufs(4))
        for nm in ["tmp0", "tmp1", "tmq0", "tmq1", "rk"]:
            GT[nm] = (sb(ph, "g_" + nm, [128, 4, 128], F32), bufs(4))
        SCs = [sb(ph, "g_sc%d" % hg, [128, 4, 8], F32) for hg in range(2)]
        bSCs = [bufs(4) for hg in range(2)]
        VBF = [sb(ph, "vbfA%d" % k, [128, 128], BF16) for k in range(2)]
        bVBF = bufs(2)
        bFMe = bufs(8)
        if stop == "A0":
            fw.barrier()
            return nc

        def H_stage(i):
            xm, bxm = X[0], bX[0]
            sp.dma(xm[:], x_t[i, :, :], writes=[bxm])
            rmsnorm_rstd(xm, bxm, 128, hb, bhb, ss, bss)
            act.op(lambda e: e.activation(out=hb[:], in_=xm[:], func=AF.Copy, scale=ss[:, 2:3]),
                   reads=[bxm, bss], writes=[bhb])
            hTt, bhTt = hT[i % 3], bhT[i % 3]
            for kc in range(8):
                pe.op(lambda e: e.transpose(out=ptr[:, kc, :], in_=hb[:, kc * 128:(kc + 1) * 128], identity=ident),
                      reads=[bhb, b_cbf], writes=[bptr], acc=True)
            dve.op(lambda e: e.tensor_copy(out=hTt[:, :, 1:129], in_=ptr[:]), reads=[bptr], writes=[bhTt])
            if i == 0:
                dve.op(lambda e: e.memset(hTt[:, :, 0:1], 0.0), writes=[bhTt])
            else:
                hTp, bhTp = hT[(i - 1) % 3], bhT[(i - 1) % 3]
                dve.op(lambda e: e.tensor_scalar(out=hTt[:, :, 0:1], in0=hTp[:, :, 128:129],
                                                 scalar1=smask[:, 2 * i + 1:2 * i + 2], scalar2=None, op0=ALU.mult),
                       reads=[bhTp, b_mask], writes=[bhTt])
                dve.op(lambda e: e.tensor_scalar(out=hTp[:, :, 129:130], in0=hTt[:, :, 1:2],
                                                 scalar1=smask[:, 2 * (i - 1):2 * (i - 1) + 1], scalar2=None, op0=ALU.mult),
                       reads=[bhTt, b_mask], writes=[bhTp])
            if i == NT - 1:
                dve.op(lambda e: e.memset(hTt[:, :, 129:130], 0.0), writes=[bhTt])

        def P_gen(i):
            par = i % 2
            hTt, bhTt = hT[i % 3], bhT[i % 3]
            RKV, bRKV, LO, bLO = RKVs[par], bRKVs[par], LOs[par], bLOs[par]
            for bg in range(9):
                pp = bg % 2
                cbs = [cb for cb in range(bg * 3, min(26, bg * 3 + 3))]
                for q, cb in enumerate(cbs):
                    for kc in range(8):
                        pe.op(lambda e: e.matmul(out=pin[pp][:, q, :], lhsT=Win[:, kc, cb * 128:(cb + 1) * 128],
                                                 rhs=hTt[:, kc, :], start=(kc == 0), stop=(kc == 7)),
                              reads=[b_Win, bhTt], writes=[bpin[pp]], acc=True)
                nq = len(cbs)
                act.op(lambda e: e.activation(out=ue[pp][:, 0:nq, :], in_=pin[pp][:, 0:nq, :], func=AF.Copy),
                       reads=[bpin[pp]], writes=[bue[pp]])
                dve.op(lambda e: e.tensor_tensor(out=t1[pp][:, 0:nq, :], in0=ue[pp][:, 0:nq, 0:128],
                                                 in1=ue[pp][:, 0:nq, 2:130], op=ALU.add),
                       reads=[bue[pp]], writes=[bt1[pp]])
                for q, cb in enumerate(cbs):
                    if cb < 24:
                        dst = RKV[:, cb // 8, cb % 8, :]
                        bd = bRKV[cb // 8][cb % 8]
                    else:
                        dst = LO[:, cb - 24, :]
                        bd = bLO
                    dve.op(lambda e: e.tensor_scalar(out=t3[:], in0=t1[pp][:, q, :], scalar1=hmu[:, cb:cb + 1],
                                                     scalar2=None, op0=ALU.mult),
                           reads=[bt1[pp], b_mu], writes=[bt3])
                    dve.op(lambda e: e.scalar_tensor_tensor(out=dst, in0=ue[pp][:, q, 1:129], scalar=omm[:, cb:cb + 1],
                                                            in1=t3[:], op0=ALU.mult, op1=ALU.add),
                           reads=[bue[pp], bt3, b_mu], writes=[bd])
                yield
        def E_pre(i):
            par = i % 2
            RKV, bRKV, LO, bLO = RKVs[par], bRKVs[par], LOs[par], bLOs[par]
            act.op(lambda e: e.activation(out=twt[:], in_=LO[:, 0, :], func=AF.Sigmoid, scale=2.0), reads=[bLO], writes=[btwt])
            pool.op(lambda e: e.tensor_scalar(out=twb[:, 0, :], in0=twt[:], scalar1=2.0, scalar2=-1.0, op0=ALU.mult, op1=ALU.add),
                    reads=[btwt], writes=[btw])
            act.op(lambda e: e.activation(out=twb[:, 1, :], in_=LO[:, 1, :], func=AF.Copy), reads=[bLO], writes=[btw])

            fmt = FM[par]
            tmt, btmt = TMb[par], bTM[par]
            egt, begt = EG[par], bEG[par]
            bont, bbont = BON[par], bBON[par]

        def E_half(i, hg):
            par = i % 2
            RKV, bRKV, LO, bLO = RKVs[par], bRKVs[par], LOs[par], bLOs[par]
            fmt = FM[par]
            tmt, btmt = TMb[par], bTM[par]
            egt, begt = EG[par], bEG[par]
            bont, bbont = BON[par], bBON[par]
            SC, bSC = SCs[hg], bSCs[hg]

            def GTl(nm):
                return GT[(nm, hg)] if (nm, hg) in GT else GT[nm]
            ebs = [4 * hg + e4 for e4 in range(4)]

            def T_(nm, e4):
                return GTl(nm)[0][:, e4, :]

            def B_(nm, e4):
                return GTl(nm)[1][e4]
            for e4, eb in enumerate(ebs):
                plb, bplb = PL[e4 % 2], bPL[e4 % 2]
                es_ = slice(eb * 128, (eb + 1) * 128)
                for slot in range(4):
                    wa = slot // 2
                    pe.op(lambda e: e.matmul(out=plb[:, slot, :], lhsT=w2b[:, slot, es_], rhs=twb[:, wa, :],
                                             start=True, stop=True), reads=[b_w2, btw], writes=[bplb], acc=True)
                for d in range(2):
                    act.op(lambda e: e.activation(out=T_("sgw%d" % d, e4), in_=plb[:, d, :], func=AF.Sigmoid,
                                                  bias=pcol("w0f" if d == 0 else "w0b", eb)),
                           reads=[bplb, b_pvec], writes=[B_("sgw%d" % d, e4)])
                    act.op(lambda e: e.activation(out=T_("a%d" % d, e4), in_=plb[:, 2 + d, :], func=AF.Sigmoid,
                                                  bias=pcol("a0f" if d == 0 else "a0b", eb)),
                           reads=[bplb, b_pvec], writes=[B_("a%d" % d, e4)])
            yield
            for e4, eb in enumerate(ebs):
                act.op(lambda e: e.activation(out=T_("kk", e4), in_=RKV[:, 1, eb, :], func=AF.Copy, scale=pcol("kk", eb)),
                       reads=[bRKV[1][eb], b_pvec], writes=[B_("kk", e4)])
            for e4, eb in enumerate(ebs):
                act.op(lambda e: e.activation(out=T_("kk2", e4), in_=T_("kk", e4), func=AF.Square),
                       reads=[B_("kk", e4)], writes=[B_("kk2", e4)])
            yield
            for e4 in range(4):
                for d in range(2):
                    dve.op(lambda e: e.tensor_tensor_scan(out=T_("cs%d" % d, e4), data0=ones[:], data1=T_("sgw%d" % d, e4),
                                                          initial=0.0, op0=ALU.mult, op1=ALU.add),
                           reads=[B_("sgw%d" % d, e4), b_ones], writes=[B_("cs%d" % d, e4)])
            yield
            for e4 in range(4):
                pe.op(lambda e: e.matmul(out=pss[:, 0, :], lhsT=BO, rhs=T_("kk2", e4), start=True, stop=True),
                      reads=[B_("kk2", e4), b_cf], writes=[bpss[0]])
                dve.op(lambda e: e.tensor_scalar(out=T_("kk2", e4), in0=pss[:, 0, :], scalar1=1e-19, scalar2=None,
                                                 op0=ALU.max), reads=[bpss[0]], writes=[B_("kk2", e4)])
            yield
            for e4, eb in enumerate(ebs):
                dve.op(lambda e: e.scalar_tensor_tensor(out=T_("rk", e4), in0=RKV[:, 0, eb, :], scalar=pcol("rk", eb),
                                                        in1=RKV[:, 1, eb, :], op0=ALU.mult, op1=ALU.mult),
                       reads=[bRKV[0][eb], bRKV[1][eb], b_pvec], writes=[B_("rk", e4)])
            for e4, eb in enumerate(ebs):
                pe.op(lambda e: e.matmul(out=pss[:, 1, :], lhsT=BO, rhs=T_("rk", e4), start=True, stop=True),
                      reads=[B_("rk", e4), b_cf], writes=[bpss[1]])
                dve.op(lambda e: e.tensor_tensor(out=bont[:, eb, :], in0=pss[:, 1, :], in1=RKV[:, 2, eb, :], op=ALU.mult),
                       reads=[bpss[1], bRKV[2][eb]], writes=[bbont])
            yield
            for e4 in range(4):
                for d in range(2):
                    pool.op(lambda e: e.tensor_tensor(out=T_("u%d" % d, e4), in0=T_("cs%d" % d, e4), in1=T_("sgw%d" % d, e4),
                                                      op=ALU.subtract),
                            reads=[B_("cs%d" % d, e4), B_("sgw%d" % d, e4)], writes=[B_("u%d" % d, e4)])
                dve.op(lambda e: e.tensor_scalar(out=SC[:, e4, 0:1], in0=GTl("cs1")[0][:, e4, 127:128], scalar1=-CDEC,
                                                 scalar2=None, op0=ALU.mult), reads=[B_("cs1", e4)], writes=[bSC[e4]])
                dve.op(lambda e: e.tensor_scalar(out=SC[:, e4, 1:2], in0=GTl("cs1")[0][:, e4, 127:128], scalar1=CDEC,
                                                 scalar2=None, op0=ALU.mult), reads=[B_("cs1", e4)], writes=[bSC[e4]])
            yield
            for e4 in range(4):
                act.op(lambda e: e.activation(out=T_("kk2", e4), in_=T_("kk2", e4), func=AF.Ln),
                       reads=[B_("kk2", e4)], writes=[B_("kk2", e4)])
            for e4 in range(4):
                act.op(lambda e: e.activation(out=T_("kk2", e4), in_=T_("kk2", e4), func=AF.Exp, scale=-0.5),
                       reads=[B_("kk2", e4)], writes=[B_("kk2", e4)])
            yield
            for e4 in range(4):
                act.op(lambda e: e.activation(out=SC[:, e4, 2:3], in_=GTl("cs0")[0][:, e4, 127:128], func=AF.Exp, scale=-CDEC),
                       reads=[B_("cs0", e4)], writes=[bSC[e4]])
                act.op(lambda e: e.activation(out=SC[:, e4, 3:4], in_=SC[:, e4, 0:1], func=AF.Exp),
                       reads=[bSC[e4]], writes=[bSC[e4]])
                act.op(lambda e: e.activation(out=T_("sgw0", e4), in_=T_("cs0", e4), func=AF.Exp, scale=-CDEC),
                       reads=[B_("cs0", e4)], writes=[B_("sgw0", e4)])
                act.op(lambda e: e.activation(out=T_("cs0", e4), in_=T_("cs0", e4), func=AF.Exp, scale=CDEC),
                       reads=[B_("cs0", e4)], writes=[B_("cs0", e4)])
                act.op(lambda e: e.activation(out=T_("u0", e4), in_=T_("u0", e4), func=AF.Exp, scale=-CDEC),
                       reads=[B_("u0", e4)], writes=[B_("u0", e4)])
                act.op(lambda e: e.activation(out=T_("sgw1", e4), in_=T_("u1", e4), func=AF.Exp, scale=CDEC,
                                              bias=SC[:, e4, 0:1]),
                       reads=[B_("u1", e4), bSC[e4]], writes=[B_("sgw1", e4)])
                act.op(lambda e: e.activation(out=T_("cs1", e4), in_=T_("cs1", e4), func=AF.Exp, scale=CDEC,
                                              bias=SC[:, e4, 0:1]),
                       reads=[B_("cs1", e4), bSC[e4]], writes=[B_("cs1", e4)])
                act.op(lambda e: e.activation(out=T_("u1", e4), in_=T_("u1", e4), func=AF.Exp, scale=-CDEC,
                                              bias=SC[:, e4, 1:2]),
                       reads=[B_("u1", e4), bSC[e4]], writes=[B_("u1", e4)])
            E1n, E2n, E3n = ("sgw0", "sgw1"), ("cs0", "u1"), ("u0", "cs1")
            yield
            for e4 in range(4):
                dve.op(lambda e: e.tensor_tensor(out=T_("kkn", e4), in0=T_("kk", e4), in1=T_("kk2", e4), op=ALU.mult),
                       reads=[B_("kk", e4), B_("kk2", e4)], writes=[B_("kkn", e4)])
            yield
            for e4, eb in enumerate(ebs):
                for d in range(2):
                    pool.op(lambda e: e.tensor_scalar(out=egt[:, d, eb * 64:(eb + 1) * 64], in0=ones[:, 0:64],
                                                      scalar1=SC[:, e4, 2 + d:3 + d],
                                                      scalar2=smask[:, 2 * i + d:2 * i + d + 1], op0=ALU.mult, op1=ALU.mult),
                            reads=[bSC[e4], b_ones, b_mask], writes=[begt])
            yield
            for e4, eb in enumerate(ebs):
                for d in range(2):
                    dve.op(lambda e: e.tensor_scalar(out=T_("tmp%d" % d, e4), in0=T_("a%d" % d, e4), scalar1=-1.0,
                                                     scalar2=pcol("ka", eb), op0=ALU.add, op1=ALU.mult),
                            reads=[B_("a%d" % d, e4), b_pvec], writes=[B_("tmp%d" % d, e4)])
                    dve.op(lambda e: e.tensor_tensor(out=T_("tmq%d" % d, e4), in0=T_("kkn", e4), in1=T_("a%d" % d, e4),
                                                     op=ALU.mult),
                            reads=[B_("kkn", e4), B_("a%d" % d, e4)], writes=[B_("tmq%d" % d, e4)])
            yield
            def s11(kind, e4, eb, d):
                es_ = slice(eb * 128, (eb + 1) * 128)
                r_ap, k_ap = RKV[:, 0, eb, :], RKV[:, 1, eb, :]
                br, bk = bRKV[0][eb], bRKV[1][eb]
                E1, E2, E3 = E1n[d], E2n[d], E3n[d]
                if kind == 0:
                    dve.op(lambda e: e.scalar_tensor_tensor(out=T_("tmp%d" % d, e4), in0=T_("tmp%d" % d, e4), scalar=1.0,
                                                            in1=k_ap, op0=ALU.add, op1=ALU.mult),
                           reads=[B_("tmp%d" % d, e4), bk], writes=[B_("tmp%d" % d, e4)])
                elif kind == 1:
                    dve.op(lambda e: e.tensor_tensor(out=fmt[:, d, 0, es_], in0=r_ap, in1=T_(E1, e4), op=ALU.mult),
                           reads=[br, B_(E1, e4)], writes=[bFMe[eb]])
                elif kind == 2:
                    dve.op(lambda e: e.scalar_tensor_tensor(out=fmt[:, d, 2, es_], in0=T_("kkn", e4), scalar=-1.0,
                                                            in1=T_(E3, e4), op0=ALU.mult, op1=ALU.mult),
                           reads=[B_("kkn", e4), B_(E3, e4)], writes=[bFMe[eb]])
                elif kind == 3:
                    dve.op(lambda e: e.tensor_tensor(out=fmt[:, d, 3, es_], in0=T_("tmq%d" % d, e4), in1=T_(E2, e4),
                                                     op=ALU.mult),
                           reads=[B_("tmq%d" % d, e4), B_(E2, e4)], writes=[bFMe[eb]])
                else:
                    dve.op(lambda e: e.tensor_tensor(out=fmt[:, d, 1, es_], in0=T_("tmp%d" % d, e4), in1=T_(E2, e4),
                                                     op=ALU.mult),
                           reads=[B_("tmp%d" % d, e4), B_(E2, e4)], writes=[bFMe[eb]])
            for kind in range(5):
                for e4, eb in enumerate(ebs):
                    for d in range(2):
                        s11(kind, e4, eb, d)
            yield
            for e4, eb in enumerate(ebs):
                es_ = slice(eb * 128, (eb + 1) * 128)
                vb, bvb = VBF[e4 % 2], bVBF[e4 % 2]
                act.op(lambda e: e.activation(out=vb[:], in_=RKV[:, 2, eb, :], func=AF.Copy),
                       reads=[bRKV[2][eb]], writes=[bvb])
                srcs = [fmt[:, 0, 1, es_], fmt[:, 0, 3, es_], fmt[:, 0, 2, es_],
                        fmt[:, 1, 1, es_], fmt[:, 1, 3, es_], fmt[:, 1, 2, es_], vb[:]]
                ptq, bptq = PTQ[e4 % 2], bPTQ[e4 % 2]
                for qi, s_ap in enumerate(srcs):
                    pe.op(lambda e: e.transpose(out=ptq[:, qi, :], in_=s_ap, identity=ident),
                          reads=[bFMe[eb], bvb, b_cbf], writes=[bptq], acc=True)
                act.op(lambda e: e.activation(out=tmt[:, :, es_], in_=ptq[:, 0:7, :], func=AF.Copy),
                       reads=[bptq], writes=[btmt])

        def E_post(i):
            par = i % 2
            fmt = FM[par]
            tmt, btmt = TMb[par], bTM[par]
            egt, begt = EG[par], bEG[par]
            bont, bbont = BON[par], bBON[par]
            pool.dma(fm_s[i, :, :, :, :], fmt[:], reads=bFMe)
            pool.dma(tm_s[i, :, :, :], tmt[:], reads=[btmt])
            pool.dma(eg_s[i, :, :, :], egt[:], reads=[begt])
            pool.dma(bon_s[i, :, :], bont[:].rearrange("p a b -> p (a b)"), reads=[bbont])

        def run_interleaved(gens):
            gens = [g for g in gens if g is not None]
            while gens:
                for g in list(gens):
                    try:
                        next(g)
                    except StopIteration:
                        gens.remove(g)

        def delayed(gen, n):
            for _ in range(n):
                yield
            yield from gen

        def E_all(i):
            E_pre(i)
            return [E_half(i, 0), delayed(E_half(i, 1), 4)]

        H_stage(0)
        for i in range(NT):
            if i + 1 < NT:
                H_stage(i + 1)
            gens = [P_gen(i)]
            if i > 0:
                gens += E_all(i - 1)
            run_interleaved(gens)
            if i > 0:
                E_post(i - 1)
        run_interleaved(E_all(NT - 1))
        E_post(NT - 1)
        fw.barrier()
    if stop == "A":
        return nc

    with ExitStack() as ph:
        I4, SU4, U4, SL4, L4 = [cbf[:, m, :].rearrange("p (a b) -> p a b", b=128) for m in range(5)]
        banks = [psum(ph, "bkB%d" % i, [128, 512], F32) for i in range(8)]
        bbank = pbufs(8)
        bank_ctr = [0]

        def next_bank():
            k = bank_ctr[0] % 8
            bank_ctr[0] += 1
            return banks[k], bbank[k]

        FMd = [sb(ph, "fmB%d" % i, [128, 4, 1024], BF16) for i in range(2)]
        bFMd = bufs(2)
        TMd = [sb(ph, "tmB%d" % i, [128, 3, 1024], BF16) for i in range(2)]
        bTMd = bufs(2)
        TMv = [sb(ph, "tvB%d" % i, [128, 1024], BF16) for i in range(2)]
        bTMv = bufs(2)
        EGd = [sb(ph, "egB%d" % i, [128, 512], F32) for i in range(2)]
        bEGd = bufs(2)

        def mat16(name):
            return sb(ph, name, [128, 16, 128], BF16), bufs(4)
        Pm, bP = mat16("PmB")
        PTm, bPT = mat16("PTmB")
        P2m, bP2 = mat16("P2mB")
        PT2m, bPT2 = mat16("PT2mB")
        Xm, bXm = mat16("XmB")
        XTm, bXTm = mat16("XTmB")
        Aak, bAak = mat16("AakB")
        Arb, bArb = mat16("ArbB")
        Ark, bArk = mat16("ArkB")
        M1 = sb(ph, "M1B", [128, 16, 64], BF16)
        bM1 = bufs(2)
        AtT = sb(ph, "AtTB", [128, 8, 128], BF16)
        bAtT = bufs(2)
        Usb = sb(ph, "UsbB", [128, 16, 64], BF16)
        bUsb = bufs(2)
        h32 = sb(ph, "h32B", [128, 512], F32)
        hbf = sb(ph, "hbfB", [128, 8, 64], BF16)
        bh = Buf()
        OT = [sb(ph, "OTB%d" % i, [128, 8, 128], F32) for i in range(2)]
        bOT = bufs(2)

        def load(step, i, d):
            p = step % 2
            sp.dma(FMd[p][:], fm_s[i, :, d, :, :], writes=[bFMd[p]])
            sp.dma(TMd[p][:], tm_s[i, :, 3 * d:3 * d + 3, :], writes=[bTMd[p]])
            sp.dma(TMv[p][:], tm_s[i, :, 6, :], writes=[bTMv[p]])
            sp.dma(EGd[p][:], eg_s[i, :, d, :], writes=[bEGd[p]])

        step = 0
        order = [(i, 0) for i in range(NT)] + [(i, 1) for i in range(NT - 1, -1, -1)]
        load(0, order[0][0], order[0][1])
        for idx, (i, d) in enumerate(order):
            p = step % 2
            if idx + 1 < len(order):
                load(step + 1, order[idx + 1][0], order[idx + 1][1])
            if idx == 0 or idx == NT:
                dve.op(lambda e: e.memset(h32[:], 0.0), writes=[bh])
                dve.op(lambda e: e.memset(hbf[:], 0.0), writes=[bh])
            fm, bfm, tmd, btmd, tv, btv, eg, beg = FMd[p], bFMd[p], TMd[p], bTMd[p], TMv[p], bTMv[p], EGd[p], bEGd[p]
            m_su, m_u, m_sl = (SU4, U4, SL4) if d == 0 else (SL4, L4, SU4)

            def fmh(q, h):
                eb, j = h // 2, h % 2
                return fm[j * 64:(j + 1) * 64, q, eb * 128:(eb + 1) * 128]

            def tmh(q, h):
                return tmd[:, q, h * 64:(h + 1) * 64]

            def vh(h):
                return tv[:, h * 64:(h + 1) * 64]

            def prod(lq, rq, mask, dst, bdst):
                for hb8 in range(2):
                    bkj = [next_bank(), next_bank()]
                    for e4 in range(4):
                        for j in range(2):
                            h = hb8 * 8 + e4 * 2 + j
                            bk, bbk = bkj[j]
                            pe.op(lambda e: e.matmul(out=bk[:, e4 * 128:(e4 + 1) * 128], lhsT=fmh(lq, h), rhs=fmh(rq, h),
                                                     start=True, stop=True), reads=[bfm], writes=[bbk], acc=True)
                    for j in range(2):
                        bk, bbk = bkj[j]
                        dve.op(lambda e: e.tensor_tensor(out=dst[:, hb8 * 8 + j:hb8 * 8 + 8:2, :],
                                                         in0=bk[:].rearrange("p (a b) -> p a b", b=128), in1=mask, op=ALU.mult),
                               reads=[bbk, b_cbf], writes=[bdst[hb8 * 2], bdst[hb8 * 2 + 1]])
            prod(3, 2, m_su, Pm, bP)
            prod(2, 3, m_sl, PTm, bPT)
            prod(1, 2, m_su, Aak, bAak)
            prod(3, 0, m_u, Arb, bArb)
            prod(1, 0, m_u, Ark, bArk)
            for hbk in range(4):
                hs = slice(hbk * 4, (hbk + 1) * 4)
                dve.op(lambda e: e.tensor_tensor(out=Xm[:, hs, :], in0=Pm[:, hs, :], in1=I4, op=ALU.add),
                       reads=[bP[hbk], b_cbf], writes=[bXm[hbk]])
                dve.op(lambda e: e.tensor_tensor(out=XTm[:, hs, :], in0=PTm[:, hs, :], in1=I4, op=ALU.add),
                       reads=[bPT[hbk], b_cbf], writes=[bXTm[hbk]])
            cur = (Pm, bP, PTm, bPT)
            nxt = (P2m, bP2, PT2m, bPT2)
            for lev in range(6):
                Pc, bPc, PTc, bPTc = cur
                Pn, bPn, PTn, bPTn = nxt
                last = (lev == 5)

                def mm_batch(lhs, blhs, rhs, brhs, evac):
                    for hbk in range(4):
                        bk, bbk = next_bank()
                        for hh in range(4):
                            h = hbk * 4 + hh
                            pe.op(lambda e: e.matmul(out=bk[:, hh * 128:(hh + 1) * 128], lhsT=lhs[:, h, :], rhs=rhs[:, h, :],
                                                     start=True, stop=True),
                                  reads=[blhs[hbk], brhs[hbk]], writes=[bbk], acc=True)
                        evac(hbk, bk, bbk)

                def ev_copy(dst, bdst):
                    def f(hbk, bk, bbk):
                        act.op(lambda e: e.activation(out=dst[:, hbk * 4:(hbk + 1) * 4, :],
                                                      in_=bk[:].rearrange("p (a b) -> p a b", b=128), func=AF.Copy),
                               reads=[bbk], writes=[bdst[hbk]])
                    return f

                def ev_add(dst, bdst):
                    def f(hbk, bk, bbk):
                        hs = slice(hbk * 4, (hbk + 1) * 4)
                        dve.op(lambda e: e.tensor_tensor(out=dst[:, hs, :], in0=bk[:].rearrange("p (a b) -> p a b", b=128),
                                                         in1=dst[:, hs, :], op=ALU.add),
                               reads=[bbk, bdst[hbk]], writes=[bdst[hbk]])
                    return f
                mm_batch(PTc, bPTc, Pc, bPc, ev_copy(Pn, bPn))
                if not last:
                    mm_batch(Pc, bPc, PTc, bPTc, ev_copy(PTn, bPTn))
                mm_batch(XTm, bXTm, Pn, bPn, ev_add(Xm, bXm))
                if not last:
                    mm_batch(Pn, bPn, XTm, bXTm, ev_add(XTm, bXTm))
                cur, nxt = nxt, cur
            for g8 in range(2):
                bk, bbk = next_bank()
                for hh in range(8):
                    h = g8 * 8 + hh
                    pe.op(lambda e: e.matmul(out=bk[:, hh * 64:(hh + 1) * 64], lhsT=Aak[:, h, :], rhs=vh(h),
                                             start=True, stop=True), reads=[bAak[h // 4], btv], writes=[bbk], acc=True)
                act.op(lambda e: e.activation(out=M1[:, g8 * 8:(g8 + 1) * 8, :],
                                              in_=bk[:].rearrange("p (a b) -> p a b", b=64), func=AF.Copy),
                       reads=[bbk], writes=[bM1[g8]])
            for g4 in range(2):
                bk, bbk = next_bank()
                for e4 in range(4):
                    eb = g4 * 4 + e4
                    for j in range(2):
                        h = eb * 2 + j
                        pe.op(lambda e: e.matmul(out=bk[j * 64:(j + 1) * 64, e4 * 128:(e4 + 1) * 128], lhsT=tmh(2, h),
                                                 rhs=Xm[:, h, :], start=True, stop=True),
                              reads=[btmd, bXm[h // 4]], writes=[bbk], acc=True)
                act.op(lambda e: e.activation(out=AtT[:, g4 * 4:(g4 + 1) * 4, :],
                                              in_=bk[:].rearrange("p (a b) -> p a b", b=128), func=AF.Copy),
                       reads=[bbk], writes=[bAtT[g4]])
            for g8 in range(2):
                bk, bbk = next_bank()
                for hh in range(8):
                    h = g8 * 8 + hh
                    eb, j = h // 2, h % 2
                    js = slice(j * 64, (j + 1) * 64)
                    pe.op(lambda e: e.matmul(out=bk[:, hh * 64:(hh + 1) * 64], lhsT=AtT[js, eb, :], rhs=hbf[js, eb, :],
                                             start=True, stop=False), reads=[bAtT[eb // 4], bh], writes=[bbk], acc=True)
                    pe.op(lambda e: e.matmul(out=bk[:, hh * 64:(hh + 1) * 64], lhsT=Xm[:, h, :], rhs=M1[:, h, :],
                                             start=False, stop=True), reads=[bXm[h // 4], bM1[g8]], writes=[bbk], acc=True)
                dve.op(lambda e: e.tensor_copy(out=Usb[:, g8 * 8:(g8 + 1) * 8, :],
                                               in_=bk[:].rearrange("p (a b) -> p a b", b=64)),
                       reads=[bbk], writes=[bUsb[g8]])
            ot, bot = OT[p], bOT[p]
            for g4 in range(2):
                bk, bbk = next_bank()
                for e4 in range(4):
                    eb = g4 * 4 + e4
                    for j in range(2):
                        h = eb * 2 + j
                        js = slice(j * 64, (j + 1) * 64)
                        o_ap = bk[js, e4 * 128:(e4 + 1) * 128]
                        pe.op(lambda e: e.matmul(out=o_ap, lhsT=hbf[js, eb, :], rhs=fmh(0, h), start=True, stop=False),
                              reads=[bh, bfm], writes=[bbk], acc=True)
                        pe.op(lambda e: e.matmul(out=o_ap, lhsT=Usb[:, h, :], rhs=Arb[:, h, :], start=False, stop=False),
                              reads=[bUsb[h // 8], bArb[h // 4]], writes=[bbk], acc=True)
                        pe.op(lambda e: e.matmul(out=o_ap, lhsT=vh(h), rhs=Ark[:, h, :], start=False, stop=True),
                              reads=[btv, bArk[h // 4]], writes=[bbk], acc=True)
                act.op(lambda e: e.activation(out=ot[:, g4 * 4:(g4 + 1) * 4, :],
                                              in_=bk[:].rearrange("p (a b) -> p a b", b=128), func=AF.Copy),
                       reads=[bbk], writes=[bot])
            pool.dma(yT_s[d, i, :, :], ot[:].rearrange("p a b -> p (a b)"), reads=[bot])
            bk, bbk = next_bank()
            for eb in range(8):
                for j in range(2):
                    h = eb * 2 + j
                    js = slice(j * 64, (j + 1) * 64)
                    o_ap = bk[js, eb * 64:(eb + 1) * 64]
                    pe.op(lambda e: e.matmul(out=o_ap, lhsT=tmh(1, h), rhs=Usb[:, h, :], start=True, stop=False),
                          reads=[btmd, bUsb[h // 8]], writes=[bbk], acc=True)
                    pe.op(lambda e: e.matmul(out=o_ap, lhsT=tmh(0, h), rhs=vh(h), start=False, stop=True),
                          reads=[btmd, btv], writes=[bbk], acc=True)
            dve.op(lambda e: e.tensor_tensor(out=h32[:], in0=bk[:], in1=h32[:], op=ALU.add), reads=[bbk, bh], writes=[bh])
            dve.op(lambda e: e.tensor_tensor(out=h32[:], in0=h32[:], in1=eg[:], op=ALU.mult), reads=[bh, beg], writes=[bh])
            act.op(lambda e: e.activation(out=hbf[:].rearrange("p a b -> p (a b)"), in_=h32[:], func=AF.Copy),
                   reads=[bh], writes=[bh])
            step += 1
        fw.barrier()
    if stop == "B":
        return nc

    with ExitStack() as ph:
        stage = sb(ph, "stageC", [128, 1536], F32)
        b_stage = Buf()
        Wout, b_Wout = load_weight_bf16(ph, "Wout", w_rwout, 1024, None, stage, b_stage)
        Wg, b_Wg = load_weight_bf16(ph, "Wg", w_rwin[:, :, 3328:4352], 1024, "g0", stage, b_stage)
        YF = [sb(ph, "yfC%d" % i, [128, 8, 128], F32) for i in range(2)]
        YB = [sb(ph, "ybC%d" % i, [128, 8, 128], F32) for i in range(2)]
        BN = [sb(ph, "bnC%d" % i, [128, 8, 128], F32) for i in range(2)]
        XC = [sb(ph, "xC%d" % i, [128, 1024], F32) for i in range(2)]
        bIN = bufs(2)
        yg = [sb(ph, "ygC%d" % i, [128, 8, 128], BF16) for i in range(2)]
        byg = bufs(2)
        pout = [psum(ph, "poutC%d" % i, [128, 512], F32) for i in range(2)]
        bpout = pbufs(2)
        ptr = psum(ph, "ptrC", [128, 8, 128], BF16)
        bptr = Buf(True)
        ptrx = psum(ph, "ptrxC", [128, 8, 128], BF16)
        bptrx = Buf(True)
        pg = [psum(ph, "pgC%d" % k, [128, 4, 128], F32) for k in range(2)]
        bpg = pbufs(2)
        x1 = [sb(ph, "x1C%d" % i, [128, 1024], F32) for i in range(2)]
        bx1 = bufs(2)
        junk = sb(ph, "junkC", [128, 1024], BF16)
        bjunk = Buf()
        ss = sb(ph, "ssC", [128, 4], F32)
        bss = Buf()
        hb = sb(ph, "hbC", [128, 1024], BF16)
        bhb = Buf()
        junkx = sb(ph, "junkxC", [128, 1024], BF16)
        bjunkx = Buf()
        ssx = sb(ph, "ssxC", [128, 4], F32)
        bssx = Buf()
        hbx = sb(ph, "hbxC", [128, 1024], BF16)
        bhbx = Buf()
        hTx = sb(ph, "hTxC", [128, 8, 128], BF16)
        bhTx = Buf()
        SGt = sb(ph, "SGtC", [128, 8, 128], F32)
        bSGt = bufs(2)
        h1T = [sb(ph, "h1TC%d" % i, [128, 8, 128], BF16) for i in range(2)]
        bh1T = bufs(2)
        Yw = sb(ph, "YwC", [128, 8, 128], F32)
        bYw = bufs(2)
        SQ = sb(ph, "SQC", [128, 8, 128], F32)
        bSQ = bufs(2)
        pm = [psum(ph, "pmC%d" % k, [128, 4, 128], F32) for k in range(2)]
        bpm = pbufs(2)

        def loadC(i):
            p = i % 2
            sp.dma(YF[p][:].rearrange("p a b -> p (a b)"), yT_s[0, i, :, :], writes=[bIN[p]])
            sp.dma(YB[p][:].rearrange("p a b -> p (a b)"), yT_s[1, i, :, :], writes=[bIN[p]])
            sp.dma(BN[p][:].rearrange("p a b -> p (a b)"), bon_s[i, :, :], writes=[bIN[p]])
            sp.dma(XC[p][:], x_t[i, :, :], writes=[bIN[p]])

        def Gg_gen(i):
            p = i % 2
            bin_ = bIN[p]
            rmsnorm_rstd(XC[p], bin_, 128, junkx, bjunkx, ssx, bssx)
            act.op(lambda e: e.activation(out=hbx[:], in_=XC[p][:], func=AF.Copy, scale=ssx[:, 2:3]),
                   reads=[bin_, bssx], writes=[bhbx])
            for kc in range(8):
                pe.op(lambda e: e.transpose(out=ptrx[:, kc, :], in_=hbx[:, kc * 128:(kc + 1) * 128], identity=ident),
                      reads=[bhbx, b_cbf], writes=[bptrx], acc=True)
            dve.op(lambda e: e.tensor_copy(out=hTx[:], in_=ptrx[:]), reads=[bptrx], writes=[bhTx])
            yield
            for hf in range(2):
                hs = slice(hf * 4, (hf + 1) * 4)
                for e4 in range(4):
                    cb = hf * 4 + e4
                    for kc in range(8):
                        pe.op(lambda e: e.matmul(out=pg[hf][:, e4, :], lhsT=Wg[:, kc, cb * 128:(cb + 1) * 128],
                                                 rhs=hTx[:, kc, :], start=(kc == 0), stop=(kc == 7)),
                              reads=[b_Wg, bhTx], writes=[bpg[hf]], acc=True)
                act.op(lambda e: e.activation(out=SGt[:, hs, :], in_=pg[hf][:], func=AF.Sigmoid),
                       reads=[bpg[hf]], writes=[bSGt[hf]])
                dve.op(lambda e: e.tensor_tensor(out=SGt[:, hs, :], in0=pg[hf][:], in1=SGt[:, hs, :], op=ALU.mult),
                       reads=[bpg[hf], bSGt[hf]], writes=[bSGt[hf]])
                yield

        def Gn_gen(i):
            p = i % 2
            bin_ = bIN[p]
            for hf in range(2):
                hs = slice(hf * 4, (hf + 1) * 4)
                dve.op(lambda e: e.tensor_tensor(out=Yw[:, hs, :], in0=YF[p][:, hs, :], in1=YB[p][:, hs, :], op=ALU.add),
                       reads=[bin_], writes=[bYw[hf]])
                for e4 in range(4):
                    pe.op(lambda e: e.matmul(out=pm[hf][:, e4, :], lhsT=BOm, rhs=Yw[:, hf * 4 + e4, :], start=True, stop=True),
                          reads=[bYw[hf], b_cf], writes=[bpm[hf]], acc=True)
                dve.op(lambda e: e.tensor_tensor(out=Yw[:, hs, :], in0=Yw[:, hs, :], in1=pm[hf][:], op=ALU.subtract),
                       reads=[bYw[hf], bpm[hf]], writes=[bYw[hf]])
                act.op(lambda e: e.activation(out=SQ[:, hs, :], in_=Yw[:, hs, :], func=AF.Square),
                       reads=[bYw[hf]], writes=[bSQ[hf]])
                yield
                for e4 in range(4):
                    pe.op(lambda e: e.matmul(out=pm[hf][:, e4, :], lhsT=BOm, rhs=SQ[:, hf * 4 + e4, :], start=True, stop=True),
                          reads=[bSQ[hf], b_cf], writes=[bpm[hf]], acc=True)
                act.op(lambda e: e.activation(out=SQ[:, hs, :], in_=pm[hf][:], func=AF.Ln, bias=cst[:, 1:2]),
                       reads=[bpm[hf], b_cst], writes=[bSQ[hf]])
                act.op(lambda e: e.activation(out=SQ[:, hs, :], in_=SQ[:, hs, :], func=AF.Exp, scale=-0.5),
                       reads=[bSQ[hf]], writes=[bSQ[hf]])
                dve.op(lambda e: e.tensor_tensor(out=Yw[:, hs, :], in0=Yw[:, hs, :], in1=SQ[:, hs, :], op=ALU.mult),
                       reads=[bYw[hf], bSQ[hf]], writes=[bYw[hf]])
                yield
                for e4 in range(4):
                    eb = hf * 4 + e4
                    act.op(lambda e: e.activation(out=Yw[:, eb, :], in_=Yw[:, eb, :], func=AF.Identity,
                                                  scale=pcol("lnxg", eb), bias=pcol("lnxb", eb)),
                           reads=[bYw[hf], b_pvec], writes=[bYw[hf]])
                dve.op(lambda e: e.tensor_tensor(out=Yw[:, hs, :], in0=Yw[:, hs, :], in1=BN[p][:, hs, :], op=ALU.add),
                       reads=[bYw[hf], bin_], writes=[bYw[hf]])
                dve.op(lambda e: e.tensor_tensor(out=yg[p][:, hs, :], in0=Yw[:, hs, :], in1=SGt[:, hs, :], op=ALU.mult),
                       reads=[bYw[hf], bSGt[hf]], writes=[byg[p]])
                yield

        def O_gen(i):
            p = i % 2
            bin_ = bIN[p]
            for half in range(2):
                for eb in range(8):
                    pe.op(lambda e: e.matmul(out=pout[half][:], lhsT=yg[p][:, eb, :], rhs=Wout[:, eb, half * 512:(half + 1) * 512],
                                             start=(eb == 0), stop=(eb == 7)),
                          reads=[byg[p], b_Wout], writes=[bpout[half]], acc=True)
                dve.op(lambda e: e.tensor_tensor(out=x1[p][:, half * 512:(half + 1) * 512], in0=pout[half][:],
                                                 in1=XC[p][:, half * 512:(half + 1) * 512], op=ALU.add),
                       reads=[bpout[half], bin_], writes=[bx1[p]])
                yield
            pool.dma(x1_s[i, :, :], x1[p][:], reads=[bx1[p]])
            rmsnorm_rstd(x1[p], bx1[p], 128, junk, bjunk, ss, bss)
            act.op(lambda e: e.activation(out=hb[:], in_=x1[p][:], func=AF.Copy, scale=ss[:, 2:3]),
                   reads=[bx1[p], bss], writes=[bhb])
            yield
            for kc in range(8):
                pe.op(lambda e: e.transpose(out=ptr[:, kc, :], in_=hb[:, kc * 128:(kc + 1) * 128], identity=ident),
                      reads=[bhb, b_cbf], writes=[bptr], acc=True)
            dve.op(lambda e: e.tensor_copy(out=h1T[p][:], in_=ptr[:]), reads=[bptr], writes=[bh1T[p]])
            pool.dma(h1T_s[i, :, :, :], h1T[p][:], reads=[bh1T[p]])

        def run_il(gens):
            gens = [g for g in gens if g is not None]
            while gens:
                for g in list(gens):
                    try:
                        next(g)
                    except StopIteration:
                        gens.remove(g)

        loadC(0)
        if NT > 1:
            loadC(1)
        run_il([Gg_gen(0), Gn_gen(0)])
        for i in range(NT):
            if i + 2 < NT:
                pass
            run_il([O_gen(i), Gg_gen(i + 1) if i + 1 < NT else None, Gn_gen(i + 1) if i + 1 < NT else None])
            if i + 2 < NT:
                loadC(i + 2)
        fw.barrier()
    if stop == "C":
        return nc

    with ExitStack() as ph:
        stage = sb(ph, "stageD", [128, 1536], F32)
        b_stage = Buf()
        Wc, b_Wc = load_weight_bf16(ph, "Wc", w_cvin, 3072, "g1", stage, b_stage)
        H1 = [sb(ph, "H1D%d" % i, [128, 8, 512], BF16) for i in range(2)]
        bH1 = bufs(2)
        pb = [psum(ph, "pbD%d" % i, [128, 512], F32) for i in range(6)]
        bpb = pbufs(6)
        sgl = [sb(ph, "sglD%d" % i, [128, 512], F32) for i in range(2)]
        bsgl = bufs(2)
        zt = [sb(ph, "ztD%d" % i, [128, 512], BF16) for i in range(2)]
        bzt = bufs(2)
        sg1 = [sb(ph, "sg1D%d" % i, [128, 512], F32) for i in range(2)]
        bsg1 = bufs(2)
        zero = sb(ph, "zeroD", [128, 16], BF16)
        bzero = Buf()
        dve.op(lambda e: e.memset(zero[:], 0.0), writes=[bzero])
        for cbk in range(8):
            pool.dma(z_s[cbk, :, 0:16], zero[:, 0:16], reads=[bzero])
            pool.dma(z_s[cbk, :, T + 16:T + 32], zero[:, 0:16], reads=[bzero])

        def loadD(g):
            p = g % 2
            for tt in range(4):
                sp.dma(H1[p][:, :, tt * 128:(tt + 1) * 128], h1T_s[4 * g + tt, :, :, :], writes=[bH1[p]])
        loadD(0)
        cnt = 0
        for g in range(NG):
            p = g % 2
            if g + 1 < NG:
                loadD(g + 1)
            for cbk in range(8):
                q = cnt % 2
                cnt += 1
                pbs = [pb[q * 3 + k] for k in range(3)]
                bpbs = [bpb[q * 3 + k] for k in range(3)]
                for k in range(3):
                    c0 = k * 1024 + cbk * 128
                    for kc in range(8):
                        pe.op(lambda e: e.matmul(out=pbs[k][:], lhsT=Wc[:, kc, c0:c0 + 128], rhs=H1[p][:, kc, :],
                                                 start=(kc == 0), stop=(kc == 7)),
                              reads=[b_Wc, bH1[p]], writes=[bpbs[k]], acc=True)
                act.op(lambda e: e.activation(out=sgl[q][:], in_=pbs[1][:], func=AF.Sigmoid, bias=pcol("bin", 8 + cbk)),
                       reads=[bpbs[1], b_pvec], writes=[bsgl[q]])
                dve.op(lambda e: e.scalar_tensor_tensor(out=zt[q][:], in0=pbs[0][:], scalar=pcol("bin", cbk), in1=sgl[q][:],
                                                        op0=ALU.add, op1=ALU.mult),
                       reads=[bpbs[0], bsgl[q], b_pvec], writes=[bzt[q]])
                pool.dma(z_s[cbk, :, 16 + g * 512:16 + (g + 1) * 512], zt[q][:], reads=[bzt[q]])
                act.op(lambda e: e.activation(out=sg1[q][:], in_=pbs[2][:], func=AF.Sigmoid, bias=pcol("bin", 16 + cbk)),
                       reads=[bpbs[2], b_pvec], writes=[bsg1[q]])
                dve.op(lambda e: e.scalar_tensor_tensor(out=sg1[q][:], in0=pbs[2][:], scalar=pcol("bin", 16 + cbk), in1=sg1[q][:],
                                                        op0=ALU.add, op1=ALU.mult),
                       reads=[bpbs[2], bsg1[q], b_pvec], writes=[bsg1[q]])
                pool.dma(sg1_s[g, cbk, :, :], sg1[q][:], reads=[bsg1[q]])
        fw.barrier()
    if stop == "D1":
        return nc

    with ExitStack() as ph:
        stage = sb(ph, "stageE", [128, 1536], F32)
        b_stage = Buf()
        Wo, b_Wo = load_weight_bf16(ph, "Wo", w_cvout, 1024, None, stage, b_stage)
        bcs = sb(ph, "bcsE", [128, 2, 1024], F32)
        b_bc = Buf()
        sp.dma(bcs[:], bc_d[:, :, :], writes=[b_bc])
        zw = [sb(ph, "zwE%d" % i, [128, 542], BF16) for i in range(3)]
        bzw = bufs(3)
        zc = sb(ph, "zcE", [128, 8, 512], F32)
        bzc = bufs(8)
        sq = [sb(ph, "sqE%d" % i, [128, 512], F32) for i in range(2)]
        bsq = bufs(2)
        DG = sb(ph, "DGE", [128, 248, 128], BF16)
        b_DG = Buf()
        for idx in range(248):
            dve.op(lambda e: e.tensor_scalar(out=DG[:, idx, :], in0=ident, scalar1=pvec[:, PV["dw"] + idx:PV["dw"] + idx + 1],
                                             scalar2=None, op0=ALU.mult), reads=[b_cbf, b_pvec], writes=[b_DG])
        pconv = [psum(ph, "pconvE%d" % i, [128, 512], F32) for i in range(2)]
        bpconv = pbufs(2)
        pmean = psum(ph, "pmeanE", [128, 512], F32)
        bpmean = Buf(True)
        pvar = psum(ph, "pvarE", [128, 512], F32)
        bpvar = Buf(True)
        rs = sb(ph, "rsE", [128, 512], F32)
        brs = Buf()
        sg1 = [sb(ph, "sg1E%d" % i, [128, 512], F32) for i in range(2)]
        bsg1 = bufs(2)
        s1 = [sb(ph, "s1E%d" % i, [128, 512], F32) for i in range(2)]
        bs1 = bufs(2)
        ZF = sb(ph, "ZFE", [128, 8, 512], BF16)
        bZF = Buf()
        pout = [psum(ph, "poutE%d" % i, [128, 512], F32) for i in range(4)]
        bpout = pbufs(4)
        x1 = [sb(ph, "x1E%d" % i, [128, 1024], F32) for i in range(2)]
        bx1 = bufs(2)
        x2 = [sb(ph, "x2E%d" % i, [128, 1024], F32) for i in range(2)]
        bx2 = bufs(2)
        yo = [sb(ph, "yoE%d" % i, [128, 1024], F32) for i in range(2)]
        byo = bufs(2)
        junk = sb(ph, "junkE", [128, 1024], F32)
        bjunk = Buf()
        ss = sb(ph, "ssE", [128, 4], F32)
        bss = Buf()
        lc = 0
        oc = 0
        for g in range(NG):
            for cbk in range(8):
                q = lc % 3
                lc += 1
                sp.dma(zw[q][:], z_s[cbk, :, g * 512 + 1:g * 512 + 543], writes=[bzw[q]])
                dve.op(lambda e: e.tensor_scalar(out=zw[q][:, 0:15], in0=zw[q][:, 0:15], scalar1=cmask[:, 2 * g:2 * g + 1],
                                                 scalar2=None, op0=ALU.mult), reads=[bzw[q], b_mask], writes=[bzw[q]])
                dve.op(lambda e: e.tensor_scalar(out=zw[q][:, 527:542], in0=zw[q][:, 527:542],
                                                 scalar1=cmask[:, 2 * g + 1:2 * g + 2], scalar2=None, op0=ALU.mult),
                       reads=[bzw[q], b_mask], writes=[bzw[q]])
                pc, bpc = pconv[cbk % 2], bpconv[cbk % 2]
                for j in range(31):
                    pe.op(lambda e: e.matmul(out=pc[:], lhsT=DG[:, cbk * 31 + j, :], rhs=zw[q][:, j:j + 512],
                                             start=(j == 0), stop=(j == 30)),
                          reads=[b_DG, bzw[q]], writes=[bpc], acc=True)
                act.op(lambda e: e.activation(out=zc[:, cbk, :], in_=pc[:], func=AF.Identity, bias=pcol("bdw", cbk)),
                       reads=[bpc, b_pvec], writes=[bzc[cbk]])
                pe.op(lambda e: e.matmul(out=pmean[:], lhsT=O1k, rhs=zc[:, cbk, :], start=(cbk == 0), stop=(cbk == 7)),
                      reads=[bzc[cbk], b_cf], writes=[bpmean], acc=True)
            for cbk in range(8):
                q = cbk % 2
                dve.op(lambda e: e.tensor_tensor(out=zc[:, cbk, :], in0=zc[:, cbk, :], in1=pmean[:], op=ALU.subtract),
                       reads=[bzc[cbk], bpmean], writes=[bzc[cbk]])
                act.op(lambda e: e.activation(out=sq[q][:], in_=zc[:, cbk, :], func=AF.Square),
                       reads=[bzc[cbk]], writes=[bsq[q]])
                pe.op(lambda e: e.matmul(out=pvar[:], lhsT=O1k, rhs=sq[q][:], start=(cbk == 0), stop=(cbk == 7)),
                      reads=[bsq[q], b_cf], writes=[bpvar], acc=True)
            act.op(lambda e: e.activation(out=rs[:], in_=pvar[:], func=AF.Ln, bias=cst[:, 0:1]),
                   reads=[bpvar, b_cst], writes=[brs])
            act.op(lambda e: e.activation(out=rs[:], in_=rs[:], func=AF.Exp, scale=-0.5), reads=[brs], writes=[brs])
            for cbk in range(8):
                q = cbk % 2
                sp.dma(sg1[q][:], sg1_s[g, cbk, :, :], writes=[bsg1[q]])
                dve.op(lambda e: e.tensor_tensor(out=s1[q][:], in0=zc[:, cbk, :], in1=rs[:], op=ALU.mult),
                       reads=[bzc[cbk], brs], writes=[bs1[q]])
                act.op(lambda e: e.activation(out=s1[q][:], in_=s1[q][:], func=AF.Silu, scale=pcol("lng", cbk),
                                              bias=pcol("lnb", cbk)), reads=[bs1[q], b_pvec], writes=[bs1[q]])
                dve.op(lambda e: e.tensor_tensor(out=ZF[:, cbk, :], in0=s1[q][:], in1=sg1[q][:], op=ALU.mult),
                       reads=[bs1[q], bsg1[q]], writes=[bZF])
            for tt in range(4):
                i = 4 * g + tt
                p = oc % 2
                oc += 1
                sp.dma(x1[p][:], x1_s[i, :, :], writes=[bx1[p]])
                for half in range(2):
                    pk = p * 2 + half
                    hs = slice(half * 512, (half + 1) * 512)
                    for cbk in range(8):
                        pe.op(lambda e: e.matmul(out=pout[pk][:], lhsT=ZF[:, cbk, tt * 128:(tt + 1) * 128],
                                                 rhs=Wo[:, cbk, hs], start=(cbk == 0), stop=(cbk == 7)),
                              reads=[bZF, b_Wo], writes=[bpout[pk]], acc=True)
                    dve.op(lambda e: e.tensor_tensor(out=x2[p][:, hs], in0=pout[pk][:], in1=bcs[:, 0, hs], op=ALU.add),
                           reads=[bpout[pk], b_bc], writes=[bx2[p]])
                dve.op(lambda e: e.tensor_tensor(out=x2[p][:], in0=x2[p][:], in1=x1[p][:], op=ALU.add),
                       reads=[bx2[p], bx1[p]], writes=[bx2[p]])
                rmsnorm_rstd(x2[p], bx2[p], 128, junk, bjunk, ss, bss)
                dve.op(lambda e: e.scalar_tensor_tensor(out=yo[p][:], in0=x2[p][:], scalar=ss[:, 2:3], in1=bcs[:, 1, :],
                                                        op0=ALU.mult, op1=ALU.mult),
                       reads=[bx2[p], bss, b_bc], writes=[byo[p]])
                pool.dma(y_out[i, :, :], yo[p][:], reads=[byo[p]])
        fw.barrier()
    return nc


def build(T, stop=None):
    dry = _build(T, stop, None)
    plan = dry._fw.get_plan()
    return _build(T, stop, plan)


def _fm(v):
    v = np.asarray(v, np.float32).reshape(-1)
    return np.ascontiguousarray(v.reshape(-1, 128).T)


def _wrows(w):
    w = np.asarray(w, np.float32)
    return np.ascontiguousarray(w.reshape(8, 128, w.shape[1]).transpose(1, 0, 2))


def shared_inputs(norm_g, final_g, rw_in, rw_mu, rw_w0, rw_w2, rw_a0, rw_a2, rw_kk, rw_ka, rw_rk,
                  rw_lnx_g, rw_lnx_b, rw_out, cv_in, cv_b_in, cv_dw, cv_b_dw, cv_ln_g, cv_ln_b, cv_out, cv_b_out):
    pv = np.zeros((128, NPV), np.float32)

    def put(name, arr):
        pv[:, PV[name]:PV[name] + arr.shape[1]] = arr
    put("g0", _fm(norm_g[0]))
    put("g1", _fm(norm_g[1]))
    put("w0f", _fm(rw_w0[0, 0]))
    put("w0b", _fm(rw_w0[0, 1]))
    put("a0f", _fm(rw_a0[0, 0]))
    put("a0b", _fm(rw_a0[0, 1]))
    put("kk", _fm(rw_kk[0]))
    put("ka", _fm(rw_ka[0]))
    put("rk", _fm(rw_rk[0]))
    put("lnxg", _fm(rw_lnx_g[0]))
    put("lnxb", _fm(rw_lnx_b[0]))
    put("bdw", _fm(cv_b_dw[0]))
    put("lng", _fm(cv_ln_g[0]))
    put("lnb", _fm(cv_ln_b[0]))
    put("mu", _fm(rw_mu[0]))
    put("bin", _fm(cv_b_in[0]))
    dw = np.asarray(cv_dw[0], np.float32)
    dwf = dw.T.reshape(8, 128, 31).transpose(1, 0, 2).reshape(128, 248)
    put("dw", dwf)
    w2a2 = np.stack([np.asarray(rw_w2[0], np.float32).reshape(128, 1024),
                     np.asarray(rw_a2[0], np.float32).reshape(128, 1024)], axis=1)
    bc = np.stack([np.broadcast_to(np.asarray(cv_b_out[0], np.float32), (128, 1024)),
                   np.broadcast_to(np.asarray(final_g, np.float32), (128, 1024))], axis=1)
    r = np.arange(128)
    eye = (r[:, None] == r[None, :])
    su = (r[:, None] < r[None, :])
    u = (r[:, None] <= r[None, :])
    sl = (r[:, None] > r[None, :])
    l = (r[:, None] >= r[None, :])
    cbf = np.stack([np.tile(m.astype(np.float32), (1, 4)) for m in (eye, su, u, sl, l)], axis=1).astype(ml_dtypes.bfloat16)
    blk = ((r[:, None] // 64) == (r[None, :] // 64)).astype(np.float32)
    cf32 = np.stack([blk / 64.0, blk, np.full((128, 128), 1.0 / 1024.0, np.float32)], axis=1).astype(np.float32)
    return {
        "w_rwin": _wrows(rw_in[0]), "w_rwout": _wrows(rw_out[0]), "w_cvin": _wrows(cv_in[0]),
        "w_cvout": _wrows(cv_out[0]), "w2a2": np.ascontiguousarray(w2a2), "pvec": pv,
        "bc": np.ascontiguousarray(bc), "cbf": np.ascontiguousarray(cbf), "cf32": np.ascontiguousarray(cf32),
    }


def core_inputs(seqs):
    xs = np.concatenate([np.asarray(s, np.float32) for s in seqs], axis=0)
    T = xs.shape[0]
    NT, NG = T // 128, T // 512
    starts = set()
    o = 0
    for s in seqs:
        starts.add(o)
        o += s.shape[0]
    starts.add(T)
    x_t = xs.reshape(NT, 128, 1024)
    x_h = np.zeros((NT, 2, 1024), np.float32)
    sm = np.ones((NT, 2), np.float32)
    for i in range(NT):
        t0, t1 = i * 128, (i + 1) * 128
        if t0 not in starts:
            x_h[i, 0] = xs[t0 - 1]
        if t1 not in starts:
            x_h[i, 1] = xs[t1]
        if t1 in starts:
            sm[i, 0] = 0.0
        if t0 in starts:
            sm[i, 1] = 0.0
    cm = np.ones((NG, 2), np.float32)
    for g in range(NG):
        if g * 512 in starts:
            cm[g, 0] = 0.0
        if (g + 1) * 512 in starts:
            cm[g, 1] = 0.0
    return {
        "x_t": np.ascontiguousarray(x_t), "x_h": x_h,
        "smask": np.ascontiguousarray(np.broadcast_to(sm.reshape(1, -1), (128, NT * 2))),
        "cmask": np.ascontiguousarray(np.broadcast_to(cm.reshape(1, -1), (128, NG * 2))),
    }


def kernel(x_prompt, x_sample, norm_g, final_g, rw_in, rw_mu, rw_w0, rw_w2, rw_a0, rw_a2, rw_kk, rw_ka, rw_rk,
           rw_lnx_g, rw_lnx_b, rw_out, cv_in, cv_b_in, cv_dw, cv_b_dw, cv_ln_g, cv_ln_b, cv_out, cv_b_out):
    x_prompt = np.asarray(x_prompt, np.float32)
    x_sample = np.asarray(x_sample, np.float32)
    T = 16384
    sh = shared_inputs(norm_g, final_g, rw_in, rw_mu, rw_w0, rw_w2, rw_a0, rw_a2, rw_kk, rw_ka, rw_rk,
                       rw_lnx_g, rw_lnx_b, rw_out, cv_in, cv_b_in, cv_dw, cv_b_dw, cv_ln_g, cv_ln_b, cv_out, cv_b_out)
    cores = [core_inputs([x_prompt[0]]), core_inputs([x_prompt[1]]),
             core_inputs([x_sample[b] for b in range(8)])]
    in_maps = []
    for c in range(8):
        m = dict(sh)
        m.update(cores[min(c, 2)])
        in_maps.append(m)
    nc = build(T)
    res = run_bass_kernel_spmd(nc, in_maps, core_ids=list(range(8)))
    ys = [np.asarray(res.results[c]["y"], np.float32).reshape(T, 1024) for c in range(3)]
    y_prompt = np.stack([ys[0], ys[1]], axis=0)
    y_sample = ys[2].reshape(8, 2048, 1024)
    return (y_prompt, y_sample)
```

```python
import bisect
import numpy as np
import ml_dtypes
from contextlib import ExitStack
import concourse.bass as bass
import concourse.mybir as mybir
from concourse.bass_utils import run_bass_kernel_spmd

F32 = mybir.dt.float32
BF16 = mybir.dt.bfloat16
ALU = mybir.AluOpType
AF = mybir.ActivationFunctionType
CDEC = 0.6065306597126334


class Buf:
    __slots__ = ("w", "r", "excl")

    def __init__(self, excl=False):
        self.w = None
        self.r = {}
        self.excl = excl


def bufs(n):
    return [Buf() for _ in range(n)]


def pbufs(n):
    return [Buf(True) for _ in range(n)]


SAME_ENGINE_SYNC = ("pool", "dve", "act", "pe", "sp")


class Queue:
    def __init__(self, fw, eng, name, is_pe=False):
        self.fw = fw
        self.eng = eng
        self.name = name
        self.sem = fw.es.enter_context(fw.nc.semaphore("q_" + name))
        self.n = 0
        self.known = {}
        self.is_pe = is_pe
        self.dma_sems = []
        self.dma_cnt = []
        self.dma_k = 0
        self.same_sync = name in SAME_ENGINE_SYNC
        self.waited = set()
        self.plan = None if fw.plan is None else fw.plan[name]
        self.planset = None if self.plan is None else set(self.plan)

    def add_dma_sems(self, k):
        for i in range(k):
            self.dma_sems.append(self.fw.es.enter_context(self.fw.nc.semaphore("d_%s_%d" % (self.name, i))))
            self.dma_cnt.append(0)

    def _val(self, n):
        if self.plan is None:
            return n
        return bisect.bisect_right(self.plan, n)

    def _wait(self, tok):
        key, n = tok
        if self.known.get(key, 0) >= n:
            return
        self.known[key] = n
        if isinstance(key, Queue):
            key.waited.add(n)
            self.eng.wait_ge(key.sem, key._val(n))
        else:
            self.eng.wait_ge(key, n)

    def _deps(self, reads, writes, skip_same_waw=False):
        deps = {}

        def add(tok):
            if tok is None:
                return
            s, v = tok
            if deps.get(s, 0) < v:
                deps[s] = v
        for b in reads:
            add(b.w)
        for b in writes:
            if not (skip_same_waw and b.w is not None and b.w[0] is self):
                add(b.w)
            for s, v in b.r.items():
                add((s, v))
        if not self.same_sync:
            deps.pop(self, None)
        for s, v in deps.items():
            self._wait((s, v))

    def _record(self, tok, reads, writes):
        s, v = tok
        for b in reads:
            if b.r.get(s, 0) < v:
                b.r[s] = v
        for b in writes:
            b.w = tok
            b.r = {}

    def op(self, fn, reads=(), writes=(), acc=False):
        xreads = [b for b in reads if b.excl]
        if xreads:
            for b in xreads:
                for q, v in b.r.items():
                    if q is not self:
                        self._wait((q, v))
        self._deps(reads, writes, skip_same_waw=(acc and self.is_pe))
        ins = fn(self.eng)
        self.n += 1
        if self.planset is None or self.n in self.planset:
            ins.then_inc(self.sem, 1)
        self._record((self, self.n), reads, writes)
        return ins

    def dma(self, out, in_, reads=(), writes=()):
        self._deps(reads, writes)
        i = self.dma_k % len(self.dma_sems)
        self.dma_k += 1
        s = self.dma_sems[i]
        if self.dma_cnt[i] > 0:
            self._wait((s, 16 * self.dma_cnt[i]))
        self.eng.dma_start(out=out, in_=in_).then_inc(s, 16)
        self.dma_cnt[i] += 1
        self._record((s, 16 * self.dma_cnt[i]), reads, writes)


class FW:
    def __init__(self, nc, plan=None):
        self.nc = nc
        self.plan = plan
        self.es = ExitStack()
        self.pe = Queue(self, nc.tensor, "pe", is_pe=True)
        self.dve = Queue(self, nc.vector, "dve")
        self.act = Queue(self, nc.scalar, "act")
        self.pool = Queue(self, nc.gpsimd, "pool")
        self.sp = Queue(self, nc.sync, "sp")
        self.sp.add_dma_sems(8)
        self.pool.add_dma_sems(8)
        self.queues = [self.pe, self.dve, self.act, self.pool, self.sp]

    def all_tokens(self):
        toks = []
        for q in self.queues:
            if q.n > 0:
                toks.append((q, q.n))
            for s, c in zip(q.dma_sems, q.dma_cnt):
                if c > 0:
                    toks.append((s, 16 * c))
        return toks

    def barrier(self):
        toks = self.all_tokens()
        for q in self.queues:
            for tok in toks:
                q._wait(tok)

    def get_plan(self):
        return {q.name: sorted(q.waited) for q in self.queues}


PV = {}
_o = 0
for _n, _w in [("g0", 8), ("g1", 8), ("w0f", 8), ("w0b", 8), ("a0f", 8), ("a0b", 8), ("kk", 8), ("ka", 8),
               ("rk", 8), ("lnxg", 8), ("lnxb", 8), ("bdw", 8), ("lng", 8), ("lnb", 8), ("mu", 26),
               ("bin", 24), ("dw", 248)]:
    PV[_n] = _o
    _o += _w
NPV = _o


def _build(T, stop=None, plan=None):
    NT = T // 128
    NG = T // 512
    nc = bass.Bass("TRN2", target_bir_lowering=False)
    fw = FW(nc, plan)
    nc._fw = fw
    pe, dve, act, pool, sp = fw.pe, fw.dve, fw.act, fw.pool, fw.sp

    def dram(name, shape, dt, kind="Internal"):
        return nc.dram_tensor(name, shape, dt, kind=kind).ap()

    x_t = dram("x_t", [NT, 128, 1024], F32, "ExternalInput")
    x_h = dram("x_h", [NT, 2, 1024], F32, "ExternalInput")
    smask_d = dram("smask", [128, NT * 2], F32, "ExternalInput")
    cmask_d = dram("cmask", [128, NG * 2], F32, "ExternalInput")
    w_rwin = dram("w_rwin", [128, 8, 4352], F32, "ExternalInput")
    w_rwout = dram("w_rwout", [128, 8, 1024], F32, "ExternalInput")
    w_cvin = dram("w_cvin", [128, 8, 3072], F32, "ExternalInput")
    w_cvout = dram("w_cvout", [128, 8, 1024], F32, "ExternalInput")
    w2a2_d = dram("w2a2", [128, 2, 1024], F32, "ExternalInput")
    pvec_d = dram("pvec", [128, NPV], F32, "ExternalInput")
    bc_d = dram("bc", [128, 2, 1024], F32, "ExternalInput")
    cbf_d = dram("cbf", [128, 5, 512], BF16, "ExternalInput")
    cf32_d = dram("cf32", [128, 3, 128], F32, "ExternalInput")
    y_out = dram("y", [NT, 128, 1024], F32, "ExternalOutput")

    fm_s = dram("fm_s", [NT, 128, 2, 4, 1024], BF16)
    tm_s = dram("tm_s", [NT, 128, 7, 1024], BF16)
    eg_s = dram("eg_s", [NT, 128, 2, 512], F32)
    bon_s = dram("bon_s", [NT, 128, 1024], F32)
    yT_s = dram("yT_s", [2, NT, 128, 1024], F32)
    x1_s = dram("x1_s", [NT, 128, 1024], F32)
    h1T_s = dram("h1T_s", [NT, 128, 8, 128], BF16)
    z_s = dram("z_s", [8, 128, T + 32], BF16)
    sg1_s = dram("sg1_s", [NG, 8, 128, 512], F32)

    es = fw.es

    def sb(st, name, shape, dt):
        return st.enter_context(nc.sbuf_tensor(name, shape, dt))

    def psum(st, name, shape, dt):
        return st.enter_context(nc.psum_tensor(name, shape, dt))

    pvec = sb(es, "pvec_sb", [128, NPV], F32)
    b_pvec = Buf()
    cbf = sb(es, "cbf_sb", [128, 5, 512], BF16)
    b_cbf = Buf()
    cf32 = sb(es, "cf32_sb", [128, 3, 128], F32)
    b_cf = Buf()
    cst = sb(es, "cst", [128, 4], F32)
    b_cst = Buf()
    ones = sb(es, "ones", [128, 128], F32)
    b_ones = Buf()
    smask = sb(es, "smask_sb", [128, NT * 2], F32)
    cmask = sb(es, "cmask_sb", [128, NG * 2], F32)
    b_mask = Buf()
    sp.dma(pvec[:], pvec_d[:, :], writes=[b_pvec])
    sp.dma(cbf[:], cbf_d[:, :, :], writes=[b_cbf])
    sp.dma(cf32[:], cf32_d[:, :, :], writes=[b_cf])
    sp.dma(smask[:], smask_d[:, :], writes=[b_mask])
    sp.dma(cmask[:], cmask_d[:, :], writes=[b_mask])
    dve.op(lambda e: e.memset(cst[:, 0:1], 1e-5), writes=[b_cst])
    dve.op(lambda e: e.memset(cst[:, 1:2], 64e-5), writes=[b_cst])
    dve.op(lambda e: e.memset(cst[:, 2:3], 0.0), writes=[b_cst])
    dve.op(lambda e: e.memset(ones[:], 1.0), writes=[b_ones])
    ident = cbf[:, 0, 0:128]
    BO = cf32[:, 1, :]
    BOm = cf32[:, 0, :]
    O1k = cf32[:, 2, :]

    def pcol(name, j):
        return pvec[:, PV[name] + j: PV[name] + j + 1]

    CONSTS = [b_pvec, b_cbf, b_cf, b_cst, b_ones, b_mask]

    def load_weight_bf16(st, name, src, ncols, gname, dst_stage, b_stage, w=None):
        if w is None:
            w = sb(st, name, [128, 8, ncols], BF16)
        bw = Buf()
        for kc in range(8):
            for c0 in range(0, ncols, 1536):
                c1 = min(ncols, c0 + 1536)
                sp.dma(dst_stage[:, 0:c1 - c0], src[:, kc, c0:c1], writes=[b_stage])
                if gname is None:
                    act.op(lambda e: e.activation(out=w[:, kc, c0:c1], in_=dst_stage[:, 0:c1 - c0], func=AF.Copy),
                           reads=[b_stage], writes=[bw])
                else:
                    act.op(lambda e: e.activation(out=w[:, kc, c0:c1], in_=dst_stage[:, 0:c1 - c0], func=AF.Copy,
                                                  scale=pcol(gname, kc)),
                           reads=[b_stage, b_pvec], writes=[bw])
        return w, bw

    def rmsnorm_rstd(xt, bx, npart, junk, bjunk, ss, bss):
        act.op(lambda e: e.activation(out=junk[0:npart, :], in_=xt[0:npart, :], func=AF.Square,
                                      accum_out=ss[0:npart, 0:1]), reads=[bx], writes=[bjunk, bss])
        act.op(lambda e: e.activation(out=ss[0:npart, 1:2], in_=ss[0:npart, 0:1], func=AF.Ln,
                                      scale=1.0 / 1024.0, bias=cst[0:npart, 0:1]), reads=[bss, b_cst], writes=[bss])
        act.op(lambda e: e.activation(out=ss[0:npart, 2:3], in_=ss[0:npart, 1:2], func=AF.Exp, scale=-0.5),
               reads=[bss], writes=[bss])

    with ExitStack() as ph:
        pro = ExitStack()
        Win = sb(ph, "Win", [128, 8, 3328], BF16)
        w2b = sb(ph, "w2b", [128, 4, 1024], BF16)
        stage = sb(pro, "stageA", [128, 1536], F32)
        b_stage = Buf()
        Win, b_Win = load_weight_bf16(ph, "Win", w_rwin, 3328, "g0", stage, b_stage, w=Win)
        w2f = sb(pro, "w2f", [128, 2, 1024], F32)
        b_w2 = Buf()
        sp.dma(w2f[:], w2a2_d[:, :, :], writes=[b_w2])
        dve.op(lambda e: e.memset(w2b[:], 0.0), writes=[b_w2])
        for wa in range(2):
            for d in range(2):
                dve.op(lambda e: e.tensor_copy(out=w2b[d * 64:(d + 1) * 64, wa * 2 + d, :],
                                               in_=w2f[d * 64:(d + 1) * 64, wa, :]), reads=[b_w2], writes=[b_w2])
        fw.barrier()
        pro.close()
        omk = sb(ph, "omk", [128, 8], F32)
        b_omk = Buf()
        dve.op(lambda e: e.tensor_scalar(out=omk[:], in0=pvec[:, PV["ka"]:PV["ka"] + 8], scalar1=-1.0, scalar2=1.0,
                                         op0=ALU.mult, op1=ALU.add), reads=[b_pvec], writes=[b_omk])
        omm = sb(ph, "omm", [128, 26], F32)
        hmu = sb(ph, "hmu", [128, 26], F32)
        b_mu = Buf()
        mu_ap = pvec[:, PV["mu"]:PV["mu"] + 26]
        dve.op(lambda e: e.tensor_scalar(out=omm[:], in0=mu_ap, scalar1=-1.0, scalar2=1.0, op0=ALU.mult, op1=ALU.add),
               reads=[b_pvec], writes=[b_mu])
        dve.op(lambda e: e.tensor_scalar(out=hmu[:], in0=mu_ap, scalar1=0.5, scalar2=None, op0=ALU.mult),
               reads=[b_pvec], writes=[b_mu])

        X = [sb(ph, "xA0", [128, 1024], F32)]
        bX = bufs(1)
        ss = sb(ph, "ssA", [128, 4], F32)
        bss = Buf()
        hb = sb(ph, "hbA", [128, 1024], BF16)
        bhb = Buf()
        hT = [sb(ph, "hTA%d" % i, [128, 8, 130], BF16) for i in range(3)]
        bhT = bufs(3)
        ptr = psum(ph, "ptrA", [128, 8, 128], BF16)
        bptr = Buf(True)
        pin = [psum(ph, "pinA%d" % i, [128, 512], F32)[:, 0:390].rearrange("p (a b) -> p a b", b=130) for i in range(2)]
        bpin = pbufs(2)
        PL = [psum(ph, "plA%d" % k, [128, 4, 128], F32) for k in range(2)]
        bPL = pbufs(2)
        pss = psum(ph, "pssA", [128, 512], F32)[:, 0:256].rearrange("p (a b) -> p a b", b=128)
        bpss = pbufs(1) * 2
        PTQ = [psum(ph, "ptqA%d" % k, [128, 8, 128], BF16) for k in range(2)]
        bPTQ = pbufs(2)
        ue = [sb(ph, "ueA%d" % i, [128, 3, 130], F32) for i in range(2)]
        bue = bufs(2)
        t1 = [sb(ph, "t1A%d" % i, [128, 3, 128], F32) for i in range(1)] * 2
        bt1 = bufs(1) * 2
        t3 = sb(ph, "t3A", [128, 128], F32)
        bt3 = Buf()
        RKVs = [sb(ph, "rkvA%d" % k, [128, 3, 8, 128], F32) for k in range(2)]
        bRKVs = [[bufs(8) for _ in range(3)] for k in range(2)]
        LOs = [sb(ph, "loA%d" % k, [128, 2, 128], F32) for k in range(2)]
        bLOs = bufs(2)
        twb = sb(ph, "twA", [128, 2, 128], BF16)
        btw = Buf()
        twt = sb(ph, "twtA", [128, 128], F32)
        btwt = Buf()
        FM = [sb(ph, "fmA%d" % i, [128, 2, 4, 1024], BF16) for i in range(1)] * 2
        bFM = bufs(1) * 2
        TMb = [sb(ph, "tmA%d" % i, [128, 7, 1024], BF16) for i in range(1)] * 2
        bTM = bufs(1) * 2
        EG = [sb(ph, "egA%d" % i, [128, 2, 512], F32) for i in range(1)] * 2
        bEG = bufs(1) * 2
        BON = [sb(ph, "bonA%d" % i, [128, 8, 128], F32) for i in range(1)] * 2
        bBON = bufs(1) * 2
        GT = {}
        for nm in ["sgw0", "sgw1", "a0", "a1", "cs0", "cs1", "u0", "u1", "kk", "kk2", "kkn"]:
            for hg in range(2):
                GT[(nm, hg)] = (sb(ph, "g%d_%s" % (hg, nm), [128, 4, 128], F32), bufs(4))
        for nm in ["tmp0", "tmp1", "tmq0", "tmq1", "rk"]:
            GT[nm] = (sb(ph, "g_" + nm, [128, 4, 128], F32), bufs(4))
        SCs = [sb(ph, "g_sc%d" % hg, [128, 4, 8], F32) for hg in range(2)]
        bSCs = [bufs(4) for hg in range(2)]
        VBF = [sb(ph, "vbfA%d" % k, [128, 128], BF16) for k in range(2)]
        bVBF = bufs(2)
        bFMe = bufs(8)
        if stop == "A0":
            fw.barrier()
            return nc

        def H_stage(i):
            xm, bxm = X[0], bX[0]
            sp.dma(xm[:], x_t[i, :, :], writes=[bxm])
            rmsnorm_rstd(xm, bxm, 128, hb, bhb, ss, bss)
            act.op(lambda e: e.activation(out=hb[:], in_=xm[:], func=AF.Copy, scale=ss[:, 2:3]),
                   reads=[bxm, bss], writes=[bhb])
            hTt, bhTt = hT[i % 3], bhT[i % 3]
            for kc in range(8):
                pe.op(lambda e: e.transpose(out=ptr[:, kc, :], in_=hb[:, kc * 128:(kc + 1) * 128], identity=ident),
                      reads=[bhb, b_cbf], writes=[bptr], acc=True)
            dve.op(lambda e: e.tensor_copy(out=hTt[:, :, 1:129], in_=ptr[:]), reads=[bptr], writes=[bhTt])
            if i == 0:
                dve.op(lambda e: e.memset(hTt[:, :, 0:1], 0.0), writes=[bhTt])
            else:
                hTp, bhTp = hT[(i - 1) % 3], bhT[(i - 1) % 3]
                dve.op(lambda e: e.tensor_scalar(out=hTt[:, :, 0:1], in0=hTp[:, :, 128:129],
                                                 scalar1=smask[:, 2 * i + 1:2 * i + 2], scalar2=None, op0=ALU.mult),
                       reads=[bhTp, b_mask], writes=[bhTt])
                dve.op(lambda e: e.tensor_scalar(out=hTp[:, :, 129:130], in0=hTt[:, :, 1:2],
                                                 scalar1=smask[:, 2 * (i - 1):2 * (i - 1) + 1], scalar2=None, op0=ALU.mult),
                       reads=[bhTt, b_mask], writes=[bhTp])
            if i == NT - 1:
                dve.op(lambda e: e.memset(hTt[:, :, 129:130], 0.0), writes=[bhTt])

        def P_gen(i):
            par = i % 2
            hTt, bhTt = hT[i % 3], bhT[i % 3]
            RKV, bRKV, LO, bLO = RKVs[par], bRKVs[par], LOs[par], bLOs[par]
            for bg in range(9):
                pp = bg % 2
                cbs = [cb for cb in range(bg * 3, min(26, bg * 3 + 3))]
                for q, cb in enumerate(cbs):
                    for kc in range(8):
                        pe.op(lambda e: e.matmul(out=pin[pp][:, q, :], lhsT=Win[:, kc, cb * 128:(cb + 1) * 128],
                                                 rhs=hTt[:, kc, :], start=(kc == 0), stop=(kc == 7)),
                              reads=[b_Win, bhTt], writes=[bpin[pp]], acc=True)
                nq = len(cbs)
                act.op(lambda e: e.activation(out=ue[pp][:, 0:nq, :], in_=pin[pp][:, 0:nq, :], func=AF.Copy),
                       reads=[bpin[pp]], writes=[bue[pp]])
                dve.op(lambda e: e.tensor_tensor(out=t1[pp][:, 0:nq, :], in0=ue[pp][:, 0:nq, 0:128],
                                                 in1=ue[pp][:, 0:nq, 2:130], op=ALU.add),
                       reads=[bue[pp]], writes=[bt1[pp]])
                for q, cb in enumerate(cbs):
                    if cb < 24:
                        dst = RKV[:, cb // 8, cb % 8, :]
                        bd = bRKV[cb // 8][cb % 8]
                    else:
                        dst = LO[:, cb - 24, :]
                        bd = bLO
                    dve.op(lambda e: e.tensor_scalar(out=t3[:], in0=t1[pp][:, q, :], scalar1=hmu[:, cb:cb + 1],
                                                     scalar2=None, op0=ALU.mult),
                           reads=[bt1[pp], b_mu], writes=[bt3])
                    dve.op(lambda e: e.scalar_tensor_tensor(out=dst, in0=ue[pp][:, q, 1:129], scalar=omm[:, cb:cb + 1],
                                                            in1=t3[:], op0=ALU.mult, op1=ALU.add),
                           reads=[bue[pp], bt3, b_mu], writes=[bd])
                yield
        def E_pre(i):
            par = i % 2
            RKV, bRKV, LO, bLO = RKVs[par], bRKVs[par], LOs[par], bLOs[par]
            act.op(lambda e: e.activation(out=twt[:], in_=LO[:, 0, :], func=AF.Sigmoid, scale=2.0), reads=[bLO], writes=[btwt])
            pool.op(lambda e: e.tensor_scalar(out=twb[:, 0, :], in0=twt[:], scalar1=2.0, scalar2=-1.0, op0=ALU.mult, op1=ALU.add),
                    reads=[btwt], writes=[btw])
            act.op(lambda e: e.activation(out=twb[:, 1, :], in_=LO[:, 1, :], func=AF.Copy), reads=[bLO], writes=[btw])

            fmt = FM[par]
            tmt, btmt = TMb[par], bTM[par]
            egt, begt = EG[par], bEG[par]
            bont, bbont = BON[par], bBON[par]

        def E_half(i, hg):
            par = i % 2
            RKV, bRKV, LO, bLO = RKVs[par], bRKVs[par], LOs[par], bLOs[par]
            fmt = FM[par]
            tmt, btmt = TMb[par], bTM[par]
            egt, begt = EG[par], bEG[par]
            bont, bbont = BON[par], bBON[par]
            SC, bSC = SCs[hg], bSCs[hg]

            def GTl(nm):
                return GT[(nm, hg)] if (nm, hg) in GT else GT[nm]
            ebs = [4 * hg + e4 for e4 in range(4)]

            def T_(nm, e4):
                return GTl(nm)[0][:, e4, :]

            def B_(nm, e4):
                return GTl(nm)[1][e4]
            for e4, eb in enumerate(ebs):
                plb, bplb = PL[e4 % 2], bPL[e4 % 2]
                es_ = slice(eb * 128, (eb + 1) * 128)
                for slot in range(4):
                    wa = slot // 2
                    pe.op(lambda e: e.matmul(out=plb[:, slot, :], lhsT=w2b[:, slot, es_], rhs=twb[:, wa, :],
                                             start=True, stop=True), reads=[b_w2, btw], writes=[bplb], acc=True)
                for d in range(2):
                    act.op(lambda e: e.activation(out=T_("sgw%d" % d, e4), in_=plb[:, d, :], func=AF.Sigmoid,
                                                  bias=pcol("w0f" if d == 0 else "w0b", eb)),
                           reads=[bplb, b_pvec], writes=[B_("sgw%d" % d, e4)])
                    act.op(lambda e: e.activation(out=T_("a%d" % d, e4), in_=plb[:, 2 + d, :], func=AF.Sigmoid,
                                                  bias=pcol("a0f" if d == 0 else "a0b", eb)),
                           reads=[bplb, b_pvec], writes=[B_("a%d" % d, e4)])
            yield
            for e4, eb in enumerate(ebs):
                act.op(lambda e: e.activation(out=T_("kk", e4), in_=RKV[:, 1, eb, :], func=AF.Copy, scale=pcol("kk", eb)),
                       reads=[bRKV[1][eb], b_pvec], writes=[B_("kk", e4)])
            for e4, eb in enumerate(ebs):
                act.op(lambda e: e.activation(out=T_("kk2", e4), in_=T_("kk", e4), func=AF.Square),
                       reads=[B_("kk", e4)], writes=[B_("kk2", e4)])
            yield
            for e4 in range(4):
                for d in range(2):
                    dve.op(lambda e: e.tensor_tensor_scan(out=T_("cs%d" % d, e4), data0=ones[:], data1=T_("sgw%d" % d, e4),
                                                          initial=0.0, op0=ALU.mult, op1=ALU.add),
                           reads=[B_("sgw%d" % d, e4), b_ones], writes=[B_("cs%d" % d, e4)])
            yield
            for e4 in range(4):
                pe.op(lambda e: e.matmul(out=pss[:, 0, :], lhsT=BO, rhs=T_("kk2", e4), start=True, stop=True),
                      reads=[B_("kk2", e4), b_cf], writes=[bpss[0]])
                dve.op(lambda e: e.tensor_scalar(out=T_("kk2", e4), in0=pss[:, 0, :], scalar1=1e-19, scalar2=None,
                                                 op0=ALU.max), reads=[bpss[0]], writes=[B_("kk2", e4)])
            yield
            for e4, eb in enumerate(ebs):
                dve.op(lambda e: e.scalar_tensor_tensor(out=T_("rk", e4), in0=RKV[:, 0, eb, :], scalar=pcol("rk", eb),
                                                        in1=RKV[:, 1, eb, :], op0=ALU.mult, op1=ALU.mult),
                       reads=[bRKV[0][eb], bRKV[1][eb], b_pvec], writes=[B_("rk", e4)])
            for e4, eb in enumerate(ebs):
                pe.op(lambda e: e.matmul(out=pss[:, 1, :], lhsT=BO, rhs=T_("rk", e4), start=True, stop=True),
                      reads=[B_("rk", e4), b_cf], writes=[bpss[1]])
                dve.op(lambda e: e.tensor_tensor(out=bont[:, eb, :], in0=pss[:, 1, :], in1=RKV[:, 2, eb, :], op=ALU.mult),
                       reads=[bpss[1], bRKV[2][eb]], writes=[bbont])
            yield
            for e4 in range(4):
                for d in range(2):
                    pool.op(lambda e: e.tensor_tensor(out=T_("u%d" % d, e4), in0=T_("cs%d" % d, e4), in1=T_("sgw%d" % d, e4),
                                                      op=ALU.subtract),
                            reads=[B_("cs%d" % d, e4), B_("sgw%d" % d, e4)], writes=[B_("u%d" % d, e4)])
                dve.op(lambda e: e.tensor_scalar(out=SC[:, e4, 0:1], in0=GTl("cs1")[0][:, e4, 127:128], scalar1=-CDEC,
                                                 scalar2=None, op0=ALU.mult), reads=[B_("cs1", e4)], writes=[bSC[e4]])
                dve.op(lambda e: e.tensor_scalar(out=SC[:, e4, 1:2], in0=GTl("cs1")[0][:, e4, 127:128], scalar1=CDEC,
                                                 scalar2=None, op0=ALU.mult), reads=[B_("cs1", e4)], writes=[bSC[e4]])
            yield
            for e4 in range(4):
                act.op(lambda e: e.activation(out=T_("kk2", e4), in_=T_("kk2", e4), func=AF.Ln),
                       reads=[B_("kk2", e4)], writes=[B_("kk2", e4)])
            for e4 in range(4):
                act.op(lambda e: e.activation(out=T_("kk2", e4), in_=T_("kk2", e4), func=AF.Exp, scale=-0.5),
                       reads=[B_("kk2", e4)], writes=[B_("kk2", e4)])
            yield
            for e4 in range(4):
                act.op(lambda e: e.activation(out=SC[:, e4, 2:3], in_=GTl("cs0")[0][:, e4, 127:128], func=AF.Exp, scale=-CDEC),
                       reads=[B_("cs0", e4)], writes=[bSC[e4]])
                act.op(lambda e: e.activation(out=SC[:, e4, 3:4], in_=SC[:, e4, 0:1], func=AF.Exp),
                       reads=[bSC[e4]], writes=[bSC[e4]])
                act.op(lambda e: e.activation(out=T_("sgw0", e4), in_=T_("cs0", e4), func=AF.Exp, scale=-CDEC),
                       reads=[B_("cs0", e4)], writes=[B_("sgw0", e4)])
                act.op(lambda e: e.activation(out=T_("cs0", e4), in_=T_("cs0", e4), func=AF.Exp, scale=CDEC),
                       reads=[B_("cs0", e4)], writes=[B_("cs0", e4)])
                act.op(lambda e: e.activation(out=T_("u0", e4), in_=T_("u0", e4), func=AF.Exp, scale=-CDEC),
                       reads=[B_("u0", e4)], writes=[B_("u0", e4)])
                act.op(lambda e: e.activation(out=T_("sgw1", e4), in_=T_("u1", e4), func=AF.Exp, scale=CDEC,
                                              bias=SC[:, e4, 0:1]),
                       reads=[B_("u1", e4), bSC[e4]], writes=[B_("sgw1", e4)])
                act.op(lambda e: e.activation(out=T_("cs1", e4), in_=T_("cs1", e4), func=AF.Exp, scale=CDEC,
                                              bias=SC[:, e4, 0:1]),
                       reads=[B_("cs1", e4), bSC[e4]], writes=[B_("cs1", e4)])
                act.op(lambda e: e.activation(out=T_("u1", e4), in_=T_("u1", e4), func=AF.Exp, scale=-CDEC,
                                              bias=SC[:, e4, 1:2]),
                       reads=[B_("u1", e4), bSC[e4]], writes=[B_("u1", e4)])
            E1n, E2n, E3n = ("sgw0", "sgw1"), ("cs0", "u1"), ("u0", "cs1")
            yield
            for e4 in range(4):
                dve.op(lambda e: e.tensor_tensor(out=T_("kkn", e4), in0=T_("kk", e4), in1=T_("kk2", e4), op=ALU.mult),
                       reads=[B_("kk", e4), B_("kk2", e4)], writes=[B_("kkn", e4)])
            yield
            for e4, eb in enumerate(ebs):
                for d in range(2):
                    pool.op(lambda e: e.tensor_scalar(out=egt[:, d, eb * 64:(eb + 1) * 64], in0=ones[:, 0:64],
                                                      scalar1=SC[:, e4, 2 + d:3 + d],
                                                      scalar2=smask[:, 2 * i + d:2 * i + d + 1], op0=ALU.mult, op1=ALU.mult),
                            reads=[bSC[e4], b_ones, b_mask], writes=[begt])
            yield
            for e4, eb in enumerate(ebs):
                for d in range(2):
                    act.op(lambda e: e.activation(out=T_("tmp%d" % d, e4), in_=T_("a%d" % d, e4), func=AF.Identity,
                                                  scale=pcol("ka", eb), bias=omk[:, eb:eb + 1]),
                           reads=[B_("a%d" % d, e4), b_pvec, b_omk], writes=[B_("tmp%d" % d, e4)])
                    pool.op(lambda e: e.tensor_tensor(out=T_("tmq%d" % d, e4), in0=T_("kkn", e4), in1=T_("a%d" % d, e4),
                                                      op=ALU.mult),
                            reads=[B_("kkn", e4), B_("a%d" % d, e4)], writes=[B_("tmq%d" % d, e4)])
            yield
            def s11(kind, e4, eb, d):
                es_ = slice(eb * 128, (eb + 1) * 128)
                r_ap, k_ap = RKV[:, 0, eb, :], RKV[:, 1, eb, :]
                br, bk = bRKV[0][eb], bRKV[1][eb]
                E1, E2, E3 = E1n[d], E2n[d], E3n[d]
                if kind == 0:
                    dve.op(lambda e: e.tensor_tensor(out=T_("tmp%d" % d, e4), in0=T_("tmp%d" % d, e4), in1=k_ap, op=ALU.mult),
                           reads=[B_("tmp%d" % d, e4), bk], writes=[B_("tmp%d" % d, e4)])
                elif kind == 1:
                    pool.op(lambda e: e.tensor_tensor(out=fmt[:, d, 0, es_], in0=r_ap, in1=T_(E1, e4), op=ALU.mult),
                            reads=[br, B_(E1, e4)], writes=[bFMe[eb]])
                elif kind == 2:
                    dve.op(lambda e: e.scalar_tensor_tensor(out=fmt[:, d, 2, es_], in0=T_("kkn", e4), scalar=-1.0,
                                                            in1=T_(E3, e4), op0=ALU.mult, op1=ALU.mult),
                           reads=[B_("kkn", e4), B_(E3, e4)], writes=[bFMe[eb]])
                elif kind == 3:
                    dve.op(lambda e: e.tensor_tensor(out=fmt[:, d, 3, es_], in0=T_("tmq%d" % d, e4), in1=T_(E2, e4),
                                                     op=ALU.mult),
                           reads=[B_("tmq%d" % d, e4), B_(E2, e4)], writes=[bFMe[eb]])
                else:
                    dve.op(lambda e: e.tensor_tensor(out=fmt[:, d, 1, es_], in0=T_("tmp%d" % d, e4), in1=T_(E2, e4),
                                                     op=ALU.mult),
                           reads=[B_("tmp%d" % d, e4), B_(E2, e4)], writes=[bFMe[eb]])
            for kind in range(5):
                for e4, eb in enumerate(ebs):
                    for d in range(2):
                        s11(kind, e4, eb, d)
            yield
            for e4, eb in enumerate(ebs):
                es_ = slice(eb * 128, (eb + 1) * 128)
                vb, bvb = VBF[e4 % 2], bVBF[e4 % 2]
                act.op(lambda e: e.activation(out=vb[:], in_=RKV[:, 2, eb, :], func=AF.Copy),
                       reads=[bRKV[2][eb]], writes=[bvb])
                srcs = [fmt[:, 0, 1, es_], fmt[:, 0, 3, es_], fmt[:, 0, 2, es_],
                        fmt[:, 1, 1, es_], fmt[:, 1, 3, es_], fmt[:, 1, 2, es_], vb[:]]
                ptq, bptq = PTQ[e4 % 2], bPTQ[e4 % 2]
                for qi, s_ap in enumerate(srcs):
                    pe.op(lambda e: e.transpose(out=ptq[:, qi, :], in_=s_ap, identity=ident),
                          reads=[bFMe[eb], bvb, b_cbf], writes=[bptq], acc=True)
                act.op(lambda e: e.activation(out=tmt[:, :, es_], in_=ptq[:, 0:7, :], func=AF.Copy),
                       reads=[bptq], writes=[btmt])

        def E_post(i):
            par = i % 2
            fmt = FM[par]
            tmt, btmt = TMb[par], bTM[par]
            egt, begt = EG[par], bEG[par]
            bont, bbont = BON[par], bBON[par]
            pool.dma(fm_s[i, :, :, :, :], fmt[:], reads=bFMe)
            pool.dma(tm_s[i, :, :, :], tmt[:], reads=[btmt])
            pool.dma(eg_s[i, :, :, :], egt[:], reads=[begt])
            pool.dma(bon_s[i, :, :], bont[:].rearrange("p a b -> p (a b)"), reads=[bbont])

        def run_interleaved(gens):
            gens = [g for g in gens if g is not None]
            while gens:
                for g in list(gens):
                    try:
                        next(g)
                    except StopIteration:
                        gens.remove(g)

        def delayed(gen, n):
            for _ in range(n):
                yield
            yield from gen

        def E_all(i):
            E_pre(i)
            return [E_half(i, 0), delayed(E_half(i, 1), 4)]

        H_stage(0)
        for i in range(NT):
            if i + 1 < NT:
                H_stage(i + 1)
            gens = [P_gen(i)]
            if i > 0:
                gens += E_all(i - 1)
            run_interleaved(gens)
            if i > 0:
                E_post(i - 1)
        run_interleaved(E_all(NT - 1))
        E_post(NT - 1)
        fw.barrier()
    if stop == "A":
        return nc

    with ExitStack() as ph:
        I4, SU4, U4, SL4, L4 = [cbf[:, m, :].rearrange("p (a b) -> p a b", b=128) for m in range(5)]
        banks = [psum(ph, "bkB%d" % i, [128, 512], F32) for i in range(8)]
        bbank = pbufs(8)
        bank_ctr = [0]

        def next_bank():
            k = bank_ctr[0] % 8
            bank_ctr[0] += 1
            return banks[k], bbank[k]

        FMd = [sb(ph, "fmB%d" % i, [128, 4, 1024], BF16) for i in range(2)]
        bFMd = bufs(2)
        TMd = [sb(ph, "tmB%d" % i, [128, 3, 1024], BF16) for i in range(2)]
        bTMd = bufs(2)
        TMv = [sb(ph, "tvB%d" % i, [128, 1024], BF16) for i in range(2)]
        bTMv = bufs(2)
        EGd = [sb(ph, "egB%d" % i, [128, 512], F32) for i in range(2)]
        bEGd = bufs(2)

        def mat16(name):
            return sb(ph, name, [128, 16, 128], BF16), bufs(4)
        Pm, bP = mat16("PmB")
        PTm, bPT = mat16("PTmB")
        P2m, bP2 = mat16("P2mB")
        PT2m, bPT2 = mat16("PT2mB")
        Xm, bXm = mat16("XmB")
        XTm, bXTm = mat16("XTmB")
        Aak, bAak = mat16("AakB")
        Arb, bArb = mat16("ArbB")
        Ark, bArk = mat16("ArkB")
        M1 = sb(ph, "M1B", [128, 16, 64], BF16)
        bM1 = bufs(2)
        AtT = sb(ph, "AtTB", [128, 8, 128], BF16)
        bAtT = bufs(2)
        Usb = sb(ph, "UsbB", [128, 16, 64], BF16)
        bUsb = bufs(2)
        h32 = sb(ph, "h32B", [128, 512], F32)
        hbf = sb(ph, "hbfB", [128, 8, 64], BF16)
        bh = Buf()
        OT = [sb(ph, "OTB%d" % i, [128, 8, 128], F32) for i in range(2)]
        bOT = bufs(2)

        def load(step, i, d):
            p = step % 2
            sp.dma(FMd[p][:], fm_s[i, :, d, :, :], writes=[bFMd[p]])
            sp.dma(TMd[p][:], tm_s[i, :, 3 * d:3 * d + 3, :], writes=[bTMd[p]])
            sp.dma(TMv[p][:], tm_s[i, :, 6, :], writes=[bTMv[p]])
            sp.dma(EGd[p][:], eg_s[i, :, d, :], writes=[bEGd[p]])

        step = 0
        order = [(i, 0) for i in range(NT)] + [(i, 1) for i in range(NT - 1, -1, -1)]
        load(0, order[0][0], order[0][1])
        for idx, (i, d) in enumerate(order):
            p = step % 2
            if idx + 1 < len(order):
                load(step + 1, order[idx + 1][0], order[idx + 1][1])
            if idx == 0 or idx == NT:
                dve.op(lambda e: e.memset(h32[:], 0.0), writes=[bh])
                dve.op(lambda e: e.memset(hbf[:], 0.0), writes=[bh])
            fm, bfm, tmd, btmd, tv, btv, eg, beg = FMd[p], bFMd[p], TMd[p], bTMd[p], TMv[p], bTMv[p], EGd[p], bEGd[p]
            m_su, m_u, m_sl = (SU4, U4, SL4) if d == 0 else (SL4, L4, SU4)

            def fmh(q, h):
                eb, j = h // 2, h % 2
                return fm[j * 64:(j + 1) * 64, q, eb * 128:(eb + 1) * 128]

            def tmh(q, h):
                return tmd[:, q, h * 64:(h + 1) * 64]

            def vh(h):
                return tv[:, h * 64:(h + 1) * 64]

            def prod(lq, rq, mask, dst, bdst):
                for hb8 in range(2):
                    bkj = [next_bank(), next_bank()]
                    for e4 in range(4):
                        for j in range(2):
                            h = hb8 * 8 + e4 * 2 + j
                            bk, bbk = bkj[j]
                            pe.op(lambda e: e.matmul(out=bk[:, e4 * 128:(e4 + 1) * 128], lhsT=fmh(lq, h), rhs=fmh(rq, h),
                                                     start=True, stop=True), reads=[bfm], writes=[bbk], acc=True)
                    for j in range(2):
                        bk, bbk = bkj[j]
                        dve.op(lambda e: e.tensor_tensor(out=dst[:, hb8 * 8 + j:hb8 * 8 + 8:2, :],
                                                         in0=bk[:].rearrange("p (a b) -> p a b", b=128), in1=mask, op=ALU.mult),
                               reads=[bbk, b_cbf], writes=[bdst[hb8 * 2], bdst[hb8 * 2 + 1]])
            prod(3, 2, m_su, Pm, bP)
            prod(2, 3, m_sl, PTm, bPT)
            prod(1, 2, m_su, Aak, bAak)
            prod(3, 0, m_u, Arb, bArb)
            prod(1, 0, m_u, Ark, bArk)
            for hbk in range(4):
                hs = slice(hbk * 4, (hbk + 1) * 4)
                dve.op(lambda e: e.tensor_tensor(out=Xm[:, hs, :], in0=Pm[:, hs, :], in1=I4, op=ALU.add),
                       reads=[bP[hbk], b_cbf], writes=[bXm[hbk]])
                dve.op(lambda e: e.tensor_tensor(out=XTm[:, hs, :], in0=PTm[:, hs, :], in1=I4, op=ALU.add),
                       reads=[bPT[hbk], b_cbf], writes=[bXTm[hbk]])
            cur = (Pm, bP, PTm, bPT)
            nxt = (P2m, bP2, PT2m, bPT2)
            for lev in range(6):
                Pc, bPc, PTc, bPTc = cur
                Pn, bPn, PTn, bPTn = nxt
                last = (lev == 5)

                def mm_batch(lhs, blhs, rhs, brhs, evac):
                    for hbk in range(4):
                        bk, bbk = next_bank()
                        for hh in range(4):
                            h = hbk * 4 + hh
                            pe.op(lambda e: e.matmul(out=bk[:, hh * 128:(hh + 1) * 128], lhsT=lhs[:, h, :], rhs=rhs[:, h, :],
                                                     start=True, stop=True),
                                  reads=[blhs[hbk], brhs[hbk]], writes=[bbk], acc=True)
                        evac(hbk, bk, bbk)

                def ev_copy(dst, bdst):
                    def f(hbk, bk, bbk):
                        act.op(lambda e: e.activation(out=dst[:, hbk * 4:(hbk + 1) * 4, :],
                                                      in_=bk[:].rearrange("p (a b) -> p a b", b=128), func=AF.Copy),
                               reads=[bbk], writes=[bdst[hbk]])
                    return f

                def ev_add(dst, bdst):
                    def f(hbk, bk, bbk):
                        hs = slice(hbk * 4, (hbk + 1) * 4)
                        dve.op(lambda e: e.tensor_tensor(out=dst[:, hs, :], in0=bk[:].rearrange("p (a b) -> p a b", b=128),
                                                         in1=dst[:, hs, :], op=ALU.add),
                               reads=[bbk, bdst[hbk]], writes=[bdst[hbk]])
                    return f
                mm_batch(PTc, bPTc, Pc, bPc, ev_copy(Pn, bPn))
                if not last:
                    mm_batch(Pc, bPc, PTc, bPTc, ev_copy(PTn, bPTn))
                mm_batch(XTm, bXTm, Pn, bPn, ev_add(Xm, bXm))
                if not last:
                    mm_batch(Pn, bPn, XTm, bXTm, ev_add(XTm, bXTm))
                cur, nxt = nxt, cur
            for g8 in range(2):
                bk, bbk = next_bank()
                for hh in range(8):
                    h = g8 * 8 + hh
                    pe.op(lambda e: e.matmul(out=bk[:, hh * 64:(hh + 1) * 64], lhsT=Aak[:, h, :], rhs=vh(h),
                                             start=True, stop=True), reads=[bAak[h // 4], btv], writes=[bbk], acc=True)
                act.op(lambda e: e.activation(out=M1[:, g8 * 8:(g8 + 1) * 8, :],
                                              in_=bk[:].rearrange("p (a b) -> p a b", b=64), func=AF.Copy),
                       reads=[bbk], writes=[bM1[g8]])
            for g4 in range(2):
                bk, bbk = next_bank()
                for e4 in range(4):
                    eb = g4 * 4 + e4
                    for j in range(2):
                        h = eb * 2 + j
                        pe.op(lambda e: e.matmul(out=bk[j * 64:(j + 1) * 64, e4 * 128:(e4 + 1) * 128], lhsT=tmh(2, h),
                                                 rhs=Xm[:, h, :], start=True, stop=True),
                              reads=[btmd, bXm[h // 4]], writes=[bbk], acc=True)
                act.op(lambda e: e.activation(out=AtT[:, g4 * 4:(g4 + 1) * 4, :],
                                              in_=bk[:].rearrange("p (a b) -> p a b", b=128), func=AF.Copy),
                       reads=[bbk], writes=[bAtT[g4]])
            for g8 in range(2):
                bk, bbk = next_bank()
                for hh in range(8):
                    h = g8 * 8 + hh
                    eb, j = h // 2, h % 2
                    js = slice(j * 64, (j + 1) * 64)
                    pe.op(lambda e: e.matmul(out=bk[:, hh * 64:(hh + 1) * 64], lhsT=AtT[js, eb, :], rhs=hbf[js, eb, :],
                                             start=True, stop=False), reads=[bAtT[eb // 4], bh], writes=[bbk], acc=True)
                    pe.op(lambda e: e.matmul(out=bk[:, hh * 64:(hh + 1) * 64], lhsT=Xm[:, h, :], rhs=M1[:, h, :],
                                             start=False, stop=True), reads=[bXm[h // 4], bM1[g8]], writes=[bbk], acc=True)
                dve.op(lambda e: e.tensor_copy(out=Usb[:, g8 * 8:(g8 + 1) * 8, :],
                                               in_=bk[:].rearrange("p (a b) -> p a b", b=64)),
                       reads=[bbk], writes=[bUsb[g8]])
            ot, bot = OT[p], bOT[p]
            for g4 in range(2):
                bk, bbk = next_bank()
                for e4 in range(4):
                    eb = g4 * 4 + e4
                    for j in range(2):
                        h = eb * 2 + j
                        js = slice(j * 64, (j + 1) * 64)
                        o_ap = bk[js, e4 * 128:(e4 + 1) * 128]
                        pe.op(lambda e: e.matmul(out=o_ap, lhsT=hbf[js, eb, :], rhs=fmh(0, h), start=True, stop=False),
                              reads=[bh, bfm], writes=[bbk], acc=True)
                        pe.op(lambda e: e.matmul(out=o_ap, lhsT=Usb[:, h, :], rhs=Arb[:, h, :], start=False, stop=False),
                              reads=[bUsb[h // 8], bArb[h // 4]], writes=[bbk], acc=True)
                        pe.op(lambda e: e.matmul(out=o_ap, lhsT=vh(h), rhs=Ark[:, h, :], start=False, stop=True),
                              reads=[btv, bArk[h // 4]], writes=[bbk], acc=True)
                act.op(lambda e: e.activation(out=ot[:, g4 * 4:(g4 + 1) * 4, :],
                                              in_=bk[:].rearrange("p (a b) -> p a b", b=128), func=AF.Copy),
                       reads=[bbk], writes=[bot])
            pool.dma(yT_s[d, i, :, :], ot[:].rearrange("p a b -> p (a b)"), reads=[bot])
            bk, bbk = next_bank()
            for eb in range(8):
                for j in range(2):
                    h = eb * 2 + j
                    js = slice(j * 64, (j + 1) * 64)
                    o_ap = bk[js, eb * 64:(eb + 1) * 64]
                    pe.op(lambda e: e.matmul(out=o_ap, lhsT=tmh(1, h), rhs=Usb[:, h, :], start=True, stop=False),
                          reads=[btmd, bUsb[h // 8]], writes=[bbk], acc=True)
                    pe.op(lambda e: e.matmul(out=o_ap, lhsT=tmh(0, h), rhs=vh(h), start=False, stop=True),
                          reads=[btmd, btv], writes=[bbk], acc=True)
            dve.op(lambda e: e.tensor_tensor(out=h32[:], in0=bk[:], in1=h32[:], op=ALU.add), reads=[bbk, bh], writes=[bh])
            dve.op(lambda e: e.tensor_tensor(out=h32[:], in0=h32[:], in1=eg[:], op=ALU.mult), reads=[bh, beg], writes=[bh])
            act.op(lambda e: e.activation(out=hbf[:].rearrange("p a b -> p (a b)"), in_=h32[:], func=AF.Copy),
                   reads=[bh], writes=[bh])
            step += 1
        fw.barrier()
    if stop == "B":
        return nc

    with ExitStack() as ph:
        stage = sb(ph, "stageC", [128, 1536], F32)
        b_stage = Buf()
        Wout, b_Wout = load_weight_bf16(ph, "Wout", w_rwout, 1024, None, stage, b_stage)
        Wg, b_Wg = load_weight_bf16(ph, "Wg", w_rwin[:, :, 3328:4352], 1024, "g0", stage, b_stage)
        YF = [sb(ph, "yfC%d" % i, [128, 8, 128], F32) for i in range(2)]
        YB = [sb(ph, "ybC%d" % i, [128, 8, 128], F32) for i in range(2)]
        BN = [sb(ph, "bnC%d" % i, [128, 8, 128], F32) for i in range(2)]
        XC = [sb(ph, "xC%d" % i, [128, 1024], F32) for i in range(2)]
        bIN = bufs(2)
        yg = [sb(ph, "ygC%d" % i, [128, 8, 128], BF16) for i in range(2)]
        byg = bufs(2)
        pout = [psum(ph, "poutC%d" % i, [128, 512], F32) for i in range(2)]
        bpout = pbufs(2)
        ptr = psum(ph, "ptrC", [128, 8, 128], BF16)
        bptr = Buf(True)
        ptrx = psum(ph, "ptrxC", [128, 8, 128], BF16)
        bptrx = Buf(True)
        pg = [psum(ph, "pgC%d" % k, [128, 4, 128], F32) for k in range(2)]
        bpg = pbufs(2)
        x1 = [sb(ph, "x1C%d" % i, [128, 1024], F32) for i in range(2)]
        bx1 = bufs(2)
        junk = sb(ph, "junkC", [128, 1024], BF16)
        bjunk = Buf()
        ss = sb(ph, "ssC", [128, 4], F32)
        bss = Buf()
        hb = sb(ph, "hbC", [128, 1024], BF16)
        bhb = Buf()
        junkx = sb(ph, "junkxC", [128, 1024], BF16)
        bjunkx = Buf()
        ssx = sb(ph, "ssxC", [128, 4], F32)
        bssx = Buf()
        hbx = sb(ph, "hbxC", [128, 1024], BF16)
        bhbx = Buf()
        hTx = sb(ph, "hTxC", [128, 8, 128], BF16)
        bhTx = Buf()
        SGt = sb(ph, "SGtC", [128, 8, 128], F32)
        bSGt = bufs(2)
        h1T = [sb(ph, "h1TC%d" % i, [128, 8, 128], BF16) for i in range(2)]
        bh1T = bufs(2)
        Yw = sb(ph, "YwC", [128, 8, 128], F32)
        bYw = bufs(2)
        SQ = sb(ph, "SQC", [128, 8, 128], F32)
        bSQ = bufs(2)
        pm = [psum(ph, "pmC%d" % k, [128, 4, 128], F32) for k in range(2)]
        bpm = pbufs(2)

        def loadC(i):
            p = i % 2
            sp.dma(YF[p][:].rearrange("p a b -> p (a b)"), yT_s[0, i, :, :], writes=[bIN[p]])
            sp.dma(YB[p][:].rearrange("p a b -> p (a b)"), yT_s[1, i, :, :], writes=[bIN[p]])
            sp.dma(BN[p][:].rearrange("p a b -> p (a b)"), bon_s[i, :, :], writes=[bIN[p]])
            sp.dma(XC[p][:], x_t[i, :, :], writes=[bIN[p]])

        def Gg_gen(i):
            p = i % 2
            bin_ = bIN[p]
            rmsnorm_rstd(XC[p], bin_, 128, junkx, bjunkx, ssx, bssx)
            act.op(lambda e: e.activation(out=hbx[:], in_=XC[p][:], func=AF.Copy, scale=ssx[:, 2:3]),
                   reads=[bin_, bssx], writes=[bhbx])
            for kc in range(8):
                pe.op(lambda e: e.transpose(out=ptrx[:, kc, :], in_=hbx[:, kc * 128:(kc + 1) * 128], identity=ident),
                      reads=[bhbx, b_cbf], writes=[bptrx], acc=True)
            dve.op(lambda e: e.tensor_copy(out=hTx[:], in_=ptrx[:]), reads=[bptrx], writes=[bhTx])
            yield
            for hf in range(2):
                hs = slice(hf * 4, (hf + 1) * 4)
                for e4 in range(4):
                    cb = hf * 4 + e4
                    for kc in range(8):
                        pe.op(lambda e: e.matmul(out=pg[hf][:, e4, :], lhsT=Wg[:, kc, cb * 128:(cb + 1) * 128],
                                                 rhs=hTx[:, kc, :], start=(kc == 0), stop=(kc == 7)),
                              reads=[b_Wg, bhTx], writes=[bpg[hf]], acc=True)
                act.op(lambda e: e.activation(out=SGt[:, hs, :], in_=pg[hf][:], func=AF.Sigmoid),
                       reads=[bpg[hf]], writes=[bSGt[hf]])
                dve.op(lambda e: e.tensor_tensor(out=SGt[:, hs, :], in0=pg[hf][:], in1=SGt[:, hs, :], op=ALU.mult),
                       reads=[bpg[hf], bSGt[hf]], writes=[bSGt[hf]])
                yield

        def Gn_gen(i):
            p = i % 2
            bin_ = bIN[p]
            for hf in range(2):
                hs = slice(hf * 4, (hf + 1) * 4)
                dve.op(lambda e: e.tensor_tensor(out=Yw[:, hs, :], in0=YF[p][:, hs, :], in1=YB[p][:, hs, :], op=ALU.add),
                       reads=[bin_], writes=[bYw[hf]])
                for e4 in range(4):
                    pe.op(lambda e: e.matmul(out=pm[hf][:, e4, :], lhsT=BOm, rhs=Yw[:, hf * 4 + e4, :], start=True, stop=True),
                          reads=[bYw[hf], b_cf], writes=[bpm[hf]], acc=True)
                dve.op(lambda e: e.tensor_tensor(out=Yw[:, hs, :], in0=Yw[:, hs, :], in1=pm[hf][:], op=ALU.subtract),
                       reads=[bYw[hf], bpm[hf]], writes=[bYw[hf]])
                act.op(lambda e: e.activation(out=SQ[:, hs, :], in_=Yw[:, hs, :], func=AF.Square),
                       reads=[bYw[hf]], writes=[bSQ[hf]])
                yield
                for e4 in range(4):
                    pe.op(lambda e: e.matmul(out=pm[hf][:, e4, :], lhsT=BOm, rhs=SQ[:, hf * 4 + e4, :], start=True, stop=True),
                          reads=[bSQ[hf], b_cf], writes=[bpm[hf]], acc=True)
                act.op(lambda e: e.activation(out=SQ[:, hs, :], in_=pm[hf][:], func=AF.Ln, bias=cst[:, 1:2]),
                       reads=[bpm[hf], b_cst], writes=[bSQ[hf]])
                act.op(lambda e: e.activation(out=SQ[:, hs, :], in_=SQ[:, hs, :], func=AF.Exp, scale=-0.5),
                       reads=[bSQ[hf]], writes=[bSQ[hf]])
                dve.op(lambda e: e.tensor_tensor(out=Yw[:, hs, :], in0=Yw[:, hs, :], in1=SQ[:, hs, :], op=ALU.mult),
                       reads=[bYw[hf], bSQ[hf]], writes=[bYw[hf]])
                yield
                for e4 in range(4):
                    eb = hf * 4 + e4
                    act.op(lambda e: e.activation(out=Yw[:, eb, :], in_=Yw[:, eb, :], func=AF.Identity,
                                                  scale=pcol("lnxg", eb), bias=pcol("lnxb", eb)),
                           reads=[bYw[hf], b_pvec], writes=[bYw[hf]])
                dve.op(lambda e: e.tensor_tensor(out=Yw[:, hs, :], in0=Yw[:, hs, :], in1=BN[p][:, hs, :], op=ALU.add),
                       reads=[bYw[hf], bin_], writes=[bYw[hf]])
                dve.op(lambda e: e.tensor_tensor(out=yg[p][:, hs, :], in0=Yw[:, hs, :], in1=SGt[:, hs, :], op=ALU.mult),
                       reads=[bYw[hf], bSGt[hf]], writes=[byg[p]])
                yield

        def O_gen(i):
            p = i % 2
            bin_ = bIN[p]
            for half in range(2):
                for eb in range(8):
                    pe.op(lambda e: e.matmul(out=pout[half][:], lhsT=yg[p][:, eb, :], rhs=Wout[:, eb, half * 512:(half + 1) * 512],
                                             start=(eb == 0), stop=(eb == 7)),
                          reads=[byg[p], b_Wout], writes=[bpout[half]], acc=True)
                dve.op(lambda e: e.tensor_tensor(out=x1[p][:, half * 512:(half + 1) * 512], in0=pout[half][:],
                                                 in1=XC[p][:, half * 512:(half + 1) * 512], op=ALU.add),
                       reads=[bpout[half], bin_], writes=[bx1[p]])
                yield
            pool.dma(x1_s[i, :, :], x1[p][:], reads=[bx1[p]])
            rmsnorm_rstd(x1[p], bx1[p], 128, junk, bjunk, ss, bss)
            act.op(lambda e: e.activation(out=hb[:], in_=x1[p][:], func=AF.Copy, scale=ss[:, 2:3]),
                   reads=[bx1[p], bss], writes=[bhb])
            yield
            for kc in range(8):
                pe.op(lambda e: e.transpose(out=ptr[:, kc, :], in_=hb[:, kc * 128:(kc + 1) * 128], identity=ident),
                      reads=[bhb, b_cbf], writes=[bptr], acc=True)
            dve.op(lambda e: e.tensor_copy(out=h1T[p][:], in_=ptr[:]), reads=[bptr], writes=[bh1T[p]])
            pool.dma(h1T_s[i, :, :, :], h1T[p][:], reads=[bh1T[p]])

        def run_il(gens):
            gens = [g for g in gens if g is not None]
            while gens:
                for g in list(gens):
                    try:
                        next(g)
                    except StopIteration:
                        gens.remove(g)

        loadC(0)
        if NT > 1:
            loadC(1)
        run_il([Gg_gen(0), Gn_gen(0)])
        for i in range(NT):
            if i + 2 < NT:
                pass
            run_il([O_gen(i), Gg_gen(i + 1) if i + 1 < NT else None, Gn_gen(i + 1) if i + 1 < NT else None])
            if i + 2 < NT:
                loadC(i + 2)
        fw.barrier()
    if stop == "C":
        return nc

    with ExitStack() as ph:
        stage = sb(ph, "stageD", [128, 1536], F32)
        b_stage = Buf()
        Wc, b_Wc = load_weight_bf16(ph, "Wc", w_cvin, 3072, "g1", stage, b_stage)
        H1 = [sb(ph, "H1D%d" % i, [128, 8, 512], BF16) for i in range(2)]
        bH1 = bufs(2)
        pb = [psum(ph, "pbD%d" % i, [128, 512], F32) for i in range(6)]
        bpb = pbufs(6)
        sgl = [sb(ph, "sglD%d" % i, [128, 512], F32) for i in range(2)]
        bsgl = bufs(2)
        zt = [sb(ph, "ztD%d" % i, [128, 512], BF16) for i in range(2)]
        bzt = bufs(2)
        sg1 = [sb(ph, "sg1D%d" % i, [128, 512], F32) for i in range(2)]
        bsg1 = bufs(2)
        zero = sb(ph, "zeroD", [128, 16], BF16)
        bzero = Buf()
        dve.op(lambda e: e.memset(zero[:], 0.0), writes=[bzero])
        for cbk in range(8):
            pool.dma(z_s[cbk, :, 0:16], zero[:, 0:16], reads=[bzero])
            pool.dma(z_s[cbk, :, T + 16:T + 32], zero[:, 0:16], reads=[bzero])

        def loadD(g):
            p = g % 2
            for tt in range(4):
                sp.dma(H1[p][:, :, tt * 128:(tt + 1) * 128], h1T_s[4 * g + tt, :, :, :], writes=[bH1[p]])
        loadD(0)
        cnt = 0
        for g in range(NG):
            p = g % 2
            if g + 1 < NG:
                loadD(g + 1)
            for cbk in range(8):
                q = cnt % 2
                cnt += 1
                pbs = [pb[q * 3 + k] for k in range(3)]
                bpbs = [bpb[q * 3 + k] for k in range(3)]
                for k in range(3):
                    c0 = k * 1024 + cbk * 128
                    for kc in range(8):
                        pe.op(lambda e: e.matmul(out=pbs[k][:], lhsT=Wc[:, kc, c0:c0 + 128], rhs=H1[p][:, kc, :],
                                                 start=(kc == 0), stop=(kc == 7)),
                              reads=[b_Wc, bH1[p]], writes=[bpbs[k]], acc=True)
                act.op(lambda e: e.activation(out=sgl[q][:], in_=pbs[1][:], func=AF.Sigmoid, bias=pcol("bin", 8 + cbk)),
                       reads=[bpbs[1], b_pvec], writes=[bsgl[q]])
                dve.op(lambda e: e.scalar_tensor_tensor(out=zt[q][:], in0=pbs[0][:], scalar=pcol("bin", cbk), in1=sgl[q][:],
                                                        op0=ALU.add, op1=ALU.mult),
                       reads=[bpbs[0], bsgl[q], b_pvec], writes=[bzt[q]])
                pool.dma(z_s[cbk, :, 16 + g * 512:16 + (g + 1) * 512], zt[q][:], reads=[bzt[q]])
                act.op(lambda e: e.activation(out=sg1[q][:], in_=pbs[2][:], func=AF.Sigmoid, bias=pcol("bin", 16 + cbk)),
                       reads=[bpbs[2], b_pvec], writes=[bsg1[q]])
                dve.op(lambda e: e.scalar_tensor_tensor(out=sg1[q][:], in0=pbs[2][:], scalar=pcol("bin", 16 + cbk), in1=sg1[q][:],
                                                        op0=ALU.add, op1=ALU.mult),
                       reads=[bpbs[2], bsg1[q], b_pvec], writes=[bsg1[q]])
                pool.dma(sg1_s[g, cbk, :, :], sg1[q][:], reads=[bsg1[q]])
        fw.barrier()
    if stop == "D1":
        return nc

    with ExitStack() as ph:
        stage = sb(ph, "stageE", [128, 1536], F32)
        b_stage = Buf()
        Wo, b_Wo = load_weight_bf16(ph, "Wo", w_cvout, 1024, None, stage, b_stage)
        bcs = sb(ph, "bcsE", [128, 2, 1024], F32)
        b_bc = Buf()
        sp.dma(bcs[:], bc_d[:, :, :], writes=[b_bc])
        zw = [sb(ph, "zwE%d" % i, [128, 542], BF16) for i in range(3)]
        bzw = bufs(3)
        zc = sb(ph, "zcE", [128, 8, 512], F32)
        bzc = bufs(8)
        sq = [sb(ph, "sqE%d" % i, [128, 512], F32) for i in range(2)]
        bsq = bufs(2)
        DG = sb(ph, "DGE", [128, 248, 128], BF16)
        b_DG = Buf()
        for idx in range(248):
            dve.op(lambda e: e.tensor_scalar(out=DG[:, idx, :], in0=ident, scalar1=pvec[:, PV["dw"] + idx:PV["dw"] + idx + 1],
                                             scalar2=None, op0=ALU.mult), reads=[b_cbf, b_pvec], writes=[b_DG])
        pconv = [psum(ph, "pconvE%d" % i, [128, 512], F32) for i in range(2)]
        bpconv = pbufs(2)
        pmean = psum(ph, "pmeanE", [128, 512], F32)
        bpmean = Buf(True)
        pvar = psum(ph, "pvarE", [128, 512], F32)
        bpvar = Buf(True)
        rs = sb(ph, "rsE", [128, 512], F32)
        brs = Buf()
        sg1 = [sb(ph, "sg1E%d" % i, [128, 512], F32) for i in range(2)]
        bsg1 = bufs(2)
        s1 = [sb(ph, "s1E%d" % i, [128, 512], F32) for i in range(2)]
        bs1 = bufs(2)
        ZF = sb(ph, "ZFE", [128, 8, 512], BF16)
        bZF = Buf()
        pout = [psum(ph, "poutE%d" % i, [128, 512], F32) for i in range(4)]
        bpout = pbufs(4)
        x1 = [sb(ph, "x1E%d" % i, [128, 1024], F32) for i in range(2)]
        bx1 = bufs(2)
        x2 = [sb(ph, "x2E%d" % i, [128, 1024], F32) for i in range(2)]
        bx2 = bufs(2)
        yo = [sb(ph, "yoE%d" % i, [128, 1024], F32) for i in range(2)]
        byo = bufs(2)
        junk = sb(ph, "junkE", [128, 1024], F32)
        bjunk = Buf()
        ss = sb(ph, "ssE", [128, 4], F32)
        bss = Buf()
        lc = 0
        oc = 0
        for g in range(NG):
            for cbk in range(8):
                q = lc % 3
                lc += 1
                sp.dma(zw[q][:], z_s[cbk, :, g * 512 + 1:g * 512 + 543], writes=[bzw[q]])
                dve.op(lambda e: e.tensor_scalar(out=zw[q][:, 0:15], in0=zw[q][:, 0:15], scalar1=cmask[:, 2 * g:2 * g + 1],
                                                 scalar2=None, op0=ALU.mult), reads=[bzw[q], b_mask], writes=[bzw[q]])
                dve.op(lambda e: e.tensor_scalar(out=zw[q][:, 527:542], in0=zw[q][:, 527:542],
                                                 scalar1=cmask[:, 2 * g + 1:2 * g + 2], scalar2=None, op0=ALU.mult),
                       reads=[bzw[q], b_mask], writes=[bzw[q]])
                pc, bpc = pconv[cbk % 2], bpconv[cbk % 2]
                for j in range(31):
                    pe.op(lambda e: e.matmul(out=pc[:], lhsT=DG[:, cbk * 31 + j, :], rhs=zw[q][:, j:j + 512],
                                             start=(j == 0), stop=(j == 30)),
                          reads=[b_DG, bzw[q]], writes=[bpc], acc=True)
                act.op(lambda e: e.activation(out=zc[:, cbk, :], in_=pc[:], func=AF.Identity, bias=pcol("bdw", cbk)),
                       reads=[bpc, b_pvec], writes=[bzc[cbk]])
                pe.op(lambda e: e.matmul(out=pmean[:], lhsT=O1k, rhs=zc[:, cbk, :], start=(cbk == 0), stop=(cbk == 7)),
                      reads=[bzc[cbk], b_cf], writes=[bpmean], acc=True)
            for cbk in range(8):
                q = cbk % 2
                dve.op(lambda e: e.tensor_tensor(out=zc[:, cbk, :], in0=zc[:, cbk, :], in1=pmean[:], op=ALU.subtract),
                       reads=[bzc[cbk], bpmean], writes=[bzc[cbk]])
                act.op(lambda e: e.activation(out=sq[q][:], in_=zc[:, cbk, :], func=AF.Square),
                       reads=[bzc[cbk]], writes=[bsq[q]])
                pe.op(lambda e: e.matmul(out=pvar[:], lhsT=O1k, rhs=sq[q][:], start=(cbk == 0), stop=(cbk == 7)),
                      reads=[bsq[q], b_cf], writes=[bpvar], acc=True)
            act.op(lambda e: e.activation(out=rs[:], in_=pvar[:], func=AF.Ln, bias=cst[:, 0:1]),
                   reads=[bpvar, b_cst], writes=[brs])
            act.op(lambda e: e.activation(out=rs[:], in_=rs[:], func=AF.Exp, scale=-0.5), reads=[brs], writes=[brs])
            for cbk in range(8):
                q = cbk % 2
                sp.dma(sg1[q][:], sg1_s[g, cbk, :, :], writes=[bsg1[q]])
                dve.op(lambda e: e.tensor_tensor(out=s1[q][:], in0=zc[:, cbk, :], in1=rs[:], op=ALU.mult),
                       reads=[bzc[cbk], brs], writes=[bs1[q]])
                act.op(lambda e: e.activation(out=s1[q][:], in_=s1[q][:], func=AF.Silu, scale=pcol("lng", cbk),
                                              bias=pcol("lnb", cbk)), reads=[bs1[q], b_pvec], writes=[bs1[q]])
                dve.op(lambda e: e.tensor_tensor(out=ZF[:, cbk, :], in0=s1[q][:], in1=sg1[q][:], op=ALU.mult),
                       reads=[bs1[q], bsg1[q]], writes=[bZF])
            for tt in range(4):
                i = 4 * g + tt
                p = oc % 2
                oc += 1
                sp.dma(x1[p][:], x1_s[i, :, :], writes=[bx1[p]])
                for half in range(2):
                    pk = p * 2 + half
                    hs = slice(half * 512, (half + 1) * 512)
                    for cbk in range(8):
                        pe.op(lambda e: e.matmul(out=pout[pk][:], lhsT=ZF[:, cbk, tt * 128:(tt + 1) * 128],
                                                 rhs=Wo[:, cbk, hs], start=(cbk == 0), stop=(cbk == 7)),
                              reads=[bZF, b_Wo], writes=[bpout[pk]], acc=True)
                    dve.op(lambda e: e.tensor_tensor(out=x2[p][:, hs], in0=pout[pk][:], in1=bcs[:, 0, hs], op=ALU.add),
                           reads=[bpout[pk], b_bc], writes=[bx2[p]])
                dve.op(lambda e: e.tensor_tensor(out=x2[p][:], in0=x2[p][:], in1=x1[p][:], op=ALU.add),
                       reads=[bx2[p], bx1[p]], writes=[bx2[p]])
                rmsnorm_rstd(x2[p], bx2[p], 128, junk, bjunk, ss, bss)
                dve.op(lambda e: e.scalar_tensor_tensor(out=yo[p][:], in0=x2[p][:], scalar=ss[:, 2:3], in1=bcs[:, 1, :],
                                                        op0=ALU.mult, op1=ALU.mult),
                       reads=[bx2[p], bss, b_bc], writes=[byo[p]])
                pool.dma(y_out[i, :, :], yo[p][:], reads=[byo[p]])
        fw.barrier()
    return nc


def build(T, stop=None):
    dry = _build(T, stop, None)
    plan = dry._fw.get_plan()
    return _build(T, stop, plan)


def _fm(v):
    v = np.asarray(v, np.float32).reshape(-1)
    return np.ascontiguousarray(v.reshape(-1, 128).T)


def _wrows(w):
    w = np.asarray(w, np.float32)
    return np.ascontiguousarray(w.reshape(8, 128, w.shape[1]).transpose(1, 0, 2))


def shared_inputs(norm_g, final_g, rw_in, rw_mu, rw_w0, rw_w2, rw_a0, rw_a2, rw_kk, rw_ka, rw_rk,
                  rw_lnx_g, rw_lnx_b, rw_out, cv_in, cv_b_in, cv_dw, cv_b_dw, cv_ln_g, cv_ln_b, cv_out, cv_b_out):
    pv = np.zeros((128, NPV), np.float32)

    def put(name, arr):
        pv[:, PV[name]:PV[name] + arr.shape[1]] = arr
    put("g0", _fm(norm_g[0]))
    put("g1", _fm(norm_g[1]))
    put("w0f", _fm(rw_w0[0, 0]))
    put("w0b", _fm(rw_w0[0, 1]))
    put("a0f", _fm(rw_a0[0, 0]))
    put("a0b", _fm(rw_a0[0, 1]))
    put("kk", _fm(rw_kk[0]))
    put("ka", _fm(rw_ka[0]))
    put("rk", _fm(rw_rk[0]))
    put("lnxg", _fm(rw_lnx_g[0]))
    put("lnxb", _fm(rw_lnx_b[0]))
    put("bdw", _fm(cv_b_dw[0]))
    put("lng", _fm(cv_ln_g[0]))
    put("lnb", _fm(cv_ln_b[0]))
    put("mu", _fm(rw_mu[0]))
    put("bin", _fm(cv_b_in[0]))
    dw = np.asarray(cv_dw[0], np.float32)
    dwf = dw.T.reshape(8, 128, 31).transpose(1, 0, 2).reshape(128, 248)
    put("dw", dwf)
    w2a2 = np.stack([np.asarray(rw_w2[0], np.float32).reshape(128, 1024),
                     np.asarray(rw_a2[0], np.float32).reshape(128, 1024)], axis=1)
    bc = np.stack([np.broadcast_to(np.asarray(cv_b_out[0], np.float32), (128, 1024)),
                   np.broadcast_to(np.asarray(final_g, np.float32), (128, 1024))], axis=1)
    r = np.arange(128)
    eye = (r[:, None] == r[None, :])
    su = (r[:, None] < r[None, :])
    u = (r[:, None] <= r[None, :])
    sl = (r[:, None] > r[None, :])
    l = (r[:, None] >= r[None, :])
    cbf = np.stack([np.tile(m.astype(np.float32), (1, 4)) for m in (eye, su, u, sl, l)], axis=1).astype(ml_dtypes.bfloat16)
    blk = ((r[:, None] // 64) == (r[None, :] // 64)).astype(np.float32)
    cf32 = np.stack([blk / 64.0, blk, np.full((128, 128), 1.0 / 1024.0, np.float32)], axis=1).astype(np.float32)
    return {
        "w_rwin": _wrows(rw_in[0]), "w_rwout": _wrows(rw_out[0]), "w_cvin": _wrows(cv_in[0]),
        "w_cvout": _wrows(cv_out[0]), "w2a2": np.ascontiguousarray(w2a2), "pvec": pv,
        "bc": np.ascontiguousarray(bc), "cbf": np.ascontiguousarray(cbf), "cf32": np.ascontiguousarray(cf32),
    }


def core_inputs(seqs):
    xs = np.concatenate([np.asarray(s, np.float32) for s in seqs], axis=0)
    T = xs.shape[0]
    NT, NG = T // 128, T // 512
    starts = set()
    o = 0
    for s in seqs:
        starts.add(o)
        o += s.shape[0]
    starts.add(T)
    x_t = xs.reshape(NT, 128, 1024)
    x_h = np.zeros((NT, 2, 1024), np.float32)
    sm = np.ones((NT, 2), np.float32)
    for i in range(NT):
        t0, t1 = i * 128, (i + 1) * 128
        if t0 not in starts:
            x_h[i, 0] = xs[t0 - 1]
        if t1 not in starts:
            x_h[i, 1] = xs[t1]
        if t1 in starts:
            sm[i, 0] = 0.0
        if t0 in starts:
            sm[i, 1] = 0.0
    cm = np.ones((NG, 2), np.float32)
    for g in range(NG):
        if g * 512 in starts:
            cm[g, 0] = 0.0
        if (g + 1) * 512 in starts:
            cm[g, 1] = 0.0
    return {
        "x_t": np.ascontiguousarray(x_t), "x_h": x_h,
        "smask": np.ascontiguousarray(np.broadcast_to(sm.reshape(1, -1), (128, NT * 2))),
        "cmask": np.ascontiguousarray(np.broadcast_to(cm.reshape(1, -1), (128, NG * 2))),
    }


def kernel(x_prompt, x_sample, norm_g, final_g, rw_in, rw_mu, rw_w0, rw_w2, rw_a0, rw_a2, rw_kk, rw_ka, rw_rk,
           rw_lnx_g, rw_lnx_b, rw_out, cv_in, cv_b_in, cv_dw, cv_b_dw, cv_ln_g, cv_ln_b, cv_out, cv_b_out):
    x_prompt = np.asarray(x_prompt, np.float32)
    x_sample = np.asarray(x_sample, np.float32)
    T = 16384
    sh = shared_inputs(norm_g, final_g, rw_in, rw_mu, rw_w0, rw_w2, rw_a0, rw_a2, rw_kk, rw_ka, rw_rk,
                       rw_lnx_g, rw_lnx_b, rw_out, cv_in, cv_b_in, cv_dw, cv_b_dw, cv_ln_g, cv_ln_b, cv_out, cv_b_out)
    cores = [core_inputs([x_prompt[0]]), core_inputs([x_prompt[1]]),
             core_inputs([x_sample[b] for b in range(8)])]
    in_maps = []
    for c in range(8):
        m = dict(sh)
        m.update(cores[min(c, 2)])
        in_maps.append(m)
    nc = build(T)
    res = run_bass_kernel_spmd(nc, in_maps, core_ids=list(range(8)))
    ys = [np.asarray(res.results[c]["y"], np.float32).reshape(T, 1024) for c in range(3)]
    y_prompt = np.stack([ys[0], ys[1]], axis=0)
    y_sample = ys[2].reshape(8, 2048, 1024)
    return (y_prompt, y_sample)
```
